# Optimizing a Trainium2 kernel written in Bass

```python
import math
import jax, jax.numpy as jnp
from jax import lax
import numpy as np

D_MODEL = 1024
BATCH = 2
SEQ = 8192
DEPTH = 4

CTX_LEN = 256
GRID_W = 64
ROPE_DIM = 32
ROPE_BASE = 10000.0
DA_HEADS = 6
DA_QK = ROPE_DIM
DA_V = 2 * DA_QK
NA_HEADS = 6
NA_DIM = 64
NA_KH = 8
NA_KW = 16
MLA_HEADS = 4
MLA_Q_RANK = 256
MLA_KV_RANK = 128
MLA_NOPE = 64
MLA_ROPE = ROPE_DIM
MLA_V = 64
MIX_WIDTH = DA_HEADS * DA_V + NA_HEADS * NA_DIM + MLA_HEADS * MLA_V
SPLITS = (DA_HEADS * 2 * DA_QK, DA_HEADS * 2 * DA_QK, DA_HEADS * DA_V,
          NA_HEADS * NA_DIM, NA_HEADS * NA_DIM, NA_HEADS * NA_DIM,
          MLA_Q_RANK, MLA_KV_RANK, MLA_ROPE)
W_IN_COLS = sum(SPLITS)
SPLIT_POINTS = tuple(int(v) for v in np.cumsum(SPLITS)[:-1])
DA_SCALE = DA_QK ** -0.5
NA_SCALE = NA_DIM ** -0.5
MLA_SCALE = (MLA_NOPE + MLA_ROPE) ** -0.5
D_FF = 2816
CONV_W = 3
Q_BLOCK = 128
LN_EPS = 1e-6
NEG_INF = -1e30
DEEPNORM_ALPHA = (2 * DEPTH) ** 0.25
DEEPNORM_BETA = (8 * DEPTH) ** -0.25

kernel_name = "hybrid_diffmla_natten_convffn_dit"


def _layernorm(x):
    xf = x.astype(jnp.float32)
    mu = jnp.mean(xf, axis=-1, keepdims=True)
    var = jnp.mean(jnp.square(xf - mu), axis=-1, keepdims=True)
    return ((xf - mu) * lax.rsqrt(var + LN_EPS)).astype(x.dtype)


def _rmsnorm(x, g):
    xf = x.astype(jnp.float32)
    y = xf * lax.rsqrt(jnp.mean(jnp.square(xf), axis=-1, keepdims=True) + LN_EPS)
    return y.astype(x.dtype) * g


def _axial_tables(n, dtype):
    t = jnp.arange(n)
    row = (t // GRID_W).astype(jnp.float32)
    col = (t % GRID_W).astype(jnp.float32)
    axis_dim = ROPE_DIM // 2
    inv_freq = ROPE_BASE ** (-jnp.arange(0, axis_dim, 2, dtype=jnp.float32) / axis_dim)

    def cs(pos):
        ang = pos[:, None] * inv_freq[None, :]
        ang = jnp.concatenate([ang, ang], axis=-1)
        return jnp.cos(ang).astype(dtype), jnp.sin(ang).astype(dtype)

    cr, sr = cs(row)
    cc, sc = cs(col)
    return (cr, sr, cc, sc)


def _rope_half(x, cos, sin):
    h = x.shape[-1] // 2
    rot = jnp.concatenate([-x[..., h:], x[..., :h]], axis=-1)
    return x * cos + rot * sin


def _rope_2d(x, tabs):
    cr, sr, cc, sc = tabs
    h = x.shape[-1] // 2
    return jnp.concatenate([_rope_half(x[..., :h], cr, sr), _rope_half(x[..., h:], cc, sc)], axis=-1)


def _split_heads(proj, tabs, q_norm_w, kv_norm_w, w_uq, w_ukv):
    a_q, a_k, a_v, n_q, n_k, n_v, c_q, c_kv, k_r = jnp.split(proj, SPLIT_POINTS, axis=-1)
    b, n, _ = proj.shape
    a_q = a_q.reshape(b, n, DA_HEADS, 2, DA_QK).transpose(0, 2, 1, 3, 4)
    a_k = a_k.reshape(b, n, DA_HEADS, 2, DA_QK).transpose(0, 2, 1, 3, 4)
    a_v = a_v.reshape(b, n, DA_HEADS, DA_V).transpose(0, 2, 1, 3)
    n_q = n_q.reshape(b, n, NA_HEADS, NA_DIM).transpose(0, 2, 1, 3)
    n_k = n_k.reshape(b, n, NA_HEADS, NA_DIM).transpose(0, 2, 1, 3)
    n_v = n_v.reshape(b, n, NA_HEADS, NA_DIM).transpose(0, 2, 1, 3)
    q_c = (_rmsnorm(c_q, q_norm_w) @ w_uq).reshape(b, n, MLA_HEADS, MLA_NOPE + MLA_ROPE).transpose(0, 2, 1, 3)
    kv_c = (_rmsnorm(c_kv, kv_norm_w) @ w_ukv).reshape(b, n, MLA_HEADS, MLA_NOPE + MLA_V).transpose(0, 2, 1, 3)
    m_qn, m_qr = q_c[..., :MLA_NOPE], q_c[..., MLA_NOPE:]
    m_kn, m_v = kv_c[..., :MLA_NOPE], kv_c[..., MLA_NOPE:]
    k_r = k_r[:, None]
    if tabs is not None:
        tabs_da = tuple(t[:, None, :] for t in tabs)
        a_q = _rope_2d(a_q, tabs_da)
        a_k = _rope_2d(a_k, tabs_da)
        m_qr = _rope_2d(m_qr, tabs)
        k_r = _rope_2d(k_r, tabs)
    m_q = jnp.concatenate([m_qn, m_qr], axis=-1)
    m_k = jnp.concatenate([m_kn, jnp.broadcast_to(k_r, m_kn.shape[:-1] + (MLA_ROPE,))], axis=-1)
    return (a_q, a_k, a_v, n_q, n_k, n_v, m_q, m_k, m_v)


def _diff_attend(q, k, v, lam, scale):
    s = jnp.einsum('bhqcd,bhkcd->bhcqk', q, k).astype(jnp.float32) * scale
    p = jax.nn.softmax(s, axis=-1)
    p = p[:, :, 0] - lam * p[:, :, 1]
    return jnp.einsum('bhqk,bhkd->bhqd', p.astype(v.dtype), v)


def _softmax_attend(q, k, v, scale):
    s = jnp.einsum('bhqd,bhkd->bhqk', q, k).astype(jnp.float32) * scale
    p = jax.nn.softmax(s, axis=-1)
    return jnp.einsum('bhqk,bhkd->bhqd', p.astype(v.dtype), v)


def _sweep_query_blocks(fn, q):
    b, h, n = q.shape[:3]
    nb = n // Q_BLOCK
    qb = jnp.moveaxis(q.reshape((b, h, nb, Q_BLOCK) + q.shape[3:]), 2, 0)
    out = lax.map(fn, qb)
    return jnp.moveaxis(out, 0, 2).reshape(b, h, n, out.shape[-1])


def _neighbourhood_attend(q, k, v, k_ctx, v_ctx, rpb, scale):
    b, h, n, d = q.shape
    rows = n // GRID_W
    kh = min(NA_KH, rows)
    qg = q.reshape(b, h, rows, GRID_W, d)
    kg = k.reshape(b, h, rows, GRID_W, d)
    vg = v.reshape(b, h, rows, GRID_W, d)
    r_idx = jnp.arange(rows)
    r0 = jnp.clip(r_idx - kh // 2, 0, rows - kh)
    key_rows = r0[:, None] + jnp.arange(kh)[None, :]
    k_win = kg[:, :, key_rows]
    v_win = vg[:, :, key_rows]
    c_idx = jnp.arange(GRID_W)
    c0 = jnp.clip(c_idx - NA_KW // 2, 0, GRID_W - NA_KW)
    in_band = (c_idx[None, :] >= c0[:, None]) & (c_idx[None, :] < c0[:, None] + NA_KW)
    dr = key_rows - r_idx[:, None] + (NA_KH - 1)
    dc = jnp.clip(c_idx[None, :] - c_idx[:, None], -(NA_KW - 1), NA_KW - 1) + (NA_KW - 1)
    bias = rpb[:, dr[:, None, :, None], dc[None, :, None, :]]
    s_win = jnp.einsum('bhrqd,bhrjkd->bhrqjk', qg, k_win).astype(jnp.float32) * scale
    s_win = s_win + bias[None].astype(jnp.float32)
    s_win = jnp.where(in_band[:, None, :], s_win, NEG_INF)
    s_ctx = jnp.einsum('bhrqd,bhcd->bhrqc', qg, k_ctx).astype(jnp.float32) * scale
    nw = kh * GRID_W
    s = jnp.concatenate([s_win.reshape(b, h, rows, GRID_W, nw), s_ctx], axis=-1)
    p = jax.nn.softmax(s, axis=-1).astype(v.dtype)
    p_win = p[..., :nw].reshape(b, h, rows, GRID_W, kh, GRID_W)
    p_ctx = p[..., nw:]
    out = (jnp.einsum('bhrqjk,bhrjkd->bhrqd', p_win, v_win)
           + jnp.einsum('bhrqc,bhcd->bhrqd', p_ctx, v_ctx))
    return out.reshape(b, h, n, d)


def _merge(o):
    b, h, n, d = o.shape
    return o.transpose(0, 2, 1, 3).reshape(b, n, h * d)


def _mix_out(o_a, o_b, o_c, diff_norm_w, lam_init, w_out):
    o_a = _rmsnorm(o_a, diff_norm_w) * (1.0 - lam_init)
    return jnp.concatenate([_merge(o_a), _merge(o_b), _merge(o_c)], axis=-1) @ w_out


def _conv_ffn(h, w_up, conv_w, conv_b, w_down):
    u = h @ w_up
    u = lax.conv_general_dilated(u, conv_w[:, None, :], window_strides=(1,),
                                 padding=((CONV_W // 2, CONV_W // 2),),
                                 dimension_numbers=('NWC', 'WIO', 'NWC'),
                                 feature_group_count=2 * D_FF) + conv_b
    g, val = jnp.split(u, 2, axis=-1)
    return (jax.nn.silu(g) * val) @ w_down


def _ada(cond, w_ada, b_ada):
    return jnp.split(jax.nn.silu(cond) @ w_ada + b_ada, 6, axis=-1)


def _modulate(x, shift, scale):
    return _layernorm(x) * (1.0 + scale) + shift


def _post_norm(x, y, g, b):
    return _layernorm(DEEPNORM_ALPHA * x + y) * g + b


def setup_inputs(seed: int = 0) -> dict:
    key = jax.random.key(seed)
    ks = jax.random.split(key, 26)
    f32 = jnp.float32

    def nrm(k, shape, s):
        return jax.random.normal(k, shape, f32) * s

    def gain(k, shape):
        return 1.0 + nrm(k, shape, 0.02)

    L = DEPTH
    return {
        "x": nrm(ks[0], (BATCH, SEQ, D_MODEL), 1.0),
        "c": nrm(ks[1], (BATCH, D_MODEL), 1.0),
        "ctx": nrm(ks[2], (BATCH, CTX_LEN, D_MODEL), 1.0),
        "c_ctx": nrm(ks[3], (D_MODEL,), 1.0),
        "w_ada": nrm(ks[4], (L, D_MODEL, 6 * D_MODEL), D_MODEL ** -0.5),
        "b_ada": nrm(ks[5], (L, 6 * D_MODEL), 0.02),
        "w_in": nrm(ks[6], (L, D_MODEL, W_IN_COLS), D_MODEL ** -0.5),
        "lam_q1": nrm(ks[7], (L, DA_QK), 0.1),
        "lam_k1": nrm(ks[8], (L, DA_QK), 0.1),
        "lam_q2": nrm(ks[9], (L, DA_QK), 0.1),
        "lam_k2": nrm(ks[10], (L, DA_QK), 0.1),
        "diff_norm_w": gain(ks[11], (L, DA_V)),
        "na_rpb": nrm(ks[12], (L, NA_HEADS, 2 * NA_KH - 1, 2 * NA_KW - 1), 0.1),
        "mla_q_norm_w": gain(ks[13], (L, MLA_Q_RANK)),
        "mla_kv_norm_w": gain(ks[14], (L, MLA_KV_RANK)),
        "w_uq": nrm(ks[15], (L, MLA_Q_RANK, MLA_HEADS * (MLA_NOPE + MLA_ROPE)), MLA_Q_RANK ** -0.5),
        "w_ukv": nrm(ks[16], (L, MLA_KV_RANK, MLA_HEADS * (MLA_NOPE + MLA_V)), MLA_KV_RANK ** -0.5),
        "w_out": nrm(ks[17], (L, MIX_WIDTH, D_MODEL), MIX_WIDTH ** -0.5 * DEEPNORM_BETA),
        "ln1_g": gain(ks[18], (L, D_MODEL)),
        "ln1_b": nrm(ks[19], (L, D_MODEL), 0.02),
        "w_up": nrm(ks[20], (L, D_MODEL, 2 * D_FF), D_MODEL ** -0.5),
        "conv_w": nrm(ks[21], (L, CONV_W, 2 * D_FF), CONV_W ** -0.5),
        "conv_b": nrm(ks[22], (L, 2 * D_FF), 0.02),
        "w_down": nrm(ks[23], (L, D_FF, D_MODEL), D_FF ** -0.5 * DEEPNORM_BETA),
        "ln2_g": gain(ks[24], (L, D_MODEL)),
        "ln2_b": nrm(ks[25], (L, D_MODEL), 0.02),
    }


def reference(x, c, ctx, c_ctx, w_ada, b_ada, w_in, lam_q1, lam_k1, lam_q2, lam_k2,
              diff_norm_w, na_rpb, mla_q_norm_w, mla_kv_norm_w, w_uq, w_ukv, w_out,
              ln1_g, ln1_b, w_up, conv_w, conv_b, w_down, ln2_g, ln2_b):
    n = x.shape[1]
    tabs = _axial_tables(n, x.dtype)
    for l in range(DEPTH):
        sa, ca, ga, sf, cf, gf = _ada(c[:, None, :], w_ada[l], b_ada[l])
        sa_c, ca_c, ga_c, sf_c, cf_c, gf_c = _ada(c_ctx, w_ada[l], b_ada[l])
        h = _modulate(x, sa, ca)
        h_c = _modulate(ctx, sa_c, ca_c)
        aq, ak, av, nq, nk, nv, mq, mk, mv = _split_heads(
            h @ w_in[l], tabs, mla_q_norm_w[l], mla_kv_norm_w[l], w_uq[l], w_ukv[l])
        caq, cak, cav, cnq, cnk, cnv, cmq, cmk, cmv = _split_heads(
            h_c @ w_in[l], None, mla_q_norm_w[l], mla_kv_norm_w[l], w_uq[l], w_ukv[l])
        lam_init = 0.8 - 0.6 * math.exp(-0.3 * l)
        lam = (jnp.exp(jnp.sum(lam_q1[l] * lam_k1[l])) - jnp.exp(jnp.sum(lam_q2[l] * lam_k2[l]))
               + lam_init)

        ak_all = jnp.concatenate([cak, ak], axis=2)
        av_all = jnp.concatenate([cav, av], axis=2)
        mk_all = jnp.concatenate([cmk, mk], axis=2)
        mv_all = jnp.concatenate([cmv, mv], axis=2)
        o_a = _sweep_query_blocks(lambda qb: _diff_attend(qb, ak_all, av_all, lam, DA_SCALE), aq)
        o_b = _neighbourhood_attend(nq, nk, nv, cnk, cnv, na_rpb[l], NA_SCALE)
        o_c = _sweep_query_blocks(lambda qb: _softmax_attend(qb, mk_all, mv_all, MLA_SCALE), mq)
        mix = _mix_out(o_a, o_b, o_c, diff_norm_w[l], lam_init, w_out[l])
        x = _post_norm(x, ga * mix, ln1_g[l], ln1_b[l])
        y = _conv_ffn(_modulate(x, sf, cf), w_up[l], conv_w[l], conv_b[l], w_down[l])
        x = _post_norm(x, gf * y, ln2_g[l], ln2_b[l])

        if l < DEPTH - 1:
            co_a = _diff_attend(caq, cak, cav, lam, DA_SCALE)
            co_b = _softmax_attend(cnq, cnk, cnv, NA_SCALE)
            co_c = _softmax_attend(cmq, cmk, cmv, MLA_SCALE)
            cmix = _mix_out(co_a, co_b, co_c, diff_norm_w[l], lam_init, w_out[l])
            ctx = _post_norm(ctx, ga_c * cmix, ln1_g[l], ln1_b[l])
            cy = _conv_ffn(_modulate(ctx, sf_c, cf_c), w_up[l], conv_w[l], conv_b[l], w_down[l])
            ctx = _post_norm(ctx, gf_c * cy, ln2_g[l], ln2_b[l])
    return x
```

```python
import math
from contextlib import ExitStack

import numpy as np
import concourse.bass as bass
import concourse.mybir as mybir
from concourse.bass_utils import run_bass_kernel_spmd

F32 = mybir.dt.float32
BF16 = mybir.dt.bfloat16
AF = mybir.ActivationFunctionType
ALU = mybir.AluOpType

D = 1024
CTX = 256
GW = 64
DFF = 2816
NEG = -30000.0
LN_EPS = 1e-6
DEPTH_FULL = 4
DBG_KINDS = ("a", "n", "m")
ALPHA = (2 * DEPTH_FULL) ** 0.25
DA_SCALE = 32 ** -0.5
NA_SCALE = 64 ** -0.5
MLA_SCALE = 96 ** -0.5

O_AQ, O_AK, O_AV, O_NQ, O_NK, O_NV, O_CQ, O_CKV, O_KR = 0, 384, 768, 1152, 1536, 1920, 2304, 2560, 2688


def _rot_src(n):
    f = np.arange(n)
    j = f % 16
    return np.where(j < 8, f + 8, f - 8)


def _win_cols():
    aq = np.arange(384)
    cols = []
    cols.append(O_AQ + aq)
    cols.append(O_AQ + _rot_src(384))
    cols.append(O_AK + aq)
    cols.append(O_AK + _rot_src(384))
    cols.append(O_NQ + aq)
    cols.append(O_NK + aq)
    cols.append(O_KR + np.arange(32))
    cols.append(O_KR + _rot_src(32))
    cols.append(O_AV + aq)
    cols.append(O_NV + aq)
    cols.append(O_CQ + np.arange(384))
    return np.concatenate(cols)


WA_COLS = 3520
C_AQ, C_AQR, C_AK, C_AKR, C_NQ, C_NK, C_KR, C_KRR, C_AV, C_NV, C_CQ = 0, 384, 768, 1152, 1536, 1920, 2304, 2336, 2368, 2752, 3136


def _wuq_rot_cols():
    c = np.arange(384)
    h, f = c // 96, c % 96
    r = f - 64
    rr = np.where(r % 16 < 8, r + 8, r - 8)
    return np.where(f < 64, c, h * 96 + 64 + rr)


def _rope_tables(n):
    t = np.arange(n)
    row = (t // GW).astype(np.float32)
    col = (t % GW).astype(np.float32)
    inv = (10000.0 ** (-np.arange(0, 16, 2, dtype=np.float32) / 16)).astype(np.float32)
    cosT = np.zeros((128, n), np.float32)
    sinT = np.zeros((128, n), np.float32)
    for p in range(128):
        f = p % 32
        pos = row if f < 16 else col
        j = f % 16
        ang = (pos * inv[j % 8]).astype(np.float32)
        cosT[p] = np.cos(ang)
        sinT[p] = np.sin(ang) * (-1.0 if j < 8 else 1.0)
    return cosT, sinT


def _na_plan(n):
    rows = n // GW
    kh = min(8, rows)
    uniq = {}
    plan = []
    for qp in range(rows // 2):
        r0a = min(max(2 * qp - kh // 2, 0), rows - kh)
        r0b = min(max(2 * qp + 1 - kh // 2, 0), rows - kh)
        lo, hi = r0a // 2, (r0b + kh - 1) // 2
        ent = []
        for kp in range(lo, hi + 1):
            key = []
            for khalf in range(2):
                for qhalf in range(2):
                    kr, qr = 2 * kp + khalf, 2 * qp + qhalf
                    r0 = min(max(qr - kh // 2, 0), rows - kh)
                    key.append(kr - qr + 7 if (r0 <= kr < r0 + kh) else -1)
            key = tuple(key)
            if key not in uniq:
                uniq[key] = len(uniq)
            ent.append((kp, uniq[key]))
        plan.append(ent)
    return plan, list(uniq.keys())


def _na_tables(rpb, keys):
    L = rpb.shape[0]
    c = np.arange(GW)
    c0 = np.clip(c - 8, 0, GW - 16)
    band = (c[None, :] >= c0[:, None]) & (c[None, :] < c0[:, None] + 16)
    dc = np.clip(c[None, :] - c[:, None], -15, 15) + 15
    out = np.full((L, 6, len(keys), 128, 128), NEG, np.float32)
    for ti, key in enumerate(keys):
        for khalf in range(2):
            for qhalf in range(2):
                a = key[khalf * 2 + qhalf]
                if a < 0:
                    continue
                blk = rpb[:, :, a, :][:, :, dc.T]
                blk = np.where(band.T[None, None], blk, np.float32(NEG))
                out[:, :, ti, khalf * 64:(khalf + 1) * 64, qhalf * 64:(qhalf + 1) * 64] = blk
    return out


class Buf:
    __slots__ = ("w", "r", "dsem", "keep", "name")

    def __init__(self, name="", keep=False):
        self.w = None
        self.r = {}
        self.dsem = None
        self.keep = keep
        self.name = name


class Sem:
    _n = 0

    def __init__(self, h):
        self.h = h
        Sem._n += 1
        self.idx = Sem._n
        self.cnt = 0


class Sched:
    ROLL = 30000

    def __init__(self, nc, es):
        self.nc = nc
        self.es = es
        self.eng = {"pe": nc.tensor, "act": nc.scalar, "dve": nc.vector, "pool": nc.gpsimd, "sp": nc.sync}
        self.sem = {}
        self.cnt = {}
        self.waited = {e: {} for e in self.eng}
        self.nsem = 0
        for e in self.eng:
            self.sem[e] = self.newsem(e)
            self.cnt[e] = 0
        self.dsems = []
        self.dfree = []
        self.scope = []

    def release_scope(self):
        for b in self.scope:
            if not b.keep and b.dsem is not None:
                self.dfree.append(b.dsem)
                b.dsem = None
        self.scope = [b for b in self.scope if b.keep and False]

    def newsem(self, name):
        self.nsem += 1
        return Sem(self.es.enter_context(self.nc.semaphore(f"{name}_{self.nsem}")))

    def _deps(self, r, w):
        deps = []
        for b in r:
            if b.w is not None:
                deps.append(b.w)
        for b in w:
            if b.w is not None:
                deps.append(b.w)
            deps.extend(b.r.values())
        return deps

    def _wait(self, e, deps):
        eng = self.eng[e]
        wd = self.waited[e]
        for (se, sem, val) in deps:
            if se == "pe" and e == "pe":
                continue
            if wd.get(sem.idx, 0) >= val:
                continue
            eng.wait_ge(sem.h, val)
            wd[sem.idx] = val

    def op(self, e, fn, r=(), w=()):
        self._wait(e, self._deps(r, w))
        ins = fn(self.eng[e])
        if self.cnt[e] >= self.ROLL:
            self.sem[e] = self.newsem(e)
            self.cnt[e] = 0
        self.cnt[e] += 1
        ins.then_inc(self.sem[e].h, 1)
        tok = (e, self.sem[e], self.cnt[e])
        for b in r:
            b.r[(e, self.sem[e].idx)] = tok
        for b in w:
            b.w = tok
            b.r = {}
        return tok

    def dma(self, out, in_, rd, wr, q="sp"):
        rd_dram = rd.name.startswith("D:")
        wr_dram = wr.name.startswith("D:")
        assert rd_dram != wr_dram
        own = rd if wr_dram else wr
        deps = self._deps([] if rd_dram else [rd], [] if wr_dram else [wr])
        self._wait(q, deps)
        ins = self.eng[q].dma_start(out=out, in_=in_)
        if own.dsem is None:
            while self.dfree and self.dfree[-1].cnt > 40000:
                self.dfree.pop()
            if self.dfree:
                own.dsem = self.dfree.pop()
            else:
                own.dsem = self.newsem("d")
                self.dsems.append(own.dsem)
            self.scope.append(own)
        own.dsem.cnt += 16
        assert own.dsem.cnt < 65000, "DMA semaphore overflow"
        ins.then_inc(own.dsem.h, 16)
        tok = ("dma", own.dsem, own.dsem.cnt)
        if wr_dram:
            rd.r[("dma", own.dsem.idx)] = tok
        else:
            wr.w = tok
            wr.r = {}
        return tok

    def barrier(self):
        toks = [(e, self.sem[e], self.cnt[e]) for e in self.eng if self.cnt[e] > 0]
        toks += [("dma", d, d.cnt) for d in self.dsems if d.cnt > 0]
        for e in self.eng:
            self._wait(e, toks)
        self.release_scope()


class Ctx:
    pass


def build(N, DEPTH, stop_after=None):
    NT = N // 128
    NTOK = N + CTX
    NTT = NTOK // 128
    NKC = NTT
    nc = bass.Bass("TRN2", target_bir_lowering=False)
    es = ExitStack()
    K = Sched(nc, es)

    def din(name, shape, dt=F32):
        return nc.dram_tensor(name, list(shape), dt, kind="ExternalInput").ap()

    def dscr(name, shape, dt):
        return nc.dram_tensor(name, list(shape), dt, kind="Internal").ap()

    x_in = din("x", [N, D]); ctx_in = din("ctx", [CTX, D]); cc_in = din("cc", [128, 8, 2])
    w_ada = din("w_ada", [DEPTH, D, 6 * D]); b_ada = din("b_ada", [DEPTH, 6 * D])
    w_a = din("w_a", [DEPTH, D, WA_COLS])
    lamv = din("lamv", [DEPTH, 2, 2, 32]); dnw = din("dnw", [DEPTH, 64])
    plan, tkeys = _na_plan(N)
    NTAB = len(tkeys)
    natab = din("natab", [DEPTH, 6, NTAB, 128, 128])
    qnw = din("qnw", [DEPTH, 384])
    w_uq = din("w_uq", [DEPTH, 256, 384]); w_uqr = din("w_uqr", [DEPTH, 256, 384])
    w_ukn = din("w_ukn", [DEPTH, 128, 256]); w_ukv = din("w_ukv", [DEPTH, 128, 256])
    w_out = din("w_out", [DEPTH, D, D]); w_up = din("w_up", [DEPTH, D, 2 * DFF]); w_down = din("w_down", [DEPTH, DFF, D])
    lnp = din("lnp", [DEPTH, 4, D])
    convp = din("convp", [DEPTH, 128, 44, 4])
    ident_in = din("ident", [128, 128]); cosT_in = din("cosT", [128, N]); sinT_in = din("sinT", [128, N])
    y_out = nc.dram_tensor("y", [N, D], F32, kind="ExternalOutput").ap()

    xc_d = dscr("xc_d", [CTX, D], F32)
    x1_d = dscr("x1_d", [NTOK, D], F32)
    ada_d = dscr("ada_d", [DEPTH, 2, 2, D], F32)
    qa_d = dscr("qa_d", [4, 96, NTOK], BF16); ka_d = dscr("ka_d", [4, 96, NTOK], BF16)
    qn_d = dscr("qn_d", [384, NTOK], BF16); kn_d = dscr("kn_d", [384, NTOK], BF16)
    qm_d = dscr("qm_d", [4, 96, NTOK], BF16); km_d = dscr("km_d", [4, 96, NTOK], BF16)
    va_d = dscr("va_d", [NTOK, 6, 65], BF16); vn_d = dscr("vn_d", [NTOK, 6, 65], BF16); vm_d = dscr("vm_d", [NTOK, 4, 65], BF16)
    mix_d = dscr("mix_d", [8, 128, NTOK], BF16)
    h2_d = dscr("h2_d", [8, 128, NTOK], BF16)
    at_d = dscr("at_d", [22, 128, NTOK], BF16)
    B = {n: Buf("D:" + n, keep=True) for n in ["y", "xc", "x1", "ada", "qa", "ka", "qn", "kn", "qm", "km", "va", "vn", "vm", "mix", "h2", "at", "IN"]}

    uid = [0]

    def sb(st, name, shape, dt):
        uid[0] += 1
        return st.enter_context(nc.sbuf_tensor(f"s{uid[0]}_{name}", list(shape), dt))

    def ps(st, name, shape, dt=F32):
        uid[0] += 1
        return st.enter_context(nc.psum_tensor(f"p{uid[0]}_{name}", list(shape), dt))

    ident = sb(es, "ident", [128, 128], F32); b_ident = Buf("ident", True)
    epsb = sb(es, "epsb", [128, 1], F32); b_eps = Buf()
    modfm = sb(es, "modfm", [128, DEPTH, 48, 2], F32); b_mod = Buf()
    lam_sb = sb(es, "lam_sb", [128, DEPTH, 2], F32); b_lam = Buf()
    K.dma(ident[:], ident_in, B["IN"], b_ident)
    K.op("pool", lambda e: e.memset(epsb[:], LN_EPS), w=[b_eps])

    def x_tile_ap(l, t, final=False):
        if t < NT:
            src = x_in if l == 0 else y_out
            return src[t * 128:(t + 1) * 128, :], (B["IN"] if l == 0 else B["y"])
        src = ctx_in if l == 0 else xc_d
        return src[(t - NT) * 128:(t - NT + 1) * 128, :], (B["IN"] if l == 0 else B["xc"])

    def x_out_ap(t):
        if t < NT:
            return y_out[t * 128:(t + 1) * 128, :], B["y"]
        return xc_d[(t - NT) * 128:(t - NT + 1) * 128, :], B["xc"]

    def load_weight(st, name, src, nchunk, ncols, dst, dbuf, colblk=2048):
        stg = [sb(st, f"{name}_s{i}", [128, colblk], F32) for i in range(2)]
        sbf = [Buf() for _ in range(2)]
        i = 0
        for c in range(nchunk):
            for c0 in range(0, ncols, colblk):
                cw = min(colblk, ncols - c0)
                K.dma(stg[i % 2][:, :cw], src[c * 128:(c + 1) * 128, c0:c0 + cw], B["IN"], sbf[i % 2])
                s_ = stg[i % 2]
                K.op("pool", lambda e, s_=s_, c=c, c0=c0, cw=cw: e.tensor_copy(out=dst[:, c, c0:c0 + cw], in_=s_[:, :cw]),
                     r=[sbf[i % 2]], w=[dbuf])
                i += 1

    def rstd_op(out_ap, in_ap, scale, rbufs, wbuf):
        K.op("act", lambda e: e.activation(out=out_ap, in_=in_ap, func=AF.Ln, bias=epsb[:], scale=scale), r=rbufs + [b_eps], w=[wbuf])
        K.op("act", lambda e: e.activation(out=out_ap, in_=out_ap, func=AF.Exp, scale=-0.5), r=[wbuf], w=[wbuf])

    with ExitStack() as st:
        csb = sb(st, "csb", [128, 8, 2], F32); b_c = Buf()
        K.dma(csb[:], cc_in, B["IN"], b_c)
        K.op("act", lambda e: e.activation(out=csb[:], in_=csb[:], func=AF.Silu), r=[b_c], w=[b_c])
        ones2 = sb(st, "ones2", [1, 2], F32); b_o2 = Buf()
        K.op("pool", lambda e: e.memset(ones2[:], 1.0), w=[b_o2])
        wst = [sb(st, f"wst{i}", [128, 8, 512], F32) for i in range(2)]; b_wst = [Buf(), Buf()]
        bst = [sb(st, f"bst{i}", [1, 512], F32) for i in range(2)]; b_bst = [Buf(), Buf()]
        pfm = [ps(st, f"pfm{i}", [128, 4, 2]) for i in range(2)]; b_pfm = [Buf(), Buf()]
        prow = [ps(st, f"prow{i}", [2, 512]) for i in range(2)]; b_prow = [Buf(), Buf()]
        rsb = [sb(st, f"rsb{i}", [2, 512], F32) for i in range(2)]; b_rsb = [Buf(), Buf()]
        it = 0
        for l in range(DEPTH):
            for cb in range(12):
                j = it % 2
                it += 1
                K.dma(wst[j][:], w_ada[l, :, cb * 512:(cb + 1) * 512].rearrange("(c p) n -> p c n", p=128), B["IN"], b_wst[j])
                K.dma(bst[j][:], b_ada[l:l + 1, cb * 512:(cb + 1) * 512], B["IN"], b_bst[j])
                for cc in range(4):
                    for dc in range(8):
                        K.op("pe", lambda e, j=j, cc=cc, dc=dc: e.matmul(pfm[j][:, cc, :], lhsT=wst[j][:, dc, cc * 128:(cc + 1) * 128],
                                                                          rhs=csb[:, dc, :], start=(dc == 0), stop=False),
                             r=[b_wst[j], b_c], w=[b_pfm[j]])
                    K.op("pe", lambda e, j=j, cc=cc: e.matmul(pfm[j][:, cc, :], lhsT=bst[j][:, cc * 128:(cc + 1) * 128], rhs=ones2[:],
                                                              start=False, stop=True), r=[b_bst[j], b_o2], w=[b_pfm[j]])
                is_scale = cb in (2, 3, 8, 9)
                K.op("dve", lambda e, j=j, l=l, cb=cb, a=(1.0 if is_scale else 0.0): e.tensor_scalar_add(
                    out=modfm[:, l, cb * 4:(cb + 1) * 4, :], in0=pfm[j][:], scalar1=a), r=[b_pfm[j]], w=[b_mod])
                if cb in (4, 5, 10, 11):
                    for dc in range(8):
                        K.op("pe", lambda e, j=j, dc=dc: e.matmul(prow[j][:], lhsT=csb[:, dc, :], rhs=wst[j][:, dc, :], start=(dc == 0), stop=False),
                             r=[b_wst[j], b_c], w=[b_prow[j]])
                    K.op("pe", lambda e, j=j: e.matmul(prow[j][:], lhsT=ones2[:], rhs=bst[j][:], start=False, stop=True),
                         r=[b_bst[j], b_o2], w=[b_prow[j]])
                    K.op("act", lambda e, j=j: e.copy(out=rsb[j][:], in_=prow[j][:]), r=[b_prow[j]], w=[b_rsb[j]])
                    g = 0 if cb < 6 else 1
                    half = cb % 2
                    K.dma(ada_d[l, :, g, half * 512:(half + 1) * 512], rsb[j][:], b_rsb[j], B["ada"])
        lv = sb(st, "lv", [128, DEPTH, 2, 2, 32], F32); b_lv = Buf()
        K.dma(lv[:].rearrange("p l a c b -> p (l a c b)"), lamv.rearrange("l a c b -> (l a c b)").partition_broadcast(128), B["IN"], b_lv)
        lt = sb(st, "lt", [128, DEPTH, 2, 32], F32); b_lt = Buf()
        ls = sb(st, "ls", [128, DEPTH, 2], F32); b_ls = Buf()
        K.op("dve", lambda e: e.tensor_tensor(out=lt[:], in0=lv[:, :, 0, :, :], in1=lv[:, :, 1, :, :], op=ALU.mult), r=[b_lv], w=[b_lt])
        K.op("dve", lambda e: e.tensor_reduce(out=ls[:], in_=lt[:], axis=mybir.AxisListType.X, op=ALU.add), r=[b_lt], w=[b_ls])
        K.op("act", lambda e: e.activation(out=ls[:], in_=ls[:], func=AF.Exp), r=[b_ls], w=[b_ls])
        for l in range(DEPTH):
            lam_init = 0.8 - 0.6 * math.exp(-0.3 * l)
            K.op("dve", lambda e, l=l, li=lam_init: e.scalar_tensor_tensor(out=lam_sb[:, l, 0:1], in0=ls[:, l, 1:2], scalar=-li, in1=ls[:, l, 0:1],
                                                                            op0=ALU.add, op1=ALU.subtract), r=[b_ls], w=[b_lam])
        K.barrier()

    def phaseA(l):
        with ExitStack() as st:
            wa = sb(st, "wa", [128, 8, WA_COLS], BF16); b_wa = Buf()
            load_weight(st, "wa", w_a[l], 8, WA_COLS, wa, b_wa, colblk=1760)
            wq = sb(st, "wq", [128, 2, 384], BF16); b_wq = Buf()
            wqr = sb(st, "wqr", [128, 2, 384], BF16); b_wqr = Buf()
            wkn = sb(st, "wkn", [128, 1, 256], BF16); b_wkn = Buf()
            wkv = sb(st, "wkv", [128, 1, 256], BF16); b_wkv = Buf()
            load_weight(st, "wq", w_uq[l], 2, 384, wq, b_wq, colblk=384)
            load_weight(st, "wqr", w_uqr[l], 2, 384, wqr, b_wqr, colblk=384)
            load_weight(st, "wkn", w_ukn[l], 1, 256, wkn, b_wkn, colblk=256)
            load_weight(st, "wkv", w_ukv[l], 1, 256, wkv, b_wkv, colblk=256)
            gq = sb(st, "gq", [128, 384], F32); b_gq = Buf()
            K.dma(gq[:], qnw[l].partition_broadcast(128), B["IN"], b_gq)
            xt = [sb(st, f"xt{i}", [128, D], F32) for i in range(2)]; b_xt = [Buf(), Buf()]
            xn = [sb(st, f"xn{i}", [128, D], F32) for i in range(2)]; b_xn = [Buf(), Buf()]
            stt = sb(st, "stt", [128, 2, 6], F32); b_stt = Buf()
            mv = sb(st, "mv", [128, 2], F32); b_mv = Buf()
            rs = sb(st, "rs", [128, 1], F32); b_rs = Buf()
            nb = sb(st, "nb", [128, 1], F32); b_nb = Buf()
            hT = [sb(st, f"hT{i}", [128, 8, 512], BF16) for i in range(2)]; b_hT = [Buf(), Buf()]
            cs = [sb(st, f"cs{i}", [128, 512], F32) for i in range(2)]; b_cs = [Buf(), Buf()]
            sn = [sb(st, f"sn{i}", [128, 512], F32) for i in range(2)]; b_sn = [Buf(), Buf()]
            t1 = [sb(st, f"t1{i}", [128, 512], F32) for i in range(2)]; b_t1 = [Buf(), Buf()]
            t2 = [sb(st, f"t2{i}", [128, 512], F32) for i in range(2)]; b_t2 = [Buf(), Buf()]
            ob = [sb(st, f"ob{i}", [128, 512], BF16) for i in range(3)]; b_ob = [Buf() for _ in range(3)]
            vb = [sb(st, f"vb{i}", [128, 6, 65], BF16) for i in range(4)]; b_vb = [Buf() for _ in range(4)]
            cqs = sb(st, "cqs", [128, 384], F32); b_cqs = Buf()
            cqn = sb(st, "cqn", [128, 384], F32); b_cqn = Buf()
            junk = sb(st, "junk", [128, 384], F32); b_junk = Buf()
            ssq = sb(st, "ssq", [128, 2], F32); b_ssq = Buf()
            cT = [sb(st, f"cT{i}", [128, 3, 512], BF16) for i in range(2)]; b_cT = [Buf(), Buf()]
            ptr = [ps(st, f"ptr{i}", [128, 128]) for i in range(2)]; b_ptr = [Buf(), Buf()]
            pfm = [ps(st, f"pA{i}", [128, 512]) for i in range(4)]; b_pfm = [Buf() for _ in range(4)]
            for i in range(4):
                K.op("pool", lambda e, i=i: e.memset(vb[i][:], 1.0), w=[b_vb[i]])
            cnt = {"tr": 0, "pf": 0, "ob": 0, "vb": 0}

            def nxt(k, n):
                v = cnt[k] % n
                cnt[k] += 1
                return v

            groups = [(g * 4, 4) for g in range(NT // 4)]
            if NT % 4:
                groups.append((NT // 4 * 4, NT % 4))
            groups.append((NT, 2))
            for gi, (t0, ntl) in enumerate(groups):
                is_ctx = t0 >= NT
                ntok = ntl * 128
                tok0 = t0 * 128
                mi = 1 if is_ctx else 0
                hb = gi % 2
                if not is_ctx:
                    K.dma(cs[hb][:, :ntok], cosT_in[:, tok0:tok0 + ntok], B["IN"], b_cs[hb])
                    K.dma(sn[hb][:, :ntok], sinT_in[:, tok0:tok0 + ntok], B["IN"], b_sn[hb])
                for ti in range(ntl):
                    t = t0 + ti
                    xb = t % 2
                    src, sbuf_ = x_tile_ap(l, t)
                    K.dma(xt[xb][:], src, sbuf_, b_xt[xb])
                    K.op("dve", lambda e, xb=xb: e.bn_stats(out=stt[:, 0, :], in_=xt[xb][:, 0:512]), r=[b_xt[xb]], w=[b_stt])
                    K.op("dve", lambda e, xb=xb: e.bn_stats(out=stt[:, 1, :], in_=xt[xb][:, 512:1024]), r=[b_xt[xb]], w=[b_stt])
                    K.op("dve", lambda e: e.bn_aggr(out=mv[:], in_=stt[:].rearrange("p c s -> p (c s)")), r=[b_stt], w=[b_mv])
                    rstd_op(rs[:], mv[:, 1:2], 1.0, [b_mv], b_rs)
                    K.op("dve", lambda e: e.scalar_tensor_tensor(out=nb[:], in0=mv[:, 0:1], scalar=-1.0, in1=rs[:], op0=ALU.mult, op1=ALU.mult),
                         r=[b_mv, b_rs], w=[b_nb])
                    K.op("act", lambda e, xb=xb: e.activation(out=xn[xb][:], in_=xt[xb][:], func=AF.Identity, bias=nb[:], scale=rs[:]),
                         r=[b_xt[xb], b_nb, b_rs], w=[b_xn[xb]])
                    for dc in range(8):
                        p = nxt("tr", 2)
                        K.op("pe", lambda e, p=p, xb=xb, dc=dc: e.transpose(out=ptr[p][:], in_=xn[xb][:, dc * 128:(dc + 1) * 128], identity=ident[:]),
                             r=[b_xn[xb], b_ident], w=[b_ptr[p]])
                        K.op("dve", lambda e, p=p, dc=dc, ti=ti: e.tensor_scalar(out=hT[hb][:, dc, ti * 128:(ti + 1) * 128], in0=ptr[p][:],
                                                                                 scalar1=modfm[:, l, 8 + dc, mi:mi + 1], scalar2=modfm[:, l, dc, mi:mi + 1],
                                                                                 op0=ALU.mult, op1=ALU.add),
                             r=[b_ptr[p], b_mod], w=[b_hT[hb]])

                def fm_mm(col0, m, pbuf):
                    for dc in range(8):
                        K.op("pe", lambda e, dc=dc: e.matmul(pfm[pbuf][:m, :ntok], lhsT=wa[:, dc, col0:col0 + m], rhs=hT[hb][:, dc, :ntok],
                                                              start=(dc == 0), stop=(dc == 7)), r=[b_wa, b_hT[hb]], w=[b_pfm[pbuf]])

                def rope_out(col0, colr, m, dst_aps, dbuf):
                    pa = nxt("pf", 4)
                    fm_mm(col0, m, pa)
                    o = nxt("ob", 3)
                    if is_ctx or colr is None:
                        K.op("act", lambda e: e.copy(out=ob[o][:m, :ntok], in_=pfm[pa][:m, :ntok]), r=[b_pfm[pa]], w=[b_ob[o]])
                    else:
                        pb = nxt("pf", 4)
                        fm_mm(colr, m, pb)
                        K.op("dve", lambda e: e.tensor_tensor(out=t1[hb][:m, :ntok], in0=pfm[pa][:m, :ntok], in1=cs[hb][:m, :ntok], op=ALU.mult),
                             r=[b_pfm[pa], b_cs[hb]], w=[b_t1[hb]])
                        K.op("dve", lambda e: e.tensor_tensor(out=t2[hb][:m, :ntok], in0=pfm[pb][:m, :ntok], in1=sn[hb][:m, :ntok], op=ALU.mult),
                             r=[b_pfm[pb], b_sn[hb]], w=[b_t2[hb]])
                        K.op("pool", lambda e: e.tensor_tensor(out=ob[o][:m, :ntok], in0=t1[hb][:m, :ntok], in1=t2[hb][:m, :ntok], op=ALU.add),
                             r=[b_t1[hb], b_t2[hb]], w=[b_ob[o]])
                    for d_ in dst_aps:
                        K.dma(d_, ob[o][:m, :ntok], b_ob[o], dbuf)

                for c in range(4):
                    rope_out(C_AQ + c * 96, C_AQR + c * 96, 96, [qa_d[c, :, tok0:tok0 + ntok]], B["qa"])
                    rope_out(C_AK + c * 96, C_AKR + c * 96, 96, [ka_d[c, :, tok0:tok0 + ntok]], B["ka"])
                for c in range(3):
                    rope_out(C_NQ + c * 128, None, 128, [qn_d[c * 128:(c + 1) * 128, tok0:tok0 + ntok]], B["qn"])
                    rope_out(C_NK + c * 128, None, 128, [kn_d[c * 128:(c + 1) * 128, tok0:tok0 + ntok]], B["kn"])
                rope_out(C_KR, C_KRR, 32, [km_d[h, 64:96, tok0:tok0 + ntok] for h in range(4)], B["km"])

                for ti in range(ntl):
                    t = t0 + ti
                    for (col0, dst, dbuf) in ((C_AV, va_d, B["va"]), (C_NV, vn_d, B["vn"])):
                        pa = nxt("pf", 4)
                        for dc in range(8):
                            K.op("pe", lambda e, dc=dc, pa=pa, col0=col0: e.matmul(pfm[pa][:, :384], lhsT=hT[hb][:, dc, ti * 128:(ti + 1) * 128],
                                                                                   rhs=wa[:, dc, col0:col0 + 384], start=(dc == 0), stop=(dc == 7)),
                                 r=[b_wa, b_hT[hb]], w=[b_pfm[pa]])
                        v = nxt("vb", 4)
                        K.op("act", lambda e, pa=pa, v=v: e.copy(out=vb[v][:, :, 0:64], in_=pfm[pa][:, :384].rearrange("p (h d) -> p h d", d=64)),
                             r=[b_pfm[pa]], w=[b_vb[v]])
                        K.dma(dst[t * 128:(t + 1) * 128, :, :], vb[v][:], b_vb[v], dbuf)
                    pa = nxt("pf", 4)
                    for dc in range(8):
                        K.op("pe", lambda e, dc=dc, pa=pa: e.matmul(pfm[pa][:, :384], lhsT=hT[hb][:, dc, ti * 128:(ti + 1) * 128],
                                                                    rhs=wa[:, dc, C_CQ:C_CQ + 384], start=(dc == 0), stop=(dc == 7)),
                             r=[b_wa, b_hT[hb]], w=[b_pfm[pa]])
                    K.op("act", lambda e, pa=pa: e.copy(out=cqs[:], in_=pfm[pa][:, :384]), r=[b_pfm[pa]], w=[b_cqs])
                    K.op("dve", lambda e: e.scalar_tensor_tensor(out=junk[:, 0:256], in0=cqs[:, 0:256], scalar=1.0, in1=cqs[:, 0:256], op0=ALU.mult, op1=ALU.mult,
                                                                 accum_out=ssq[:, 0:1]), r=[b_cqs], w=[b_junk, b_ssq])
                    K.op("dve", lambda e: e.scalar_tensor_tensor(out=junk[:, 256:384], in0=cqs[:, 256:384], scalar=1.0, in1=cqs[:, 256:384], op0=ALU.mult, op1=ALU.mult,
                                                                 accum_out=ssq[:, 1:2]), r=[b_cqs], w=[b_junk, b_ssq])
                    rstd_op(ssq[:, 0:1], ssq[:, 0:1], 1.0 / 256, [b_ssq], b_ssq)
                    rstd_op(ssq[:, 1:2], ssq[:, 1:2], 1.0 / 128, [b_ssq], b_ssq)
                    K.op("dve", lambda e: e.scalar_tensor_tensor(out=cqn[:, 0:256], in0=cqs[:, 0:256], scalar=ssq[:, 0:1], in1=gq[:, 0:256], op0=ALU.mult, op1=ALU.mult),
                         r=[b_cqs, b_ssq, b_gq], w=[b_cqn])
                    K.op("dve", lambda e: e.scalar_tensor_tensor(out=cqn[:, 256:384], in0=cqs[:, 256:384], scalar=ssq[:, 1:2], in1=gq[:, 256:384], op0=ALU.mult, op1=ALU.mult),
                         r=[b_cqs, b_ssq, b_gq], w=[b_cqn])
                    for c in range(3):
                        p = nxt("tr", 2)
                        K.op("pe", lambda e, p=p, c=c: e.transpose(out=ptr[p][:], in_=cqn[:, c * 128:(c + 1) * 128], identity=ident[:]),
                             r=[b_cqn, b_ident], w=[b_ptr[p]])
                        K.op("act", lambda e, p=p, c=c: e.copy(out=cT[hb][:, c, ti * 128:(ti + 1) * 128], in_=ptr[p][:]), r=[b_ptr[p]], w=[b_cT[hb]])
                for h in range(4):
                    pa = nxt("pf", 4)
                    for rc in range(2):
                        K.op("pe", lambda e, rc=rc, pa=pa: e.matmul(pfm[pa][:96, :ntok], lhsT=wq[:, rc, h * 96:(h + 1) * 96], rhs=cT[hb][:, rc, :ntok],
                                                                    start=(rc == 0), stop=(rc == 1)), r=[b_wq, b_cT[hb]], w=[b_pfm[pa]])
                    o = nxt("ob", 3)
                    if is_ctx:
                        K.op("act", lambda e, pa=pa, o=o: e.copy(out=ob[o][:96, :ntok], in_=pfm[pa][:96, :ntok]), r=[b_pfm[pa]], w=[b_ob[o]])
                    else:
                        pb = nxt("pf", 4)
                        for rc in range(2):
                            K.op("pe", lambda e, rc=rc, pb=pb: e.matmul(pfm[pb][:96, :ntok], lhsT=wqr[:, rc, h * 96:(h + 1) * 96], rhs=cT[hb][:, rc, :ntok],
                                                                        start=(rc == 0), stop=(rc == 1)), r=[b_wqr, b_cT[hb]], w=[b_pfm[pb]])
                        K.op("act", lambda e, pa=pa, o=o: e.copy(out=ob[o][0:64, :ntok], in_=pfm[pa][0:64, :ntok]), r=[b_pfm[pa]], w=[b_ob[o]])
                        K.op("dve", lambda e, pa=pa: e.tensor_tensor(out=t1[hb][64:96, :ntok], in0=pfm[pa][64:96, :ntok], in1=cs[hb][64:96, :ntok], op=ALU.mult),
                             r=[b_pfm[pa], b_cs[hb]], w=[b_t1[hb]])
                        K.op("dve", lambda e, pb=pb: e.tensor_tensor(out=t2[hb][64:96, :ntok], in0=pfm[pb][64:96, :ntok], in1=sn[hb][64:96, :ntok], op=ALU.mult),
                             r=[b_pfm[pb], b_sn[hb]], w=[b_t2[hb]])
                        K.op("pool", lambda e, o=o: e.tensor_tensor(out=ob[o][64:96, :ntok], in0=t1[hb][64:96, :ntok], in1=t2[hb][64:96, :ntok], op=ALU.add),
                             r=[b_t1[hb], b_t2[hb]], w=[b_ob[o]])
                    K.dma(qm_d[h, :, tok0:tok0 + ntok], ob[o][:96, :ntok], b_ob[o], B["qm"])
                    pa = nxt("pf", 4)
                    K.op("pe", lambda e, pa=pa: e.matmul(pfm[pa][:64, :ntok], lhsT=wkn[:, 0, h * 64:(h + 1) * 64], rhs=cT[hb][:, 2, :ntok], start=True, stop=True),
                         r=[b_wkn, b_cT[hb]], w=[b_pfm[pa]])
                    o = nxt("ob", 3)
                    K.op("act", lambda e, pa=pa, o=o: e.copy(out=ob[o][:64, :ntok], in_=pfm[pa][:64, :ntok]), r=[b_pfm[pa]], w=[b_ob[o]])
                    K.dma(km_d[h, 0:64, tok0:tok0 + ntok], ob[o][:64, :ntok], b_ob[o], B["km"])
                for ti in range(ntl):
                    t = t0 + ti
                    pa = nxt("pf", 4)
                    K.op("pe", lambda e, pa=pa: e.matmul(pfm[pa][:, :256], lhsT=cT[hb][:, 2, ti * 128:(ti + 1) * 128], rhs=wkv[:, 0, :], start=True, stop=True),
                         r=[b_wkv, b_cT[hb]], w=[b_pfm[pa]])
                    v = nxt("vb", 4)
                    K.op("act", lambda e, pa=pa, v=v: e.copy(out=vb[v][:, 0:4, 0:64], in_=pfm[pa][:, :256].rearrange("p (h d) -> p h d", d=64)),
                         r=[b_pfm[pa]], w=[b_vb[v]])
                    K.dma(vm_d[t * 128:(t + 1) * 128, :, :], vb[v][:, 0:4, :], b_vb[v], B["vm"])
            K.barrier()


    def phaseB(l, do_ctx):
        lam_init = 0.8 - 0.6 * math.exp(-0.3 * l)
        for kind in DBG_KINDS:
            with ExitStack() as st:
                if kind == "a":
                    NH, q_d, k_d, v_d, bq, bk, bv, scale = 6, qa_d, ka_d, va_d, B["qa"], B["ka"], B["va"], DA_SCALE
                elif kind == "n":
                    NH, q_d, k_d, v_d, bq, bk, bv, scale = 6, qn_d, kn_d, vn_d, B["qn"], B["kn"], B["vn"], NA_SCALE
                else:
                    NH, q_d, k_d, v_d, bq, bk, bv, scale = 4, qm_d, km_d, vm_d, B["qm"], B["km"], B["vm"], MLA_SCALE
                NCH = 3 if kind == "n" else 4
                kT = sb(st, "kT", [128, NCH, NTOK], BF16); b_kT = Buf()
                vv = sb(st, "vv", [128, NKC, NH, 65], BF16); b_vv = Buf()
                if kind != "n":
                    for h in range(4):
                        K.dma(kT[:96, h, :], k_d[h], bk, b_kT)
                else:
                    for c in range(3):
                        K.dma(kT[:, c, :], k_d[c * 128:(c + 1) * 128, :], bk, b_kT)
                for c0 in range(0, NKC, 8):
                    c1 = min(NKC, c0 + 8)
                    K.dma(vv[:, c0:c1, :, :], v_d[c0 * 128:c1 * 128, :, :].rearrange("(c p) h d -> p c h d", p=128), bv, b_vv)
                NSLOT = {"a": 12, "n": 6, "m": 4}[kind]
                qT = [sb(st, f"qT{i}", [128, NSLOT, 512], BF16) for i in range(2)]; b_qT = [Buf(), Buf()]
                if kind != "m":
                    for i in range(2):
                        K.op("pool", lambda e, i=i: e.memset(qT[i][:], 0.0), w=[b_qT[i]])
                pT = [sb(st, f"pT{i}", [128, 1024], BF16) for i in range(3)]; b_pT = [Buf() for _ in range(3)]
                osb = [sb(st, f"osb{i}", [65, 512], F32) for i in range(2)]; b_osb = [Buf(), Buf()]
                mixt = sb(st, "mixt", [128, 4, 128], F32); b_mixt = Buf()
                mixo = [sb(st, f"mixo{i}", [128, 512], BF16) for i in range(2)]; b_mixo = [Buf(), Buf()]
                rc_ = sb(st, "rc_", [128, 2], F32); b_rc = Buf()
                o1 = sb(st, "o1", [128, 64], F32); b_o1 = Buf()
                o2 = sb(st, "o2", [128, 64], F32); b_o2 = Buf()
                jk = sb(st, "jk", [128, 64], F32); b_jk = Buf()
                s2 = sb(st, "s2", [128, 1], F32); b_s2 = Buf()
                sc = [[ps(st, f"sc{i}{s_}", [128, 512]) for s_ in range(2)] for i in range(2)]; b_sc = [[Buf(), Buf()], [Buf(), Buf()]]
                acc = [ps(st, f"acc{i}", [65, 512]) for i in range(2)]; b_acc = [Buf(), Buf()]
                pmisc = ps(st, "pmisc", [128, 512])
                _bp = Buf()
                ptk = [pmisc[:, 0:65], pmisc[:, 128:193]]; b_ptk = [_bp, _bp]
                ptm_t = ps(st, "ptm", [128, 128]); ptm = ptm_t[:]; b_ptm = Buf()
                cnt = {"sc": 0, "pT": 0, "mixo": 0, "q": 0}

                def nxt(k, n):
                    v = cnt[k] % n
                    cnt[k] += 1
                    return v

                if kind == "a":
                    dn = sb(st, "dn", [128, 64], F32); b_dn = Buf()
                    K.dma(dn[:], dnw[l].partition_broadcast(128), B["IN"], b_dn)
                    K.op("dve", lambda e: e.tensor_scalar_mul(out=dn[:], in0=dn[:], scalar1=1.0 - lam_init), r=[b_dn], w=[b_dn])
                if kind == "n":
                    tab = sb(st, "tab", [128, 6, NTAB, 128], F32); b_tab = Buf()
                    for h in range(6):
                        K.dma(tab[:, h, :, :], natab[l, h].rearrange("t k q -> k t q"), B["IN"], b_tab)
                    sbias = [sb(st, f"sbias{i}", [128, 256], F32) for i in range(2)]; b_sbias = [Buf(), Buf()]

                def attend(qb, qoff, nq, streams, kcs, tmap=None, hpair=0):
                    nk = len(kcs)
                    pend = None
                    one_bank = 2 * nq <= 512
                    for i, kc in enumerate(kcs):
                        sbi = nxt("sc", 2)
                        for s, (ch, nr, slot, vh) in enumerate(streams):
                            dst = sc[sbi][0][:, s * nq:(s + 1) * nq] if one_bank else sc[sbi][s][:, :nq]
                            K.op("pe", lambda e, s=s, ch=ch, nr=nr, slot=slot, kc=kc, sbi=sbi: e.matmul(
                                dst, lhsT=kT[:nr, ch, kc * 128:(kc + 1) * 128],
                                rhs=qT[qb][:nr, slot, qoff:qoff + nq], start=True, stop=True), r=[b_kT, b_qT[qb]],
                                w=[b_sc[sbi][0 if one_bank else s]])
                        pi = nxt("pT", 3)
                        if tmap is not None and kc in tmap:
                            assert one_bank
                            ti_ = tmap[kc]
                            K.op("dve", lambda e, sbi=sbi, ti_=ti_: e.scalar_tensor_tensor(
                                out=sbias[sbi][:, :2 * nq].rearrange("p (s q) -> p s q", s=2), in0=sc[sbi][0][:, :2 * nq].rearrange("p (s q) -> p s q", s=2),
                                scalar=scale, in1=tab[:, 2 * hpair:2 * hpair + 2, ti_, :], op0=ALU.mult, op1=ALU.add),
                                r=[b_sc[sbi][0], b_tab], w=[b_sbias[sbi]])
                            K.op("act", lambda e, sbi=sbi, pi=pi: e.activation(out=pT[pi][:, :2 * nq], in_=sbias[sbi][:, :2 * nq], func=AF.Exp),
                                 r=[b_sbias[sbi]], w=[b_pT[pi]])
                        elif one_bank:
                            K.op("act", lambda e, sbi=sbi, pi=pi: e.activation(out=pT[pi][:, :2 * nq], in_=sc[sbi][0][:, :2 * nq], func=AF.Exp, scale=scale),
                                 r=[b_sc[sbi][0]], w=[b_pT[pi]])
                        else:
                            for s in range(2):
                                K.op("act", lambda e, sbi=sbi, pi=pi, s=s: e.activation(out=pT[pi][:, s * nq:(s + 1) * nq], in_=sc[sbi][s][:, :nq], func=AF.Exp, scale=scale),
                                     r=[b_sc[sbi][s]], w=[b_pT[pi]])
                        if pend is not None:
                            pend()

                        def mk(i=i, kc=kc, pi=pi):
                            for s, (ch, nr, slot, vh) in enumerate(streams):
                                K.op("pe", lambda e, s=s, vh=vh: e.matmul(acc[s][:, :nq], lhsT=vv[:, kc, vh, :], rhs=pT[pi][:, s * nq:(s + 1) * nq],
                                                                          start=(i == 0), stop=(i == nk - 1)), r=[b_vv, b_pT[pi]], w=[b_acc[s]])
                        pend = mk
                    pend()
                    for s in range(2):
                        K.op("act", lambda e, s=s: e.copy(out=osb[s][:, :nq], in_=acc[s][:, :nq]), r=[b_acc[s]], w=[b_osb[s]])

                def fin(nq, slot0, diff_j):
                    for qi in range(nq // 128):
                        slot = slot0 + qi
                        for s in range(2):
                            K.op("pe", lambda e, s=s: e.transpose(out=ptk[s], in_=osb[s][:65, qi * 128:(qi + 1) * 128], identity=ident[:65, :65]),
                                 r=[b_osb[s], b_ident], w=[b_ptk[s]])
                        for s in range(2):
                            K.op("dve", lambda e, s=s: e.reciprocal(out=rc_[:, s:s + 1], in_=ptk[s][:, 64:65]), r=[b_ptk[s]], w=[b_rc])
                        if diff_j is None:
                            for s in range(2):
                                K.op("dve", lambda e, s=s: e.tensor_scalar(out=mixt[:, slot, s * 64:(s + 1) * 64], in0=ptk[s][:, 0:64], scalar1=rc_[:, s:s + 1],
                                                                            scalar2=None, op0=ALU.mult), r=[b_ptk[s], b_rc], w=[b_mixt])
                        else:
                            j = diff_j
                            K.op("dve", lambda e: e.tensor_scalar(out=o1[:], in0=ptk[0][:, 0:64], scalar1=rc_[:, 0:1], scalar2=None, op0=ALU.mult),
                                 r=[b_ptk[0], b_rc], w=[b_o1])
                            K.op("dve", lambda e: e.tensor_scalar(out=o2[:], in0=ptk[1][:, 0:64], scalar1=rc_[:, 1:2], scalar2=None, op0=ALU.mult),
                                 r=[b_ptk[1], b_rc], w=[b_o2])
                            K.op("dve", lambda e: e.scalar_tensor_tensor(out=o1[:], in0=o2[:], scalar=lam_sb[:, l, 0:1], in1=o1[:], op0=ALU.mult, op1=ALU.add),
                                 r=[b_o2, b_lam, b_o1], w=[b_o1])
                            K.op("pool", lambda e: e.memset(s2[:], 0.0), w=[b_s2])
                            K.op("dve", lambda e: e.scalar_tensor_tensor(out=jk[:], in0=o1[:], scalar=1.0, in1=o1[:], op0=ALU.mult, op1=ALU.mult, accum_out=s2[:]),
                                 r=[b_o1], w=[b_jk, b_s2])
                            rstd_op(s2[:], s2[:], 1.0 / 64, [b_s2], b_s2)
                            K.op("dve", lambda e: e.scalar_tensor_tensor(out=mixt[:, slot, j * 64:(j + 1) * 64], in0=o1[:], scalar=s2[:, 0:1], in1=dn[:],
                                                                         op0=ALU.mult, op1=ALU.mult), r=[b_o1, b_s2, b_dn], w=[b_mixt])

                def flush(slot, mo, col):
                    K.op("pe", lambda e: e.transpose(out=ptm, in_=mixt[:, slot, :], identity=ident[:]), r=[b_mixt, b_ident], w=[b_ptm])
                    K.op("act", lambda e: e.copy(out=mixo[mo][:, col:col + 128], in_=ptm), r=[b_ptm], w=[b_mixo[mo]])

                qblocks = [(g * 512, min(512, N - g * 512)) for g in range((N + 511) // 512)]
                if do_ctx:
                    qblocks.append((N, CTX))
                for (q0, nq) in qblocks:
                    is_ctx = q0 >= N
                    qb = nxt("q", 2)
                    if kind == "m":
                        for h in range(4):
                            K.dma(qT[qb][:96, h, :nq], q_d[h, :, q0:q0 + nq], bq, b_qT[qb])
                    elif kind == "a":
                        for g in range(12):
                            r0 = (g % 3) * 32
                            K.dma(qT[qb][r0:r0 + 32, g, :nq], q_d[g // 3, r0:r0 + 32, q0:q0 + nq], bq, b_qT[qb])
                    else:
                        for h in range(6):
                            r0 = (h % 2) * 64
                            K.dma(qT[qb][r0:r0 + 64, h, :nq], q_d[(h // 2) * 128 + r0:(h // 2) * 128 + r0 + 64, q0:q0 + nq], bq, b_qT[qb])
                    allk = list(range(NT, NT + 2)) if is_ctx else list(range(NKC))
                    for c in range(3 if kind != "m" else 2):
                        mo = nxt("mixo", 2)
                        if kind == "a":
                            for j in range(2):
                                hh_ = 2 * c + j
                                attend(qb, 0, nq, [((2 * hh_) // 3, 96, 2 * hh_, hh_), ((2 * hh_ + 1) // 3, 96, 2 * hh_ + 1, hh_)], allk)
                                fin(nq, 0, j)
                            for qi in range(nq // 128):
                                flush(qi, mo, qi * 128)
                        elif kind == "m":
                            attend(qb, 0, nq, [(2 * c, 96, 2 * c, 2 * c), (2 * c + 1, 96, 2 * c + 1, 2 * c + 1)], allk)
                            fin(nq, 0, None)
                            for qi in range(nq // 128):
                                flush(qi, mo, qi * 128)
                        else:
                            stn = [(c, 128, 2 * c, 2 * c), (c, 128, 2 * c + 1, 2 * c + 1)]
                            if is_ctx:
                                attend(qb, 0, nq, stn, allk)
                                fin(nq, 0, None)
                                for qi in range(nq // 128):
                                    flush(qi, mo, qi * 128)
                            else:
                                for qi in range(nq // 128):
                                    qp = (q0 + qi * 128) // 128
                                    tmap = {kp: ti_ for (kp, ti_) in plan[qp]}
                                    kcs = [kp for (kp, ti_) in plan[qp]] + [NT, NT + 1]
                                    attend(qb, qi * 128, 128, stn, kcs, tmap, c)
                                    fin(128, qi, None)
                                    flush(qi, mo, qi * 128)
                        cg = c + (0 if kind == "a" else 3 if kind == "n" else 6)
                        K.dma(mix_d[cg, :, q0:q0 + nq], mixo[mo][:, :nq], b_mixo[mo], B["mix"])
                K.barrier()

    def ln_norm(src, b_src, dst, b_dst, tmp):
        stt, mv, rs, nb, b_stt, b_mv, b_rs, b_nb = tmp
        K.op("dve", lambda e: e.bn_stats(out=stt[:, 0, :], in_=src[:, 0:512]), r=[b_src], w=[b_stt])
        K.op("dve", lambda e: e.bn_stats(out=stt[:, 1, :], in_=src[:, 512:1024]), r=[b_src], w=[b_stt])
        K.op("dve", lambda e: e.bn_aggr(out=mv[:], in_=stt[:].rearrange("p c s -> p (c s)")), r=[b_stt], w=[b_mv])
        rstd_op(rs[:], mv[:, 1:2], 1.0, [b_mv], b_rs)
        K.op("dve", lambda e: e.scalar_tensor_tensor(out=nb[:], in0=mv[:, 0:1], scalar=-1.0, in1=rs[:], op0=ALU.mult, op1=ALU.mult),
             r=[b_mv, b_rs], w=[b_nb])
        K.op("act", lambda e: e.activation(out=dst[:], in_=src[:], func=AF.Identity, bias=nb[:], scale=rs[:]), r=[b_src, b_nb, b_rs], w=[b_dst])

    def mk_tmp(st, pfx):
        return (sb(st, pfx + "stt", [128, 2, 6], F32), sb(st, pfx + "mv", [128, 2], F32), sb(st, pfx + "rs", [128, 1], F32), sb(st, pfx + "nb", [128, 1], F32),
                Buf(), Buf(), Buf(), Buf())

    def postnorm(py, b_py, xres, b_xres, gate, b_gate, g_bc, b_bc, b_gb, z, b_z, xn, b_xn, tmp):
        for hh in range(2):
            K.op("dve", lambda e, hh=hh: e.tensor_tensor(out=z[:, hh * 512:(hh + 1) * 512], in0=py[hh][:], in1=gate[:, hh * 512:(hh + 1) * 512], op=ALU.mult),
                 r=[b_py[hh], b_gate], w=[b_z])
        K.op("dve", lambda e: e.scalar_tensor_tensor(out=z[:], in0=xres[:], scalar=ALPHA, in1=z[:], op0=ALU.mult, op1=ALU.add), r=[b_xres, b_z], w=[b_z])
        ln_norm(z, b_z, xn, b_xn, tmp)
        K.op("dve", lambda e: e.tensor_tensor(out=xn[:], in0=xn[:], in1=g_bc[:], op=ALU.mult), r=[b_xn, b_gb[0]], w=[b_xn])
        K.op("pool", lambda e: e.tensor_tensor(out=z[:], in0=xn[:], in1=b_bc[:], op=ALU.add), r=[b_xn, b_gb[1]], w=[b_z])

    def bc_load(st, name, src_row, dbuf_src):
        t = sb(st, name, [128, D], F32)
        b = Buf()
        K.dma(t[:], src_row.partition_broadcast(128), dbuf_src, b)
        return t, b

    def tok_tiles(do_ctx):
        return list(range(NT)) + ([NT, NT + 1] if do_ctx else [])

    def phaseC1(l, do_ctx):
        with ExitStack() as st:
            wo = sb(st, "wo", [128, 8, D], BF16); b_wo = Buf()
            load_weight(st, "wo", w_out[l], 8, D, wo, b_wo, colblk=1024)
            gate = [bc_load(st, f"gate{m}", ada_d[l, m, 0, :], B["ada"]) for m in range(2)]
            g_bc, b_g = bc_load(st, "g_bc", lnp[l, 0, :], B["IN"])
            b_bc, b_b = bc_load(st, "b_bc", lnp[l, 1, :], B["IN"])
            b_gb = (b_g, b_b)
            mT = [sb(st, f"mT{i}", [128, 8, 128], BF16) for i in range(2)]; b_mT = [Buf(), Buf()]
            xr = [sb(st, f"xr{i}", [128, D], F32) for i in range(2)]; b_xr = [Buf(), Buf()]
            z = [sb(st, f"z{i}", [128, D], F32) for i in range(2)]; b_z = [Buf(), Buf()]
            xn = sb(st, "xn", [128, D], F32); b_xn = Buf()
            xn2 = sb(st, "xn2", [128, D], F32); b_xn2 = Buf()
            h2 = [sb(st, f"h2{i}", [128, 8, 128], BF16) for i in range(2)]; b_h2 = [Buf(), Buf()]
            tmp = mk_tmp(st, "c1")
            py = [ps(st, f"py{i}", [128, 512]) for i in range(2)]; b_py = [Buf(), Buf()]
            ptr = [ps(st, f"ptr{i}", [128, 128]) for i in range(2)]; b_ptr = [Buf(), Buf()]
            ntr = 0
            mi_of = lambda t: 1 if t >= NT else 0
            for t in tok_tiles(do_ctx):
                i = t % 2
                mi = mi_of(t)
                K.dma(mT[i][:], mix_d[:, :, t * 128:(t + 1) * 128].rearrange("c p t -> p c t"), B["mix"], b_mT[i])
                src, sbuf_ = x_tile_ap(l, t)
                K.dma(xr[i][:], src, sbuf_, b_xr[i])
                for hh in range(2):
                    for fc in range(8):
                        K.op("pe", lambda e, hh=hh, fc=fc: e.matmul(py[hh][:], lhsT=mT[i][:, fc, :], rhs=wo[:, fc, hh * 512:(hh + 1) * 512],
                                                                    start=(fc == 0), stop=(fc == 7)), r=[b_mT[i], b_wo], w=[b_py[hh]])
                postnorm(py, b_py, xr[i], b_xr[i], gate[mi][0], gate[mi][1], g_bc, b_bc, b_gb, z[i], b_z[i], xn, b_xn, tmp)
                K.dma(x1_d[t * 128:(t + 1) * 128, :], z[i][:], b_z[i], B["x1"])
                ln_norm(z[i], b_z[i], xn2, b_xn2, tmp)
                for dc in range(8):
                    p = ntr % 2
                    ntr += 1
                    K.op("pe", lambda e, p=p, dc=dc: e.transpose(out=ptr[p][:], in_=xn2[:, dc * 128:(dc + 1) * 128], identity=ident[:]),
                         r=[b_xn2, b_ident], w=[b_ptr[p]])
                    K.op("dve", lambda e, p=p, dc=dc: e.tensor_scalar(out=h2[i][:, dc, :], in0=ptr[p][:], scalar1=modfm[:, l, 32 + dc, mi:mi + 1],
                                                                      scalar2=modfm[:, l, 24 + dc, mi:mi + 1], op0=ALU.mult, op1=ALU.add),
                         r=[b_ptr[p], b_mod], w=[b_h2[i]])
                K.dma(h2_d[:, :, t * 128:(t + 1) * 128].rearrange("c p t -> p c t"), h2[i][:], b_h2[i], B["h2"])
            K.barrier()

    def phaseC2(l, do_ctx):
        with ExitStack() as st:
            wu = sb(st, "wu", [128, 8, 2 * DFF], BF16); b_wu = Buf()
            load_weight(st, "wu", w_up[l], 8, 2 * DFF, wu, b_wu, colblk=1408)
            cp = sb(st, "cp", [128, 44, 4], F32); b_cp = Buf()
            K.dma(cp[:], convp[l], B["IN"], b_cp)
            hg = [sb(st, f"hg{i}", [128, 8, 514], BF16) for i in range(2)]; b_hg = [Buf(), Buf()]
            u = [[sb(st, f"u{g}{i}", [128, 514], F32) for i in range(2)] for g in range(2)]; b_u = [[Buf(), Buf()], [Buf(), Buf()]]
            cv = [[sb(st, f"cv{g}{i}", [128, 512], F32) for i in range(2)] for g in range(2)]; b_cv = [[Buf(), Buf()], [Buf(), Buf()]]
            ao = [sb(st, f"ao{i}", [128, 512], BF16) for i in range(2)]; b_ao = [Buf(), Buf()]
            pum = [[ps(st, f"pum{g}{i}", [128, 512]) for i in range(2)] for g in range(2)]; b_pum = [[Buf(), Buf()], [Buf(), Buf()]]
            puh_t = ps(st, "puh", [128, 512])
            puh = [[puh_t[:, (g * 2 + i) * 8:(g * 2 + i) * 8 + 2] for i in range(2)] for g in range(2)]; _bh = Buf(); b_puh = [[_bh, _bh], [_bh, _bh]]
            seqs = [(0, N)] + ([(N, N + CTX)] if do_ctx else [])
            gi = 0
            it = 0
            for (s0, s1) in seqs:
                for tok0 in range(s0, s1, 512):
                    ntok = min(512, s1 - tok0)
                    hb = gi % 2
                    gi += 1
                    lo = tok0 - 1 if tok0 > s0 else tok0
                    hi = tok0 + ntok + 1 if tok0 + ntok < s1 else tok0 + ntok
                    if lo == tok0 or hi == tok0 + ntok:
                        K.op("pool", lambda e, hb=hb: e.memset(hg[hb][:], 0.0), w=[b_hg[hb]])
                    K.dma(hg[hb][:, :, lo - tok0 + 1:hi - tok0 + 1], h2_d[:, :, lo:hi].rearrange("c p t -> p c t"), B["h2"], b_hg[hb])
                    for j in range(22):
                        i = it % 2
                        it += 1
                        for g in range(2):
                            col0 = g * DFF + j * 128
                            for dc in range(8):
                                K.op("pe", lambda e, g=g, dc=dc, col0=col0: e.matmul(pum[g][i][:, :ntok], lhsT=wu[:, dc, col0:col0 + 128], rhs=hg[hb][:, dc, 0:ntok],
                                                                                     start=(dc == 0), stop=(dc == 7)), r=[b_wu, b_hg[hb]], w=[b_pum[g][i]])
                            for dc in range(8):
                                K.op("pe", lambda e, g=g, dc=dc, col0=col0: e.matmul(puh[g][i], lhsT=wu[:, dc, col0:col0 + 128], rhs=hg[hb][:, dc, ntok:ntok + 2],
                                                                                     start=(dc == 0), stop=(dc == 7)), r=[b_wu, b_hg[hb]], w=[b_puh[g][i]])
                            K.op("act", lambda e, g=g: e.copy(out=u[g][i][:, 0:ntok], in_=pum[g][i][:, :ntok]), r=[b_pum[g][i]], w=[b_u[g][i]])
                            K.op("act", lambda e, g=g: e.copy(out=u[g][i][:, ntok:ntok + 2], in_=puh[g][i]), r=[b_puh[g][i]], w=[b_u[g][i]])
                            ch = g * 22 + j
                            eng = "dve"
                            K.op(eng, lambda e, g=g, ch=ch: e.tensor_scalar(out=cv[g][i][:, :ntok], in0=u[g][i][:, 0:ntok], scalar1=cp[:, ch, 0:1], scalar2=cp[:, ch, 3:4],
                                                                             op0=ALU.mult, op1=ALU.add), r=[b_u[g][i], b_cp], w=[b_cv[g][i]])
                            for k in (1, 2):
                                K.op(eng, lambda e, g=g, ch=ch, k=k: e.scalar_tensor_tensor(out=cv[g][i][:, :ntok], in0=u[g][i][:, k:k + ntok], scalar=cp[:, ch, k:k + 1],
                                                                                            in1=cv[g][i][:, :ntok], op0=ALU.mult, op1=ALU.add),
                                     r=[b_u[g][i], b_cp, b_cv[g][i]], w=[b_cv[g][i]])
                        K.op("act", lambda e: e.activation(out=cv[0][i][:, :ntok], in_=cv[0][i][:, :ntok], func=AF.Silu), r=[b_cv[0][i]], w=[b_cv[0][i]])
                        K.op("dve", lambda e: e.tensor_tensor(out=ao[i][:, :ntok], in0=cv[0][i][:, :ntok], in1=cv[1][i][:, :ntok], op=ALU.mult),
                             r=[b_cv[0][i], b_cv[1][i]], w=[b_ao[i]])
                        K.dma(at_d[j, :, tok0:tok0 + ntok], ao[i][:, :ntok], b_ao[i], B["at"])
            K.barrier()

    def phaseC3(l, do_ctx):
        with ExitStack() as st:
            wd = sb(st, "wd", [128, 22, D], BF16); b_wd = Buf()
            load_weight(st, "wd", w_down[l], 22, D, wd, b_wd, colblk=1024)
            gate = [bc_load(st, f"gate{m}", ada_d[l, m, 1, :], B["ada"]) for m in range(2)]
            g_bc, b_g = bc_load(st, "g_bc", lnp[l, 2, :], B["IN"])
            b_bc, b_b = bc_load(st, "b_bc", lnp[l, 3, :], B["IN"])
            b_gb = (b_g, b_b)
            aT = [sb(st, f"aT{i}", [128, 22, 128], BF16) for i in range(2)]; b_aT = [Buf(), Buf()]
            xr = [sb(st, f"xr{i}", [128, D], F32) for i in range(2)]; b_xr = [Buf(), Buf()]
            z = [sb(st, f"z{i}", [128, D], F32) for i in range(2)]; b_z = [Buf(), Buf()]
            xn = sb(st, "xn", [128, D], F32); b_xn = Buf()
            tmp = mk_tmp(st, "c3")
            py = [ps(st, f"py{i}", [128, 512]) for i in range(2)]; b_py = [Buf(), Buf()]
            for t in tok_tiles(do_ctx):
                i = t % 2
                mi = 1 if t >= NT else 0
                K.dma(aT[i][:], at_d[:, :, t * 128:(t + 1) * 128].rearrange("c p t -> p c t"), B["at"], b_aT[i])
                K.dma(xr[i][:], x1_d[t * 128:(t + 1) * 128, :], B["x1"], b_xr[i])
                for hh in range(2):
                    for fc in range(22):
                        K.op("pe", lambda e, hh=hh, fc=fc: e.matmul(py[hh][:], lhsT=aT[i][:, fc, :], rhs=wd[:, fc, hh * 512:(hh + 1) * 512],
                                                                    start=(fc == 0), stop=(fc == 21)), r=[b_aT[i], b_wd], w=[b_py[hh]])
                postnorm(py, b_py, xr[i], b_xr[i], gate[mi][0], gate[mi][1], g_bc, b_bc, b_gb, z[i], b_z[i], xn, b_xn, tmp)
                dst, dbuf = x_out_ap(t)
                K.dma(dst, z[i][:], b_z[i], dbuf)
            K.barrier()

    for l in range(DEPTH):
        do_ctx = l < DEPTH - 1
        for nm, fn in (("A", lambda: phaseA(l)), ("B", lambda: phaseB(l, do_ctx)), ("C1", lambda: phaseC1(l, do_ctx)),
                       ("C2", lambda: phaseC2(l, do_ctx)), ("C3", lambda: phaseC3(l, do_ctx))):
            if stop_after is not None and stop_after == "0":
                break
            fn()
            if stop_after == nm:
                break
        if stop_after is not None:
            break
    K.barrier()
    es.close()
    return nc


_NC_CACHE = {}


def _host_inputs(N, DEPTH, x, c, ctx, c_ctx, w_ada, b_ada, w_in, lam_q1, lam_k1, lam_q2, lam_k2, diff_norm_w, na_rpb,
                 mla_q_norm_w, mla_kv_norm_w, w_uq, w_ukv, w_out, ln1_g, ln1_b, w_up, conv_w, conv_b, w_down, ln2_g, ln2_b):
    f = lambda a: np.ascontiguousarray(np.asarray(a, dtype=np.float32))
    L = DEPTH
    w_in = f(w_in)[:L]
    plan, tkeys = _na_plan(N)
    cosT, sinT = _rope_tables(N)
    w_ukv = f(w_ukv)[:L].reshape(L, 128, 4, 128)
    shared = {
        "w_ada": f(w_ada)[:L], "b_ada": f(b_ada)[:L],
        "w_a": np.ascontiguousarray(w_in[:, :, _win_cols()]),
        "lamv": np.ascontiguousarray(np.stack([np.stack([f(lam_q1)[:L], f(lam_q2)[:L]], axis=1), np.stack([f(lam_k1)[:L], f(lam_k2)[:L]], axis=1)], axis=1)),
        "dnw": f(diff_norm_w)[:L],
        "natab": _na_tables(f(na_rpb)[:L], tkeys),
        "qnw": np.ascontiguousarray(np.concatenate([f(mla_q_norm_w)[:L], f(mla_kv_norm_w)[:L]], axis=1)),
        "w_uq": f(w_uq)[:L], "w_uqr": np.ascontiguousarray(f(w_uq)[:L][:, :, _wuq_rot_cols()]),
        "w_ukn": np.ascontiguousarray(w_ukv[:, :, :, 0:64].reshape(L, 128, 256)),
        "w_ukv": np.ascontiguousarray(w_ukv[:, :, :, 64:128].reshape(L, 128, 256)),
        "w_out": f(w_out)[:L], "w_up": f(w_up)[:L], "w_down": f(w_down)[:L],
        "lnp": np.ascontiguousarray(np.stack([f(ln1_g)[:L], f(ln1_b)[:L], f(ln2_g)[:L], f(ln2_b)[:L]], axis=1)),
        "convp": np.ascontiguousarray(np.concatenate([f(conv_w)[:L], f(conv_b)[:L][:, None, :]], axis=1).reshape(L, 4, 44, 128).transpose(0, 3, 2, 1)),
        "ident": np.eye(128, dtype=np.float32), "cosT": cosT, "sinT": sinT,
    }
    x = f(x); ctx = f(ctx); c = f(c); c_ctx = f(c_ctx)
    maps = []
    for b in range(x.shape[0]):
        m = dict(shared)
        m["x"] = x[b]
        m["ctx"] = ctx[b]
        m["cc"] = np.ascontiguousarray(np.stack([c[b], c_ctx], axis=0).reshape(2, 8, 128).transpose(2, 1, 0))
        maps.append(m)
    return maps


def run(N, DEPTH, inputs, stop_after=None):
    key = (N, DEPTH, stop_after)
    if key not in _NC_CACHE:
        _NC_CACHE[key] = build(N, DEPTH, stop_after)
    nc = _NC_CACHE[key]
    maps = _host_inputs(N, DEPTH, **inputs)
    res = run_bass_kernel_spmd(nc, maps, core_ids=list(range(len(maps))))
    return np.stack([np.asarray(r["y"], dtype=np.float32) for r in res.results], axis=0)


def kernel(**inputs):
    N = inputs["x"].shape[1]
    return run(N, DEPTH_FULL, inputs)
```

```python
import math
from contextlib import ExitStack

import numpy as np
import concourse.bass as bass
import concourse.mybir as mybir
from concourse.bass_utils import run_bass_kernel_spmd

F32 = mybir.dt.float32
BF16 = mybir.dt.bfloat16
AF = mybir.ActivationFunctionType
ALU = mybir.AluOpType

D = 1024
CTX = 256
GW = 64
DFF = 2816
NEG = -30000.0
LN_EPS = 1e-6
DEPTH_FULL = 4
DBG_KINDS = ("a", "n", "m")
MERGE_EXP = True
ALPHA = (2 * DEPTH_FULL) ** 0.25
DA_SCALE = 32 ** -0.5
NA_SCALE = 64 ** -0.5
MLA_SCALE = 96 ** -0.5

O_AQ, O_AK, O_AV, O_NQ, O_NK, O_NV, O_CQ, O_CKV, O_KR = 0, 384, 768, 1152, 1536, 1920, 2304, 2560, 2688


def _rot_src(n):
    f = np.arange(n)
    j = f % 16
    return np.where(j < 8, f + 8, f - 8)


def _win_cols():
    aq = np.arange(384)
    cols = []
    cols.append(O_AQ + aq)
    cols.append(O_AQ + _rot_src(384))
    cols.append(O_AK + aq)
    cols.append(O_AK + _rot_src(384))
    cols.append(O_NQ + aq)
    cols.append(O_NK + aq)
    cols.append(O_KR + np.arange(32))
    cols.append(O_KR + _rot_src(32))
    cols.append(O_AV + aq)
    cols.append(O_NV + aq)
    cols.append(O_CQ + np.arange(384))
    return np.concatenate(cols)


WA_COLS = 3520
C_AQ, C_AQR, C_AK, C_AKR, C_NQ, C_NK, C_KR, C_KRR, C_AV, C_NV, C_CQ = 0, 384, 768, 1152, 1536, 1920, 2304, 2336, 2368, 2752, 3136


def _wuq_rot_cols():
    c = np.arange(384)
    h, f = c // 96, c % 96
    r = f - 64
    rr = np.where(r % 16 < 8, r + 8, r - 8)
    return np.where(f < 64, c, h * 96 + 64 + rr)


def _rope_tables(n):
    t = np.arange(n)
    row = (t // GW).astype(np.float32)
    col = (t % GW).astype(np.float32)
    inv = (10000.0 ** (-np.arange(0, 16, 2, dtype=np.float32) / 16)).astype(np.float32)
    cosT = np.zeros((128, n), np.float32)
    sinT = np.zeros((128, n), np.float32)
    for p in range(128):
        f = p % 32
        pos = row if f < 16 else col
        j = f % 16
        ang = (pos * inv[j % 8]).astype(np.float32)
        cosT[p] = np.cos(ang)
        sinT[p] = np.sin(ang) * (-1.0 if j < 8 else 1.0)
    return cosT, sinT


def _na_plan(n):
    rows = n // GW
    kh = min(8, rows)
    uniq = {}
    plan = []
    for qp in range(rows // 2):
        r0a = min(max(2 * qp - kh // 2, 0), rows - kh)
        r0b = min(max(2 * qp + 1 - kh // 2, 0), rows - kh)
        lo, hi = r0a // 2, (r0b + kh - 1) // 2
        ent = []
        for kp in range(lo, hi + 1):
            key = []
            for khalf in range(2):
                for qhalf in range(2):
                    kr, qr = 2 * kp + khalf, 2 * qp + qhalf
                    r0 = min(max(qr - kh // 2, 0), rows - kh)
                    key.append(kr - qr + 7 if (r0 <= kr < r0 + kh) else -1)
            key = tuple(key)
            if key not in uniq:
                uniq[key] = len(uniq)
            ent.append((kp, uniq[key]))
        plan.append(ent)
    return plan, list(uniq.keys())


def _na_tables(rpb, keys):
    L = rpb.shape[0]
    c = np.arange(GW)
    c0 = np.clip(c - 8, 0, GW - 16)
    band = (c[None, :] >= c0[:, None]) & (c[None, :] < c0[:, None] + 16)
    dc = np.clip(c[None, :] - c[:, None], -15, 15) + 15
    out = np.full((L, 6, len(keys), 128, 128), NEG, np.float32)
    for ti, key in enumerate(keys):
        for khalf in range(2):
            for qhalf in range(2):
                a = key[khalf * 2 + qhalf]
                if a < 0:
                    continue
                blk = rpb[:, :, a, :][:, :, dc.T]
                blk = np.where(band.T[None, None], blk, np.float32(NEG))
                out[:, :, ti, khalf * 64:(khalf + 1) * 64, qhalf * 64:(qhalf + 1) * 64] = blk
    return out


class Buf:
    __slots__ = ("w", "r", "dsem", "keep", "name")

    def __init__(self, name="", keep=False):
        self.w = None
        self.r = {}
        self.dsem = None
        self.keep = keep
        self.name = name


class Sem:
    _n = 0

    def __init__(self, h):
        self.h = h
        Sem._n += 1
        self.idx = Sem._n
        self.cnt = 0


class Sched:
    ROLL = 30000

    def __init__(self, nc, es):
        self.nc = nc
        self.es = es
        self.eng = {"pe": nc.tensor, "act": nc.scalar, "dve": nc.vector, "pool": nc.gpsimd, "sp": nc.sync}
        self.sem = {}
        self.cnt = {}
        self.waited = {e: {} for e in self.eng}
        self.nsem = 0
        for e in self.eng:
            self.sem[e] = self.newsem(e)
            self.cnt[e] = 0
        self.dsems = []
        self.dfree = []
        self.scope = []

    def release_scope(self):
        for b in self.scope:
            if not b.keep and b.dsem is not None:
                self.dfree.append(b.dsem)
                b.dsem = None
        self.scope = [b for b in self.scope if b.keep and False]

    def newsem(self, name):
        self.nsem += 1
        return Sem(self.es.enter_context(self.nc.semaphore(f"{name}_{self.nsem}")))

    def _deps(self, r, w):
        deps = []
        for b in r:
            if b.w is not None:
                deps.append(b.w)
        for b in w:
            if b.w is not None:
                deps.append(b.w)
            deps.extend(b.r.values())
        return deps

    def _wait(self, e, deps):
        eng = self.eng[e]
        wd = self.waited[e]
        for (se, sem, val) in deps:
            if se == "pe" and e == "pe":
                continue
            if wd.get(sem.idx, 0) >= val:
                continue
            eng.wait_ge(sem.h, val)
            wd[sem.idx] = val

    def op(self, e, fn, r=(), w=()):
        self._wait(e, self._deps(r, w))
        ins = fn(self.eng[e])
        if self.cnt[e] >= self.ROLL:
            self.sem[e] = self.newsem(e)
            self.cnt[e] = 0
        self.cnt[e] += 1
        ins.then_inc(self.sem[e].h, 1)
        tok = (e, self.sem[e], self.cnt[e])
        for b in r:
            b.r[(e, self.sem[e].idx)] = tok
        for b in w:
            b.w = tok
            b.r = {}
        return tok

    def dma(self, out, in_, rd, wr, q="sp"):
        rd_dram = rd.name.startswith("D:")
        wr_dram = wr.name.startswith("D:")
        assert rd_dram != wr_dram
        own = rd if wr_dram else wr
        deps = self._deps([] if rd_dram else [rd], [] if wr_dram else [wr])
        self._wait(q, deps)
        ins = self.eng[q].dma_start(out=out, in_=in_)
        if own.dsem is None:
            while self.dfree and self.dfree[-1].cnt > 40000:
                self.dfree.pop()
            if self.dfree:
                own.dsem = self.dfree.pop()
            else:
                own.dsem = self.newsem("d")
                self.dsems.append(own.dsem)
            self.scope.append(own)
        own.dsem.cnt += 16
        assert own.dsem.cnt < 65000, "DMA semaphore overflow"
        ins.then_inc(own.dsem.h, 16)
        tok = ("dma", own.dsem, own.dsem.cnt)
        if wr_dram:
            rd.r[("dma", own.dsem.idx)] = tok
        else:
            wr.w = tok
            wr.r = {}
        return tok

    def barrier(self):
        toks = [(e, self.sem[e], self.cnt[e]) for e in self.eng if self.cnt[e] > 0]
        toks += [("dma", d, d.cnt) for d in self.dsems if d.cnt > 0]
        for e in self.eng:
            self._wait(e, toks)
        self.release_scope()


class Ctx:
    pass


def build(N, DEPTH, stop_after=None):
    NT = N // 128
    NTOK = N + CTX
    NTT = NTOK // 128
    NKC = NTT
    nc = bass.Bass("TRN2", target_bir_lowering=False)
    es = ExitStack()
    K = Sched(nc, es)

    def din(name, shape, dt=F32):
        return nc.dram_tensor(name, list(shape), dt, kind="ExternalInput").ap()

    def dscr(name, shape, dt):
        return nc.dram_tensor(name, list(shape), dt, kind="Internal").ap()

    x_in = din("x", [N, D]); ctx_in = din("ctx", [CTX, D]); cc_in = din("cc", [128, 8, 2])
    w_ada = din("w_ada", [DEPTH, D, 6 * D]); b_ada = din("b_ada", [DEPTH, 6 * D])
    w_a = din("w_a", [DEPTH, D, WA_COLS])
    lamv = din("lamv", [DEPTH, 2, 2, 32]); dnw = din("dnw", [DEPTH, 64])
    plan, tkeys = _na_plan(N)
    NTAB = len(tkeys)
    natab = din("natab", [DEPTH, 6, NTAB, 128, 128])
    qnw = din("qnw", [DEPTH, 384])
    w_uq = din("w_uq", [DEPTH, 256, 384]); w_uqr = din("w_uqr", [DEPTH, 256, 384])
    w_ukn = din("w_ukn", [DEPTH, 128, 256]); w_ukv = din("w_ukv", [DEPTH, 128, 256])
    w_out = din("w_out", [DEPTH, D, D]); w_up = din("w_up", [DEPTH, D, 2 * DFF]); w_down = din("w_down", [DEPTH, DFF, D])
    lnp = din("lnp", [DEPTH, 4, D])
    convp = din("convp", [DEPTH, 128, 44, 4])
    ident_in = din("ident", [128, 128]); cosT_in = din("cosT", [128, N]); sinT_in = din("sinT", [128, N])
    y_out = nc.dram_tensor("y", [N, D], F32, kind="ExternalOutput").ap()

    xc_d = dscr("xc_d", [CTX, D], F32)
    x1_d = dscr("x1_d", [NTOK, D], F32)
    ada_d = dscr("ada_d", [DEPTH, 2, 2, D], F32)
    qa_d = dscr("qa_d", [4, 96, NTOK], BF16); ka_d = dscr("ka_d", [4, 96, NTOK], BF16)
    qn_d = dscr("qn_d", [384, NTOK], BF16); kn_d = dscr("kn_d", [384, NTOK], BF16)
    qm_d = dscr("qm_d", [4, 96, NTOK], BF16); km_d = dscr("km_d", [4, 96, NTOK], BF16)
    va_d = dscr("va_d", [NTOK, 6, 65], BF16); vn_d = dscr("vn_d", [NTOK, 6, 65], BF16); vm_d = dscr("vm_d", [NTOK, 4, 65], BF16)
    mix_d = dscr("mix_d", [8, 128, NTOK], BF16)
    h2_d = dscr("h2_d", [8, 128, NTOK], BF16)
    at_d = dscr("at_d", [22, 128, NTOK], BF16)
    B = {n: Buf("D:" + n, keep=True) for n in ["y", "xc", "x1", "ada", "qa", "ka", "qn", "kn", "qm", "km", "va", "vn", "vm", "mix", "h2", "at", "IN"]}

    uid = [0]

    def sb(st, name, shape, dt):
        uid[0] += 1
        return st.enter_context(nc.sbuf_tensor(f"s{uid[0]}_{name}", list(shape), dt))

    def ps(st, name, shape, dt=F32):
        uid[0] += 1
        return st.enter_context(nc.psum_tensor(f"p{uid[0]}_{name}", list(shape), dt))

    ident = sb(es, "ident", [128, 128], F32); b_ident = Buf("ident", True)
    epsb = sb(es, "epsb", [128, 1], F32); b_eps = Buf()
    modfm = sb(es, "modfm", [128, DEPTH, 48, 2], F32); b_mod = Buf()
    lam_sb = sb(es, "lam_sb", [128, DEPTH, 2], F32); b_lam = Buf()
    K.dma(ident[:], ident_in, B["IN"], b_ident)
    K.op("pool", lambda e: e.memset(epsb[:], LN_EPS), w=[b_eps])

    def x_tile_ap(l, t, final=False):
        if t < NT:
            src = x_in if l == 0 else y_out
            return src[t * 128:(t + 1) * 128, :], (B["IN"] if l == 0 else B["y"])
        src = ctx_in if l == 0 else xc_d
        return src[(t - NT) * 128:(t - NT + 1) * 128, :], (B["IN"] if l == 0 else B["xc"])

    def x_out_ap(t):
        if t < NT:
            return y_out[t * 128:(t + 1) * 128, :], B["y"]
        return xc_d[(t - NT) * 128:(t - NT + 1) * 128, :], B["xc"]

    def load_weight(st, name, src, nchunk, ncols, dst, dbuf, colblk=2048):
        stg = [sb(st, f"{name}_s{i}", [128, colblk], F32) for i in range(2)]
        sbf = [Buf() for _ in range(2)]
        i = 0
        for c in range(nchunk):
            for c0 in range(0, ncols, colblk):
                cw = min(colblk, ncols - c0)
                K.dma(stg[i % 2][:, :cw], src[c * 128:(c + 1) * 128, c0:c0 + cw], B["IN"], sbf[i % 2])
                s_ = stg[i % 2]
                K.op("pool", lambda e, s_=s_, c=c, c0=c0, cw=cw: e.tensor_copy(out=dst[:, c, c0:c0 + cw], in_=s_[:, :cw]),
                     r=[sbf[i % 2]], w=[dbuf])
                i += 1

    def rstd_op(out_ap, in_ap, scale, rbufs, wbuf):
        K.op("act", lambda e: e.activation(out=out_ap, in_=in_ap, func=AF.Ln, bias=epsb[:], scale=scale), r=rbufs + [b_eps], w=[wbuf])
        K.op("act", lambda e: e.activation(out=out_ap, in_=out_ap, func=AF.Exp, scale=-0.5), r=[wbuf], w=[wbuf])

    with ExitStack() as st:
        csb = sb(st, "csb", [128, 8, 2], F32); b_c = Buf()
        K.dma(csb[:], cc_in, B["IN"], b_c)
        K.op("act", lambda e: e.activation(out=csb[:], in_=csb[:], func=AF.Silu), r=[b_c], w=[b_c])
        ones2 = sb(st, "ones2", [1, 2], F32); b_o2 = Buf()
        K.op("pool", lambda e: e.memset(ones2[:], 1.0), w=[b_o2])
        wst = [sb(st, f"wst{i}", [128, 8, 512], F32) for i in range(2)]; b_wst = [Buf(), Buf()]
        bst = [sb(st, f"bst{i}", [1, 512], F32) for i in range(2)]; b_bst = [Buf(), Buf()]
        pfm = [ps(st, f"pfm{i}", [128, 4, 2]) for i in range(2)]; b_pfm = [Buf(), Buf()]
        prow = [ps(st, f"prow{i}", [2, 512]) for i in range(2)]; b_prow = [Buf(), Buf()]
        rsb = [sb(st, f"rsb{i}", [2, 512], F32) for i in range(2)]; b_rsb = [Buf(), Buf()]
        it = 0
        for l in range(DEPTH):
            for cb in range(12):
                j = it % 2
                it += 1
                K.dma(wst[j][:], w_ada[l, :, cb * 512:(cb + 1) * 512].rearrange("(c p) n -> p c n", p=128), B["IN"], b_wst[j])
                K.dma(bst[j][:], b_ada[l:l + 1, cb * 512:(cb + 1) * 512], B["IN"], b_bst[j])
                for cc in range(4):
                    for dc in range(8):
                        K.op("pe", lambda e, j=j, cc=cc, dc=dc: e.matmul(pfm[j][:, cc, :], lhsT=wst[j][:, dc, cc * 128:(cc + 1) * 128],
                                                                          rhs=csb[:, dc, :], start=(dc == 0), stop=False),
                             r=[b_wst[j], b_c], w=[b_pfm[j]])
                    K.op("pe", lambda e, j=j, cc=cc: e.matmul(pfm[j][:, cc, :], lhsT=bst[j][:, cc * 128:(cc + 1) * 128], rhs=ones2[:],
                                                              start=False, stop=True), r=[b_bst[j], b_o2], w=[b_pfm[j]])
                is_scale = cb in (2, 3, 8, 9)
                K.op("dve", lambda e, j=j, l=l, cb=cb, a=(1.0 if is_scale else 0.0): e.tensor_scalar_add(
                    out=modfm[:, l, cb * 4:(cb + 1) * 4, :], in0=pfm[j][:], scalar1=a), r=[b_pfm[j]], w=[b_mod])
                if cb in (4, 5, 10, 11):
                    for dc in range(8):
                        K.op("pe", lambda e, j=j, dc=dc: e.matmul(prow[j][:], lhsT=csb[:, dc, :], rhs=wst[j][:, dc, :], start=(dc == 0), stop=False),
                             r=[b_wst[j], b_c], w=[b_prow[j]])
                    K.op("pe", lambda e, j=j: e.matmul(prow[j][:], lhsT=ones2[:], rhs=bst[j][:], start=False, stop=True),
                         r=[b_bst[j], b_o2], w=[b_prow[j]])
                    K.op("act", lambda e, j=j: e.copy(out=rsb[j][:], in_=prow[j][:]), r=[b_prow[j]], w=[b_rsb[j]])
                    g = 0 if cb < 6 else 1
                    half = cb % 2
                    K.dma(ada_d[l, :, g, half * 512:(half + 1) * 512], rsb[j][:], b_rsb[j], B["ada"])
        lv = sb(st, "lv", [128, DEPTH, 2, 2, 32], F32); b_lv = Buf()
        K.dma(lv[:].rearrange("p l a c b -> p (l a c b)"), lamv.rearrange("l a c b -> (l a c b)").partition_broadcast(128), B["IN"], b_lv)
        lt = sb(st, "lt", [128, DEPTH, 2, 32], F32); b_lt = Buf()
        ls = sb(st, "ls", [128, DEPTH, 2], F32); b_ls = Buf()
        K.op("dve", lambda e: e.tensor_tensor(out=lt[:], in0=lv[:, :, 0, :, :], in1=lv[:, :, 1, :, :], op=ALU.mult), r=[b_lv], w=[b_lt])
        K.op("dve", lambda e: e.tensor_reduce(out=ls[:], in_=lt[:], axis=mybir.AxisListType.X, op=ALU.add), r=[b_lt], w=[b_ls])
        K.op("act", lambda e: e.activation(out=ls[:], in_=ls[:], func=AF.Exp), r=[b_ls], w=[b_ls])
        for l in range(DEPTH):
            lam_init = 0.8 - 0.6 * math.exp(-0.3 * l)
            K.op("dve", lambda e, l=l, li=lam_init: e.scalar_tensor_tensor(out=lam_sb[:, l, 0:1], in0=ls[:, l, 1:2], scalar=-li, in1=ls[:, l, 0:1],
                                                                            op0=ALU.add, op1=ALU.subtract), r=[b_ls], w=[b_lam])
        K.barrier()

    def phaseA(l):
        with ExitStack() as st:
            wa = sb(st, "wa", [128, 8, WA_COLS], BF16); b_wa = Buf()
            load_weight(st, "wa", w_a[l], 8, WA_COLS, wa, b_wa, colblk=1760)
            wq = sb(st, "wq", [128, 2, 384], BF16); b_wq = Buf()
            wqr = sb(st, "wqr", [128, 2, 384], BF16); b_wqr = Buf()
            wkn = sb(st, "wkn", [128, 1, 256], BF16); b_wkn = Buf()
            wkv = sb(st, "wkv", [128, 1, 256], BF16); b_wkv = Buf()
            load_weight(st, "wq", w_uq[l], 2, 384, wq, b_wq, colblk=384)
            load_weight(st, "wqr", w_uqr[l], 2, 384, wqr, b_wqr, colblk=384)
            load_weight(st, "wkn", w_ukn[l], 1, 256, wkn, b_wkn, colblk=256)
            load_weight(st, "wkv", w_ukv[l], 1, 256, wkv, b_wkv, colblk=256)
            gq = sb(st, "gq", [128, 384], F32); b_gq = Buf()
            K.dma(gq[:], qnw[l].partition_broadcast(128), B["IN"], b_gq)
            xt = [sb(st, f"xt{i}", [128, D], F32) for i in range(2)]; b_xt = [Buf(), Buf()]
            xn = [sb(st, f"xn{i}", [128, D], F32) for i in range(2)]; b_xn = [Buf(), Buf()]
            stt = sb(st, "stt", [128, 2, 6], F32); b_stt = Buf()
            mv = sb(st, "mv", [128, 2], F32); b_mv = Buf()
            rs = sb(st, "rs", [128, 1], F32); b_rs = Buf()
            nb = sb(st, "nb", [128, 1], F32); b_nb = Buf()
            hT = [sb(st, f"hT{i}", [128, 8, 512], BF16) for i in range(2)]; b_hT = [Buf(), Buf()]
            cs = [sb(st, f"cs{i}", [128, 512], F32) for i in range(2)]; b_cs = [Buf(), Buf()]
            sn = [sb(st, f"sn{i}", [128, 512], F32) for i in range(2)]; b_sn = [Buf(), Buf()]
            t1 = [sb(st, f"t1{i}", [128, 512], F32) for i in range(2)]; b_t1 = [Buf(), Buf()]
            t2 = [sb(st, f"t2{i}", [128, 512], F32) for i in range(2)]; b_t2 = [Buf(), Buf()]
            ob = [sb(st, f"ob{i}", [128, 512], BF16) for i in range(3)]; b_ob = [Buf() for _ in range(3)]
            vb = [sb(st, f"vb{i}", [128, 6, 65], BF16) for i in range(4)]; b_vb = [Buf() for _ in range(4)]
            cqs = sb(st, "cqs", [128, 384], F32); b_cqs = Buf()
            cqn = sb(st, "cqn", [128, 384], F32); b_cqn = Buf()
            junk = sb(st, "junk", [128, 384], F32); b_junk = Buf()
            ssq = sb(st, "ssq", [128, 2], F32); b_ssq = Buf()
            cT = [sb(st, f"cT{i}", [128, 3, 512], BF16) for i in range(2)]; b_cT = [Buf(), Buf()]
            ptr = [ps(st, f"ptr{i}", [128, 128]) for i in range(2)]; b_ptr = [Buf(), Buf()]
            pfm = [ps(st, f"pA{i}", [128, 512]) for i in range(4)]; b_pfm = [Buf() for _ in range(4)]
            for i in range(4):
                K.op("pool", lambda e, i=i: e.memset(vb[i][:], 1.0), w=[b_vb[i]])
            cnt = {"tr": 0, "pf": 0, "ob": 0, "vb": 0}

            def nxt(k, n):
                v = cnt[k] % n
                cnt[k] += 1
                return v

            groups = [(g * 4, 4) for g in range(NT // 4)]
            if NT % 4:
                groups.append((NT // 4 * 4, NT % 4))
            groups.append((NT, 2))
            for gi, (t0, ntl) in enumerate(groups):
                is_ctx = t0 >= NT
                ntok = ntl * 128
                tok0 = t0 * 128
                mi = 1 if is_ctx else 0
                hb = gi % 2
                if not is_ctx:
                    K.dma(cs[hb][:, :ntok], cosT_in[:, tok0:tok0 + ntok], B["IN"], b_cs[hb])
                    K.dma(sn[hb][:, :ntok], sinT_in[:, tok0:tok0 + ntok], B["IN"], b_sn[hb])
                for ti in range(ntl):
                    t = t0 + ti
                    xb = t % 2
                    src, sbuf_ = x_tile_ap(l, t)
                    K.dma(xt[xb][:], src, sbuf_, b_xt[xb])
                    K.op("dve", lambda e, xb=xb: e.bn_stats(out=stt[:, 0, :], in_=xt[xb][:, 0:512]), r=[b_xt[xb]], w=[b_stt])
                    K.op("dve", lambda e, xb=xb: e.bn_stats(out=stt[:, 1, :], in_=xt[xb][:, 512:1024]), r=[b_xt[xb]], w=[b_stt])
                    K.op("dve", lambda e: e.bn_aggr(out=mv[:], in_=stt[:].rearrange("p c s -> p (c s)")), r=[b_stt], w=[b_mv])
                    rstd_op(rs[:], mv[:, 1:2], 1.0, [b_mv], b_rs)
                    K.op("dve", lambda e: e.scalar_tensor_tensor(out=nb[:], in0=mv[:, 0:1], scalar=-1.0, in1=rs[:], op0=ALU.mult, op1=ALU.mult),
                         r=[b_mv, b_rs], w=[b_nb])
                    K.op("act", lambda e, xb=xb: e.activation(out=xn[xb][:], in_=xt[xb][:], func=AF.Identity, bias=nb[:], scale=rs[:]),
                         r=[b_xt[xb], b_nb, b_rs], w=[b_xn[xb]])
                    for dc in range(8):
                        p = nxt("tr", 2)
                        K.op("pe", lambda e, p=p, xb=xb, dc=dc: e.transpose(out=ptr[p][:], in_=xn[xb][:, dc * 128:(dc + 1) * 128], identity=ident[:]),
                             r=[b_xn[xb], b_ident], w=[b_ptr[p]])
                        K.op("dve", lambda e, p=p, dc=dc, ti=ti: e.tensor_scalar(out=hT[hb][:, dc, ti * 128:(ti + 1) * 128], in0=ptr[p][:],
                                                                                 scalar1=modfm[:, l, 8 + dc, mi:mi + 1], scalar2=modfm[:, l, dc, mi:mi + 1],
                                                                                 op0=ALU.mult, op1=ALU.add),
                             r=[b_ptr[p], b_mod], w=[b_hT[hb]])

                def fm_mm(col0, m, pbuf):
                    for dc in range(8):
                        K.op("pe", lambda e, dc=dc: e.matmul(pfm[pbuf][:m, :ntok], lhsT=wa[:, dc, col0:col0 + m], rhs=hT[hb][:, dc, :ntok],
                                                              start=(dc == 0), stop=(dc == 7)), r=[b_wa, b_hT[hb]], w=[b_pfm[pbuf]])

                def rope_out(col0, colr, m, dst_aps, dbuf):
                    pa = nxt("pf", 4)
                    fm_mm(col0, m, pa)
                    o = nxt("ob", 3)
                    if is_ctx or colr is None:
                        K.op("act", lambda e: e.copy(out=ob[o][:m, :ntok], in_=pfm[pa][:m, :ntok]), r=[b_pfm[pa]], w=[b_ob[o]])
                    else:
                        pb = nxt("pf", 4)
                        fm_mm(colr, m, pb)
                        K.op("dve", lambda e: e.tensor_tensor(out=t1[hb][:m, :ntok], in0=pfm[pa][:m, :ntok], in1=cs[hb][:m, :ntok], op=ALU.mult),
                             r=[b_pfm[pa], b_cs[hb]], w=[b_t1[hb]])
                        K.op("dve", lambda e: e.tensor_tensor(out=t2[hb][:m, :ntok], in0=pfm[pb][:m, :ntok], in1=sn[hb][:m, :ntok], op=ALU.mult),
                             r=[b_pfm[pb], b_sn[hb]], w=[b_t2[hb]])
                        K.op("pool", lambda e: e.tensor_tensor(out=ob[o][:m, :ntok], in0=t1[hb][:m, :ntok], in1=t2[hb][:m, :ntok], op=ALU.add),
                             r=[b_t1[hb], b_t2[hb]], w=[b_ob[o]])
                    for d_ in dst_aps:
                        K.dma(d_, ob[o][:m, :ntok], b_ob[o], dbuf)

                for c in range(4):
                    rope_out(C_AQ + c * 96, C_AQR + c * 96, 96, [qa_d[c, :, tok0:tok0 + ntok]], B["qa"])
                    rope_out(C_AK + c * 96, C_AKR + c * 96, 96, [ka_d[c, :, tok0:tok0 + ntok]], B["ka"])
                for c in range(3):
                    rope_out(C_NQ + c * 128, None, 128, [qn_d[c * 128:(c + 1) * 128, tok0:tok0 + ntok]], B["qn"])
                    rope_out(C_NK + c * 128, None, 128, [kn_d[c * 128:(c + 1) * 128, tok0:tok0 + ntok]], B["kn"])
                rope_out(C_KR, C_KRR, 32, [km_d[h, 64:96, tok0:tok0 + ntok] for h in range(4)], B["km"])

                for ti in range(ntl):
                    t = t0 + ti
                    for (col0, dst, dbuf) in ((C_AV, va_d, B["va"]), (C_NV, vn_d, B["vn"])):
                        pa = nxt("pf", 4)
                        for dc in range(8):
                            K.op("pe", lambda e, dc=dc, pa=pa, col0=col0: e.matmul(pfm[pa][:, :384], lhsT=hT[hb][:, dc, ti * 128:(ti + 1) * 128],
                                                                                   rhs=wa[:, dc, col0:col0 + 384], start=(dc == 0), stop=(dc == 7)),
                                 r=[b_wa, b_hT[hb]], w=[b_pfm[pa]])
                        v = nxt("vb", 4)
                        K.op("act", lambda e, pa=pa, v=v: e.copy(out=vb[v][:, :, 0:64], in_=pfm[pa][:, :384].rearrange("p (h d) -> p h d", d=64)),
                             r=[b_pfm[pa]], w=[b_vb[v]])
                        K.dma(dst[t * 128:(t + 1) * 128, :, :], vb[v][:], b_vb[v], dbuf)
                    pa = nxt("pf", 4)
                    for dc in range(8):
                        K.op("pe", lambda e, dc=dc, pa=pa: e.matmul(pfm[pa][:, :384], lhsT=hT[hb][:, dc, ti * 128:(ti + 1) * 128],
                                                                    rhs=wa[:, dc, C_CQ:C_CQ + 384], start=(dc == 0), stop=(dc == 7)),
                             r=[b_wa, b_hT[hb]], w=[b_pfm[pa]])
                    K.op("act", lambda e, pa=pa: e.copy(out=cqs[:], in_=pfm[pa][:, :384]), r=[b_pfm[pa]], w=[b_cqs])
                    K.op("dve", lambda e: e.scalar_tensor_tensor(out=junk[:, 0:256], in0=cqs[:, 0:256], scalar=1.0, in1=cqs[:, 0:256], op0=ALU.mult, op1=ALU.mult,
                                                                 accum_out=ssq[:, 0:1]), r=[b_cqs], w=[b_junk, b_ssq])
                    K.op("dve", lambda e: e.scalar_tensor_tensor(out=junk[:, 256:384], in0=cqs[:, 256:384], scalar=1.0, in1=cqs[:, 256:384], op0=ALU.mult, op1=ALU.mult,
                                                                 accum_out=ssq[:, 1:2]), r=[b_cqs], w=[b_junk, b_ssq])
                    rstd_op(ssq[:, 0:1], ssq[:, 0:1], 1.0 / 256, [b_ssq], b_ssq)
                    rstd_op(ssq[:, 1:2], ssq[:, 1:2], 1.0 / 128, [b_ssq], b_ssq)
                    K.op("dve", lambda e: e.scalar_tensor_tensor(out=cqn[:, 0:256], in0=cqs[:, 0:256], scalar=ssq[:, 0:1], in1=gq[:, 0:256], op0=ALU.mult, op1=ALU.mult),
                         r=[b_cqs, b_ssq, b_gq], w=[b_cqn])
                    K.op("dve", lambda e: e.scalar_tensor_tensor(out=cqn[:, 256:384], in0=cqs[:, 256:384], scalar=ssq[:, 1:2], in1=gq[:, 256:384], op0=ALU.mult, op1=ALU.mult),
                         r=[b_cqs, b_ssq, b_gq], w=[b_cqn])
                    for c in range(3):
                        p = nxt("tr", 2)
                        K.op("pe", lambda e, p=p, c=c: e.transpose(out=ptr[p][:], in_=cqn[:, c * 128:(c + 1) * 128], identity=ident[:]),
                             r=[b_cqn, b_ident], w=[b_ptr[p]])
                        K.op("act", lambda e, p=p, c=c: e.copy(out=cT[hb][:, c, ti * 128:(ti + 1) * 128], in_=ptr[p][:]), r=[b_ptr[p]], w=[b_cT[hb]])
                for h in range(4):
                    pa = nxt("pf", 4)
                    for rc in range(2):
                        K.op("pe", lambda e, rc=rc, pa=pa: e.matmul(pfm[pa][:96, :ntok], lhsT=wq[:, rc, h * 96:(h + 1) * 96], rhs=cT[hb][:, rc, :ntok],
                                                                    start=(rc == 0), stop=(rc == 1)), r=[b_wq, b_cT[hb]], w=[b_pfm[pa]])
                    o = nxt("ob", 3)
                    if is_ctx:
                        K.op("act", lambda e, pa=pa, o=o: e.copy(out=ob[o][:96, :ntok], in_=pfm[pa][:96, :ntok]), r=[b_pfm[pa]], w=[b_ob[o]])
                    else:
                        pb = nxt("pf", 4)
                        for rc in range(2):
                            K.op("pe", lambda e, rc=rc, pb=pb: e.matmul(pfm[pb][:96, :ntok], lhsT=wqr[:, rc, h * 96:(h + 1) * 96], rhs=cT[hb][:, rc, :ntok],
                                                                        start=(rc == 0), stop=(rc == 1)), r=[b_wqr, b_cT[hb]], w=[b_pfm[pb]])
                        K.op("act", lambda e, pa=pa, o=o: e.copy(out=ob[o][0:64, :ntok], in_=pfm[pa][0:64, :ntok]), r=[b_pfm[pa]], w=[b_ob[o]])
                        K.op("dve", lambda e, pa=pa: e.tensor_tensor(out=t1[hb][64:96, :ntok], in0=pfm[pa][64:96, :ntok], in1=cs[hb][64:96, :ntok], op=ALU.mult),
                             r=[b_pfm[pa], b_cs[hb]], w=[b_t1[hb]])
                        K.op("dve", lambda e, pb=pb: e.tensor_tensor(out=t2[hb][64:96, :ntok], in0=pfm[pb][64:96, :ntok], in1=sn[hb][64:96, :ntok], op=ALU.mult),
                             r=[b_pfm[pb], b_sn[hb]], w=[b_t2[hb]])
                        K.op("pool", lambda e, o=o: e.tensor_tensor(out=ob[o][64:96, :ntok], in0=t1[hb][64:96, :ntok], in1=t2[hb][64:96, :ntok], op=ALU.add),
                             r=[b_t1[hb], b_t2[hb]], w=[b_ob[o]])
                    K.dma(qm_d[h, :, tok0:tok0 + ntok], ob[o][:96, :ntok], b_ob[o], B["qm"])
                    pa = nxt("pf", 4)
                    K.op("pe", lambda e, pa=pa: e.matmul(pfm[pa][:64, :ntok], lhsT=wkn[:, 0, h * 64:(h + 1) * 64], rhs=cT[hb][:, 2, :ntok], start=True, stop=True),
                         r=[b_wkn, b_cT[hb]], w=[b_pfm[pa]])
                    o = nxt("ob", 3)
                    K.op("act", lambda e, pa=pa, o=o: e.copy(out=ob[o][:64, :ntok], in_=pfm[pa][:64, :ntok]), r=[b_pfm[pa]], w=[b_ob[o]])
                    K.dma(km_d[h, 0:64, tok0:tok0 + ntok], ob[o][:64, :ntok], b_ob[o], B["km"])
                for ti in range(ntl):
                    t = t0 + ti
                    pa = nxt("pf", 4)
                    K.op("pe", lambda e, pa=pa: e.matmul(pfm[pa][:, :256], lhsT=cT[hb][:, 2, ti * 128:(ti + 1) * 128], rhs=wkv[:, 0, :], start=True, stop=True),
                         r=[b_wkv, b_cT[hb]], w=[b_pfm[pa]])
                    v = nxt("vb", 4)
                    K.op("act", lambda e, pa=pa, v=v: e.copy(out=vb[v][:, 0:4, 0:64], in_=pfm[pa][:, :256].rearrange("p (h d) -> p h d", d=64)),
                         r=[b_pfm[pa]], w=[b_vb[v]])
                    K.dma(vm_d[t * 128:(t + 1) * 128, :, :], vb[v][:, 0:4, :], b_vb[v], B["vm"])
            K.barrier()


    def phaseB(l, do_ctx):
        lam_init = 0.8 - 0.6 * math.exp(-0.3 * l)
        for kind in DBG_KINDS:
            with ExitStack() as st:
                if kind == "a":
                    NH, q_d, k_d, v_d, bq, bk, bv, scale = 6, qa_d, ka_d, va_d, B["qa"], B["ka"], B["va"], DA_SCALE
                elif kind == "n":
                    NH, q_d, k_d, v_d, bq, bk, bv, scale = 6, qn_d, kn_d, vn_d, B["qn"], B["kn"], B["vn"], NA_SCALE
                else:
                    NH, q_d, k_d, v_d, bq, bk, bv, scale = 4, qm_d, km_d, vm_d, B["qm"], B["km"], B["vm"], MLA_SCALE
                NCH = 3 if kind == "n" else 4
                kT = sb(st, "kT", [128, NCH, NTOK], BF16); b_kT = Buf()
                vv = sb(st, "vv", [128, NKC, NH, 65], BF16); b_vv = Buf()
                if kind != "n":
                    for h in range(4):
                        K.dma(kT[:96, h, :], k_d[h], bk, b_kT)
                else:
                    for c in range(3):
                        K.dma(kT[:, c, :], k_d[c * 128:(c + 1) * 128, :], bk, b_kT)
                for c0 in range(0, NKC, 8):
                    c1 = min(NKC, c0 + 8)
                    K.dma(vv[:, c0:c1, :, :], v_d[c0 * 128:c1 * 128, :, :].rearrange("(c p) h d -> p c h d", p=128), bv, b_vv)
                NSLOT = {"a": 12, "n": 6, "m": 4}[kind]
                qT = [sb(st, f"qT{i}", [128, NSLOT, 512], BF16) for i in range(2)]; b_qT = [Buf(), Buf()]
                if kind != "m":
                    for i in range(2):
                        K.op("pool", lambda e, i=i: e.memset(qT[i][:], 0.0), w=[b_qT[i]])
                pT = [sb(st, f"pT{i}", [128, 1024], BF16) for i in range(3)]; b_pT = [[Buf(), Buf()] for _ in range(3)]
                osb = [[sb(st, f"osb{p_}{i}", [65, 512], F32) for i in range(2)] for p_ in range(2)]; b_osb = [[Buf(), Buf()], [Buf(), Buf()]]
                mixt = [sb(st, f"mixt{i}", [128, 4, 128], F32) for i in range(2)]; b_mixt = [Buf(), Buf()]
                pending = []
                att_no = [0]

                def defer(fn):
                    pending.append([1, fn])

                def tick():
                    for it in pending:
                        it[0] -= 1
                    while pending and pending[0][0] <= 0:
                        pending.pop(0)[1]()
                mixo = [sb(st, f"mixo{i}", [128, 512], BF16) for i in range(2)]; b_mixo = [Buf(), Buf()]
                rc_ = sb(st, "rc_", [128, 2], F32); b_rc = Buf()
                o1 = sb(st, "o1", [128, 64], F32); b_o1 = Buf()
                o2 = sb(st, "o2", [128, 64], F32); b_o2 = Buf()
                jk = sb(st, "jk", [128, 64], F32); b_jk = Buf()
                s2 = sb(st, "s2", [128, 1], F32); b_s2 = Buf()
                sc2 = [ps(st, f"sc{i}", [128, 1024]) for i in range(2)]
                sc = [[sc2[i][:, s_ * 512:(s_ + 1) * 512] for s_ in range(2)] for i in range(2)]; b_sc = [[Buf(), Buf()], [Buf(), Buf()]]
                acc = [ps(st, f"acc{i}", [65, 512]) for i in range(2)]; b_acc = [Buf(), Buf()]
                pmisc = ps(st, "pmisc", [128, 512])
                _bp = Buf()
                ptk = [pmisc[:, 0:65], pmisc[:, 128:193]]; b_ptk = [_bp, _bp]
                ptm_t = ps(st, "ptm", [128, 128]); ptm = ptm_t[:]; b_ptm = Buf()
                cnt = {"sc": 0, "pT": 0, "mixo": 0, "q": 0}

                def nxt(k, n):
                    v = cnt[k] % n
                    cnt[k] += 1
                    return v

                if kind == "a":
                    dn = sb(st, "dn", [128, 64], F32); b_dn = Buf()
                    K.dma(dn[:], dnw[l].partition_broadcast(128), B["IN"], b_dn)
                    K.op("dve", lambda e: e.tensor_scalar_mul(out=dn[:], in0=dn[:], scalar1=1.0 - lam_init), r=[b_dn], w=[b_dn])
                if kind == "n":
                    tab = sb(st, "tab", [128, 6, NTAB, 128], F32); b_tab = Buf()
                    for h in range(6):
                        K.dma(tab[:, h, :, :], natab[l, h].rearrange("t k q -> k t q"), B["IN"], b_tab)
                    sbias = [sb(st, f"sbias{i}", [128, 256], F32) for i in range(2)]; b_sbias = [Buf(), Buf()]

                def attend(qb, qoff, nq, streams, kcs, tmap=None, hpair=0):
                    nk = len(kcs)
                    pend = None
                    one_bank = 2 * nq <= 512
                    for i, kc in enumerate(kcs):
                        sbi = nxt("sc", 2)
                        for s, (ch, nr, slot, vh) in enumerate(streams):
                            dst = sc[sbi][0][:, s * nq:(s + 1) * nq] if one_bank else sc[sbi][s][:, :nq]
                            K.op("pe", lambda e, s=s, ch=ch, nr=nr, slot=slot, kc=kc, sbi=sbi: e.matmul(
                                dst, lhsT=kT[:nr, ch, kc * 128:(kc + 1) * 128],
                                rhs=qT[qb][:nr, slot, qoff:qoff + nq], start=True, stop=True), r=[b_kT, b_qT[qb]],
                                w=[b_sc[sbi][0 if one_bank else s]])
                        pi = nxt("pT", 3)
                        if tmap is not None and kc in tmap:
                            assert one_bank
                            ti_ = tmap[kc]
                            K.op("dve", lambda e, sbi=sbi, ti_=ti_: e.scalar_tensor_tensor(
                                out=sbias[sbi][:, :2 * nq].rearrange("p (s q) -> p s q", s=2), in0=sc[sbi][0][:, :2 * nq].rearrange("p (s q) -> p s q", s=2),
                                scalar=scale, in1=tab[:, 2 * hpair:2 * hpair + 2, ti_, :], op0=ALU.mult, op1=ALU.add),
                                r=[b_sc[sbi][0], b_tab], w=[b_sbias[sbi]])
                            K.op("act", lambda e, sbi=sbi, pi=pi: e.activation(out=pT[pi][:, :2 * nq], in_=sbias[sbi][:, :2 * nq], func=AF.Exp),
                                 r=[b_sbias[sbi]], w=b_pT[pi])
                        elif one_bank:
                            K.op("act", lambda e, sbi=sbi, pi=pi: e.activation(out=pT[pi][:, :2 * nq], in_=sc[sbi][0][:, :2 * nq], func=AF.Exp, scale=scale),
                                 r=[b_sc[sbi][0]], w=b_pT[pi])
                        elif nq == 512 and MERGE_EXP:
                            K.op("act", lambda e, sbi=sbi, pi=pi: e.activation(out=pT[pi][:, :1024], in_=sc2[sbi][:, :1024], func=AF.Exp, scale=scale),
                                 r=b_sc[sbi], w=b_pT[pi])
                        else:
                            for s in range(2):
                                K.op("act", lambda e, sbi=sbi, pi=pi, s=s: e.activation(out=pT[pi][:, s * nq:(s + 1) * nq], in_=sc[sbi][s][:, :nq], func=AF.Exp, scale=scale),
                                     r=[b_sc[sbi][s]], w=[b_pT[pi][s]])
                        if pend is not None:
                            pend()

                        def mk(i=i, kc=kc, pi=pi):
                            for s, (ch, nr, slot, vh) in enumerate(streams):
                                K.op("pe", lambda e, s=s, vh=vh: e.matmul(acc[s][:, :nq], lhsT=vv[:, kc, vh, :], rhs=pT[pi][:, s * nq:(s + 1) * nq],
                                                                          start=(i == 0), stop=(i == nk - 1)), r=[b_vv, b_pT[pi][s]], w=[b_acc[s]])
                        pend = mk
                    pend()
                    par = att_no[0] % 2
                    att_no[0] += 1
                    for s in range(2):
                        K.op("act", lambda e, s=s: e.copy(out=osb[par][s][:, :nq], in_=acc[s][:, :nq]), r=[b_acc[s]], w=[b_osb[par][s]])
                    tick()
                    return par

                def fin(par, nq, slot0, diff_j, mo):
                    for qi in range(nq // 128):
                        slot = slot0 + qi
                        for s in range(2):
                            K.op("pe", lambda e, s=s: e.transpose(out=ptk[s], in_=osb[par][s][:65, qi * 128:(qi + 1) * 128], identity=ident[:65, :65]),
                                 r=[b_osb[par][s], b_ident], w=[b_ptk[s]])
                        for s in range(2):
                            K.op("dve", lambda e, s=s: e.reciprocal(out=rc_[:, s:s + 1], in_=ptk[s][:, 64:65]), r=[b_ptk[s]], w=[b_rc])
                        if diff_j is None:
                            for s in range(2):
                                K.op("dve", lambda e, s=s: e.tensor_scalar(out=mixt[mo][:, slot, s * 64:(s + 1) * 64], in0=ptk[s][:, 0:64], scalar1=rc_[:, s:s + 1],
                                                                            scalar2=None, op0=ALU.mult), r=[b_ptk[s], b_rc], w=[b_mixt[mo]])
                        else:
                            j = diff_j
                            K.op("dve", lambda e: e.tensor_scalar(out=o1[:], in0=ptk[0][:, 0:64], scalar1=rc_[:, 0:1], scalar2=None, op0=ALU.mult),
                                 r=[b_ptk[0], b_rc], w=[b_o1])
                            K.op("dve", lambda e: e.tensor_scalar(out=o2[:], in0=ptk[1][:, 0:64], scalar1=rc_[:, 1:2], scalar2=None, op0=ALU.mult),
                                 r=[b_ptk[1], b_rc], w=[b_o2])
                            K.op("dve", lambda e: e.scalar_tensor_tensor(out=o1[:], in0=o2[:], scalar=lam_sb[:, l, 0:1], in1=o1[:], op0=ALU.mult, op1=ALU.add),
                                 r=[b_o2, b_lam, b_o1], w=[b_o1])
                            K.op("pool", lambda e: e.memset(s2[:], 0.0), w=[b_s2])
                            K.op("dve", lambda e: e.scalar_tensor_tensor(out=jk[:], in0=o1[:], scalar=1.0, in1=o1[:], op0=ALU.mult, op1=ALU.mult, accum_out=s2[:]),
                                 r=[b_o1], w=[b_jk, b_s2])
                            rstd_op(s2[:], s2[:], 1.0 / 64, [b_s2], b_s2)
                            K.op("dve", lambda e: e.scalar_tensor_tensor(out=mixt[mo][:, slot, j * 64:(j + 1) * 64], in0=o1[:], scalar=s2[:, 0:1], in1=dn[:],
                                                                         op0=ALU.mult, op1=ALU.mult), r=[b_o1, b_s2, b_dn], w=[b_mixt[mo]])

                def flush(slot, mo, col):
                    K.op("pe", lambda e: e.transpose(out=ptm, in_=mixt[mo][:, slot, :], identity=ident[:]), r=[b_mixt[mo], b_ident], w=[b_ptm])
                    K.op("act", lambda e: e.copy(out=mixo[mo][:, col:col + 128], in_=ptm), r=[b_ptm], w=[b_mixo[mo]])

                def unit_done(par, nq, slot0, diff_j, mo, flush_slots, dma_args):
                    def stage1():
                        fin(par, nq, slot0, diff_j, mo)
                        if flush_slots:
                            def stage2():
                                for (slot, col) in flush_slots:
                                    flush(slot, mo, col)
                                if dma_args is not None:
                                    cg_, q0_, nq_ = dma_args
                                    K.dma(mix_d[cg_, :, q0_:q0_ + nq_], mixo[mo][:, :nq_], b_mixo[mo], B["mix"])
                            defer(stage2)
                    defer(stage1)

                qblocks = [(g * 512, min(512, N - g * 512)) for g in range((N + 511) // 512)]
                if do_ctx:
                    qblocks.append((N, CTX))
                for (q0, nq) in qblocks:
                    is_ctx = q0 >= N
                    qb = nxt("q", 2)
                    if kind == "m":
                        for h in range(4):
                            K.dma(qT[qb][:96, h, :nq], q_d[h, :, q0:q0 + nq], bq, b_qT[qb])
                    elif kind == "a":
                        for g in range(12):
                            r0 = (g % 3) * 32
                            K.dma(qT[qb][r0:r0 + 32, g, :nq], q_d[g // 3, r0:r0 + 32, q0:q0 + nq], bq, b_qT[qb])
                    else:
                        for h in range(6):
                            r0 = (h % 2) * 64
                            K.dma(qT[qb][r0:r0 + 64, h, :nq], q_d[(h // 2) * 128 + r0:(h // 2) * 128 + r0 + 64, q0:q0 + nq], bq, b_qT[qb])
                    allk = list(range(NT, NT + 2)) if is_ctx else list(range(NKC))
                    for c in range(3 if kind != "m" else 2):
                        mo = nxt("mixo", 2)
                        cg = c + (0 if kind == "a" else 3 if kind == "n" else 6)
                        allslots = [(qi, qi * 128) for qi in range(nq // 128)]
                        if kind == "a":
                            for j in range(2):
                                hh_ = 2 * c + j
                                par = attend(qb, 0, nq, [((2 * hh_) // 3, 96, 2 * hh_, hh_), ((2 * hh_ + 1) // 3, 96, 2 * hh_ + 1, hh_)], allk)
                                unit_done(par, nq, 0, j, mo, allslots if j == 1 else None, (cg, q0, nq))
                        elif kind == "m":
                            par = attend(qb, 0, nq, [(2 * c, 96, 2 * c, 2 * c), (2 * c + 1, 96, 2 * c + 1, 2 * c + 1)], allk)
                            unit_done(par, nq, 0, None, mo, allslots, (cg, q0, nq))
                        else:
                            stn = [(c, 128, 2 * c, 2 * c), (c, 128, 2 * c + 1, 2 * c + 1)]
                            if is_ctx:
                                par = attend(qb, 0, nq, stn, allk)
                                unit_done(par, nq, 0, None, mo, allslots, (cg, q0, nq))
                            else:
                                for qi in range(nq // 128):
                                    qp = (q0 + qi * 128) // 128
                                    tmap = {kp: ti_ for (kp, ti_) in plan[qp]}
                                    kcs = [kp for (kp, ti_) in plan[qp]] + [NT, NT + 1]
                                    par = attend(qb, qi * 128, 128, stn, kcs, tmap, c)
                                    unit_done(par, 128, qi, None, mo, [(qi, qi * 128)], (cg, q0, nq) if qi == nq // 128 - 1 else None)
                while pending:
                    tick()
                K.barrier()

    def ln_norm(src, b_src, dst, b_dst, tmp):
        stt, mv, rs, nb, b_stt, b_mv, b_rs, b_nb = tmp
        K.op("dve", lambda e: e.bn_stats(out=stt[:, 0, :], in_=src[:, 0:512]), r=[b_src], w=[b_stt])
        K.op("dve", lambda e: e.bn_stats(out=stt[:, 1, :], in_=src[:, 512:1024]), r=[b_src], w=[b_stt])
        K.op("dve", lambda e: e.bn_aggr(out=mv[:], in_=stt[:].rearrange("p c s -> p (c s)")), r=[b_stt], w=[b_mv])
        rstd_op(rs[:], mv[:, 1:2], 1.0, [b_mv], b_rs)
        K.op("dve", lambda e: e.scalar_tensor_tensor(out=nb[:], in0=mv[:, 0:1], scalar=-1.0, in1=rs[:], op0=ALU.mult, op1=ALU.mult),
             r=[b_mv, b_rs], w=[b_nb])
        K.op("act", lambda e: e.activation(out=dst[:], in_=src[:], func=AF.Identity, bias=nb[:], scale=rs[:]), r=[b_src, b_nb, b_rs], w=[b_dst])

    def mk_tmp(st, pfx):
        return (sb(st, pfx + "stt", [128, 2, 6], F32), sb(st, pfx + "mv", [128, 2], F32), sb(st, pfx + "rs", [128, 1], F32), sb(st, pfx + "nb", [128, 1], F32),
                Buf(), Buf(), Buf(), Buf())

    def postnorm(py, b_py, xres, b_xres, gate, b_gate, g_bc, b_bc, b_gb, z, b_z, xn, b_xn, tmp):
        for hh in range(2):
            K.op("dve", lambda e, hh=hh: e.tensor_tensor(out=z[:, hh * 512:(hh + 1) * 512], in0=py[hh][:], in1=gate[:, hh * 512:(hh + 1) * 512], op=ALU.mult),
                 r=[b_py[hh], b_gate], w=[b_z])
        K.op("dve", lambda e: e.scalar_tensor_tensor(out=z[:], in0=xres[:], scalar=ALPHA, in1=z[:], op0=ALU.mult, op1=ALU.add), r=[b_xres, b_z], w=[b_z])
        ln_norm(z, b_z, xn, b_xn, tmp)
        K.op("dve", lambda e: e.tensor_tensor(out=xn[:], in0=xn[:], in1=g_bc[:], op=ALU.mult), r=[b_xn, b_gb[0]], w=[b_xn])
        K.op("pool", lambda e: e.tensor_tensor(out=z[:], in0=xn[:], in1=b_bc[:], op=ALU.add), r=[b_xn, b_gb[1]], w=[b_z])

    def bc_load(st, name, src_row, dbuf_src):
        t = sb(st, name, [128, D], F32)
        b = Buf()
        K.dma(t[:], src_row.partition_broadcast(128), dbuf_src, b)
        return t, b

    def tok_tiles(do_ctx):
        return list(range(NT)) + ([NT, NT + 1] if do_ctx else [])

    def phaseC1(l, do_ctx):
        with ExitStack() as st:
            wo = sb(st, "wo", [128, 8, D], BF16); b_wo = Buf()
            load_weight(st, "wo", w_out[l], 8, D, wo, b_wo, colblk=1024)
            gate = [bc_load(st, f"gate{m}", ada_d[l, m, 0, :], B["ada"]) for m in range(2)]
            g_bc, b_g = bc_load(st, "g_bc", lnp[l, 0, :], B["IN"])
            b_bc, b_b = bc_load(st, "b_bc", lnp[l, 1, :], B["IN"])
            b_gb = (b_g, b_b)
            mT = [sb(st, f"mT{i}", [128, 8, 128], BF16) for i in range(2)]; b_mT = [Buf(), Buf()]
            xr = [sb(st, f"xr{i}", [128, D], F32) for i in range(2)]; b_xr = [Buf(), Buf()]
            z = [sb(st, f"z{i}", [128, D], F32) for i in range(2)]; b_z = [Buf(), Buf()]
            xn = sb(st, "xn", [128, D], F32); b_xn = Buf()
            xn2 = sb(st, "xn2", [128, D], F32); b_xn2 = Buf()
            h2 = [sb(st, f"h2{i}", [128, 8, 128], BF16) for i in range(2)]; b_h2 = [Buf(), Buf()]
            tmp = mk_tmp(st, "c1")
            py = [ps(st, f"py{i}", [128, 512]) for i in range(2)]; b_py = [Buf(), Buf()]
            ptr = [ps(st, f"ptr{i}", [128, 128]) for i in range(2)]; b_ptr = [Buf(), Buf()]
            ntr = 0
            mi_of = lambda t: 1 if t >= NT else 0
            for t in tok_tiles(do_ctx):
                i = t % 2
                mi = mi_of(t)
                K.dma(mT[i][:], mix_d[:, :, t * 128:(t + 1) * 128].rearrange("c p t -> p c t"), B["mix"], b_mT[i])
                src, sbuf_ = x_tile_ap(l, t)
                K.dma(xr[i][:], src, sbuf_, b_xr[i])
                for hh in range(2):
                    for fc in range(8):
                        K.op("pe", lambda e, hh=hh, fc=fc: e.matmul(py[hh][:], lhsT=mT[i][:, fc, :], rhs=wo[:, fc, hh * 512:(hh + 1) * 512],
                                                                    start=(fc == 0), stop=(fc == 7)), r=[b_mT[i], b_wo], w=[b_py[hh]])
                postnorm(py, b_py, xr[i], b_xr[i], gate[mi][0], gate[mi][1], g_bc, b_bc, b_gb, z[i], b_z[i], xn, b_xn, tmp)
                K.dma(x1_d[t * 128:(t + 1) * 128, :], z[i][:], b_z[i], B["x1"])
                ln_norm(z[i], b_z[i], xn2, b_xn2, tmp)
                for dc in range(8):
                    p = ntr % 2
                    ntr += 1
                    K.op("pe", lambda e, p=p, dc=dc: e.transpose(out=ptr[p][:], in_=xn2[:, dc * 128:(dc + 1) * 128], identity=ident[:]),
                         r=[b_xn2, b_ident], w=[b_ptr[p]])
                    K.op("dve", lambda e, p=p, dc=dc: e.tensor_scalar(out=h2[i][:, dc, :], in0=ptr[p][:], scalar1=modfm[:, l, 32 + dc, mi:mi + 1],
                                                                      scalar2=modfm[:, l, 24 + dc, mi:mi + 1], op0=ALU.mult, op1=ALU.add),
                         r=[b_ptr[p], b_mod], w=[b_h2[i]])
                K.dma(h2_d[:, :, t * 128:(t + 1) * 128].rearrange("c p t -> p c t"), h2[i][:], b_h2[i], B["h2"])
            K.barrier()

    def phaseC2(l, do_ctx):
        with ExitStack() as st:
            wu = sb(st, "wu", [128, 8, 2 * DFF], BF16); b_wu = Buf()
            load_weight(st, "wu", w_up[l], 8, 2 * DFF, wu, b_wu, colblk=1408)
            cp = sb(st, "cp", [128, 44, 4], F32); b_cp = Buf()
            K.dma(cp[:], convp[l], B["IN"], b_cp)
            hg = [sb(st, f"hg{i}", [128, 8, 514], BF16) for i in range(2)]; b_hg = [Buf(), Buf()]
            u = [[sb(st, f"u{g}{i}", [128, 514], F32) for i in range(2)] for g in range(2)]; b_u = [[Buf(), Buf()], [Buf(), Buf()]]
            cv = [[sb(st, f"cv{g}{i}", [128, 512], F32) for i in range(2)] for g in range(2)]; b_cv = [[Buf(), Buf()], [Buf(), Buf()]]
            ao = [sb(st, f"ao{i}", [128, 512], BF16) for i in range(2)]; b_ao = [Buf(), Buf()]
            pum = [[ps(st, f"pum{g}{i}", [128, 512]) for i in range(2)] for g in range(2)]; b_pum = [[Buf(), Buf()], [Buf(), Buf()]]
            puh_t = ps(st, "puh", [128, 512])
            puh = [[puh_t[:, (g * 2 + i) * 8:(g * 2 + i) * 8 + 2] for i in range(2)] for g in range(2)]; _bh = Buf(); b_puh = [[_bh, _bh], [_bh, _bh]]
            seqs = [(0, N)] + ([(N, N + CTX)] if do_ctx else [])
            gi = 0
            it = 0
            for (s0, s1) in seqs:
                for tok0 in range(s0, s1, 512):
                    ntok = min(512, s1 - tok0)
                    hb = gi % 2
                    gi += 1
                    lo = tok0 - 1 if tok0 > s0 else tok0
                    hi = tok0 + ntok + 1 if tok0 + ntok < s1 else tok0 + ntok
                    if lo == tok0 or hi == tok0 + ntok:
                        K.op("pool", lambda e, hb=hb: e.memset(hg[hb][:], 0.0), w=[b_hg[hb]])
                    K.dma(hg[hb][:, :, lo - tok0 + 1:hi - tok0 + 1], h2_d[:, :, lo:hi].rearrange("c p t -> p c t"), B["h2"], b_hg[hb])
                    for j in range(22):
                        i = it % 2
                        it += 1
                        for g in range(2):
                            col0 = g * DFF + j * 128
                            for dc in range(8):
                                K.op("pe", lambda e, g=g, dc=dc, col0=col0: e.matmul(pum[g][i][:, :ntok], lhsT=wu[:, dc, col0:col0 + 128], rhs=hg[hb][:, dc, 0:ntok],
                                                                                     start=(dc == 0), stop=(dc == 7)), r=[b_wu, b_hg[hb]], w=[b_pum[g][i]])
                            for dc in range(8):
                                K.op("pe", lambda e, g=g, dc=dc, col0=col0: e.matmul(puh[g][i], lhsT=wu[:, dc, col0:col0 + 128], rhs=hg[hb][:, dc, ntok:ntok + 2],
                                                                                     start=(dc == 0), stop=(dc == 7)), r=[b_wu, b_hg[hb]], w=[b_puh[g][i]])
                            K.op("act", lambda e, g=g: e.copy(out=u[g][i][:, 0:ntok], in_=pum[g][i][:, :ntok]), r=[b_pum[g][i]], w=[b_u[g][i]])
                            K.op("act", lambda e, g=g: e.copy(out=u[g][i][:, ntok:ntok + 2], in_=puh[g][i]), r=[b_puh[g][i]], w=[b_u[g][i]])
                            ch = g * 22 + j
                            eng = "dve"
                            K.op(eng, lambda e, g=g, ch=ch: e.tensor_scalar(out=cv[g][i][:, :ntok], in0=u[g][i][:, 0:ntok], scalar1=cp[:, ch, 0:1], scalar2=cp[:, ch, 3:4],
                                                                             op0=ALU.mult, op1=ALU.add), r=[b_u[g][i], b_cp], w=[b_cv[g][i]])
                            for k in (1, 2):
                                K.op(eng, lambda e, g=g, ch=ch, k=k: e.scalar_tensor_tensor(out=cv[g][i][:, :ntok], in0=u[g][i][:, k:k + ntok], scalar=cp[:, ch, k:k + 1],
                                                                                            in1=cv[g][i][:, :ntok], op0=ALU.mult, op1=ALU.add),
                                     r=[b_u[g][i], b_cp, b_cv[g][i]], w=[b_cv[g][i]])
                        K.op("act", lambda e: e.activation(out=cv[0][i][:, :ntok], in_=cv[0][i][:, :ntok], func=AF.Silu), r=[b_cv[0][i]], w=[b_cv[0][i]])
                        K.op("dve", lambda e: e.tensor_tensor(out=ao[i][:, :ntok], in0=cv[0][i][:, :ntok], in1=cv[1][i][:, :ntok], op=ALU.mult),
                             r=[b_cv[0][i], b_cv[1][i]], w=[b_ao[i]])
                        K.dma(at_d[j, :, tok0:tok0 + ntok], ao[i][:, :ntok], b_ao[i], B["at"])
            K.barrier()

    def phaseC3(l, do_ctx):
        with ExitStack() as st:
            wd = sb(st, "wd", [128, 22, D], BF16); b_wd = Buf()
            load_weight(st, "wd", w_down[l], 22, D, wd, b_wd, colblk=1024)
            gate = [bc_load(st, f"gate{m}", ada_d[l, m, 1, :], B["ada"]) for m in range(2)]
            g_bc, b_g = bc_load(st, "g_bc", lnp[l, 2, :], B["IN"])
            b_bc, b_b = bc_load(st, "b_bc", lnp[l, 3, :], B["IN"])
            b_gb = (b_g, b_b)
            aT = [sb(st, f"aT{i}", [128, 22, 128], BF16) for i in range(2)]; b_aT = [Buf(), Buf()]
            xr = [sb(st, f"xr{i}", [128, D], F32) for i in range(2)]; b_xr = [Buf(), Buf()]
            z = [sb(st, f"z{i}", [128, D], F32) for i in range(2)]; b_z = [Buf(), Buf()]
            xn = sb(st, "xn", [128, D], F32); b_xn = Buf()
            tmp = mk_tmp(st, "c3")
            py = [ps(st, f"py{i}", [128, 512]) for i in range(2)]; b_py = [Buf(), Buf()]
            for t in tok_tiles(do_ctx):
                i = t % 2
                mi = 1 if t >= NT else 0
                K.dma(aT[i][:], at_d[:, :, t * 128:(t + 1) * 128].rearrange("c p t -> p c t"), B["at"], b_aT[i])
                K.dma(xr[i][:], x1_d[t * 128:(t + 1) * 128, :], B["x1"], b_xr[i])
                for hh in range(2):
                    for fc in range(22):
                        K.op("pe", lambda e, hh=hh, fc=fc: e.matmul(py[hh][:], lhsT=aT[i][:, fc, :], rhs=wd[:, fc, hh * 512:(hh + 1) * 512],
                                                                    start=(fc == 0), stop=(fc == 21)), r=[b_aT[i], b_wd], w=[b_py[hh]])
                postnorm(py, b_py, xr[i], b_xr[i], gate[mi][0], gate[mi][1], g_bc, b_bc, b_gb, z[i], b_z[i], xn, b_xn, tmp)
                dst, dbuf = x_out_ap(t)
                K.dma(dst, z[i][:], b_z[i], dbuf)
            K.barrier()

    for l in range(DEPTH):
        do_ctx = l < DEPTH - 1
        for nm, fn in (("A", lambda: phaseA(l)), ("B", lambda: phaseB(l, do_ctx)), ("C1", lambda: phaseC1(l, do_ctx)),
                       ("C2", lambda: phaseC2(l, do_ctx)), ("C3", lambda: phaseC3(l, do_ctx))):
            if stop_after is not None and stop_after == "0":
                break
            fn()
            if stop_after == nm:
                break
        if stop_after is not None:
            break
    K.barrier()
    es.close()
    return nc


_NC_CACHE = {}


def _host_inputs(N, DEPTH, x, c, ctx, c_ctx, w_ada, b_ada, w_in, lam_q1, lam_k1, lam_q2, lam_k2, diff_norm_w, na_rpb,
                 mla_q_norm_w, mla_kv_norm_w, w_uq, w_ukv, w_out, ln1_g, ln1_b, w_up, conv_w, conv_b, w_down, ln2_g, ln2_b):
    f = lambda a: np.ascontiguousarray(np.asarray(a, dtype=np.float32))
    L = DEPTH
    w_in = f(w_in)[:L]
    plan, tkeys = _na_plan(N)
    cosT, sinT = _rope_tables(N)
    w_ukv = f(w_ukv)[:L].reshape(L, 128, 4, 128)
    shared = {
        "w_ada": f(w_ada)[:L], "b_ada": f(b_ada)[:L],
        "w_a": np.ascontiguousarray(w_in[:, :, _win_cols()]),
        "lamv": np.ascontiguousarray(np.stack([np.stack([f(lam_q1)[:L], f(lam_q2)[:L]], axis=1), np.stack([f(lam_k1)[:L], f(lam_k2)[:L]], axis=1)], axis=1)),
        "dnw": f(diff_norm_w)[:L],
        "natab": _na_tables(f(na_rpb)[:L], tkeys),
        "qnw": np.ascontiguousarray(np.concatenate([f(mla_q_norm_w)[:L], f(mla_kv_norm_w)[:L]], axis=1)),
        "w_uq": f(w_uq)[:L], "w_uqr": np.ascontiguousarray(f(w_uq)[:L][:, :, _wuq_rot_cols()]),
        "w_ukn": np.ascontiguousarray(w_ukv[:, :, :, 0:64].reshape(L, 128, 256)),
        "w_ukv": np.ascontiguousarray(w_ukv[:, :, :, 64:128].reshape(L, 128, 256)),
        "w_out": f(w_out)[:L], "w_up": f(w_up)[:L], "w_down": f(w_down)[:L],
        "lnp": np.ascontiguousarray(np.stack([f(ln1_g)[:L], f(ln1_b)[:L], f(ln2_g)[:L], f(ln2_b)[:L]], axis=1)),
        "convp": np.ascontiguousarray(np.concatenate([f(conv_w)[:L], f(conv_b)[:L][:, None, :]], axis=1).reshape(L, 4, 44, 128).transpose(0, 3, 2, 1)),
        "ident": np.eye(128, dtype=np.float32), "cosT": cosT, "sinT": sinT,
    }
    x = f(x); ctx = f(ctx); c = f(c); c_ctx = f(c_ctx)
    maps = []
    for b in range(x.shape[0]):
        m = dict(shared)
        m["x"] = x[b]
        m["ctx"] = ctx[b]
        m["cc"] = np.ascontiguousarray(np.stack([c[b], c_ctx], axis=0).reshape(2, 8, 128).transpose(2, 1, 0))
        maps.append(m)
    return maps


def run(N, DEPTH, inputs, stop_after=None):
    key = (N, DEPTH, stop_after)
    if key not in _NC_CACHE:
        _NC_CACHE[key] = build(N, DEPTH, stop_after)
    nc = _NC_CACHE[key]
    maps = _host_inputs(N, DEPTH, **inputs)
    res = run_bass_kernel_spmd(nc, maps, core_ids=list(range(len(maps))))
    return np.stack([np.asarray(r["y"], dtype=np.float32) for r in res.results], axis=0)


def kernel(**inputs):
    N = inputs["x"].shape[1]
    return run(N, DEPTH_FULL, inputs)
```

```python
import math
from contextlib import ExitStack

import numpy as np
import concourse.bass as bass
import concourse.mybir as mybir
from concourse.bass_utils import run_bass_kernel_spmd

F32 = mybir.dt.float32
BF16 = mybir.dt.bfloat16
AF = mybir.ActivationFunctionType
ALU = mybir.AluOpType

D = 1024
CTX = 256
GW = 64
DFF = 2816
NEG = -30000.0
LN_EPS = 1e-6
DEPTH_FULL = 4
DBG_KINDS = ("a", "n", "m")
MERGE_EXP = True
ALPHA = (2 * DEPTH_FULL) ** 0.25
DA_SCALE = 32 ** -0.5
NA_SCALE = 64 ** -0.5
MLA_SCALE = 96 ** -0.5

O_AQ, O_AK, O_AV, O_NQ, O_NK, O_NV, O_CQ, O_CKV, O_KR = 0, 384, 768, 1152, 1536, 1920, 2304, 2560, 2688


def _rot_src(n):
    f = np.arange(n)
    j = f % 16
    return np.where(j < 8, f + 8, f - 8)


def _win_cols():
    aq = np.arange(384)
    cols = []
    cols.append(O_AQ + aq)
    cols.append(O_AQ + _rot_src(384))
    cols.append(O_AK + aq)
    cols.append(O_AK + _rot_src(384))
    cols.append(O_NQ + aq)
    cols.append(O_NK + aq)
    cols.append(O_KR + np.arange(32))
    cols.append(O_KR + _rot_src(32))
    cols.append(O_AV + aq)
    cols.append(O_NV + aq)
    cols.append(O_CQ + np.arange(384))
    return np.concatenate(cols)


WA_COLS = 3520
C_AQ, C_AQR, C_AK, C_AKR, C_NQ, C_NK, C_KR, C_KRR, C_AV, C_NV, C_CQ = 0, 384, 768, 1152, 1536, 1920, 2304, 2336, 2368, 2752, 3136


def _wuq_rot_cols():
    c = np.arange(384)
    h, f = c // 96, c % 96
    r = f - 64
    rr = np.where(r % 16 < 8, r + 8, r - 8)
    return np.where(f < 64, c, h * 96 + 64 + rr)


def _rope_tables(n):
    t = np.arange(n)
    row = (t // GW).astype(np.float32)
    col = (t % GW).astype(np.float32)
    inv = (10000.0 ** (-np.arange(0, 16, 2, dtype=np.float32) / 16)).astype(np.float32)
    cosT = np.zeros((128, n), np.float32)
    sinT = np.zeros((128, n), np.float32)
    for p in range(128):
        f = p % 32
        pos = row if f < 16 else col
        j = f % 16
        ang = (pos * inv[j % 8]).astype(np.float32)
        cosT[p] = np.cos(ang)
        sinT[p] = np.sin(ang) * (-1.0 if j < 8 else 1.0)
    return cosT, sinT


def _na_plan(n):
    rows = n // GW
    kh = min(8, rows)
    uniq = {}
    plan = []
    for qp in range(rows // 2):
        r0a = min(max(2 * qp - kh // 2, 0), rows - kh)
        r0b = min(max(2 * qp + 1 - kh // 2, 0), rows - kh)
        lo, hi = r0a // 2, (r0b + kh - 1) // 2
        ent = []
        for kp in range(lo, hi + 1):
            key = []
            for khalf in range(2):
                for qhalf in range(2):
                    kr, qr = 2 * kp + khalf, 2 * qp + qhalf
                    r0 = min(max(qr - kh // 2, 0), rows - kh)
                    key.append(kr - qr + 7 if (r0 <= kr < r0 + kh) else -1)
            key = tuple(key)
            if key not in uniq:
                uniq[key] = len(uniq)
            ent.append((kp, uniq[key]))
        plan.append(ent)
    return plan, list(uniq.keys())


def _na_tables(rpb, keys):
    L = rpb.shape[0]
    c = np.arange(GW)
    c0 = np.clip(c - 8, 0, GW - 16)
    band = (c[None, :] >= c0[:, None]) & (c[None, :] < c0[:, None] + 16)
    dc = np.clip(c[None, :] - c[:, None], -15, 15) + 15
    out = np.full((L, 6, len(keys), 128, 128), NEG, np.float32)
    for ti, key in enumerate(keys):
        for khalf in range(2):
            for qhalf in range(2):
                a = key[khalf * 2 + qhalf]
                if a < 0:
                    continue
                blk = rpb[:, :, a, :][:, :, dc.T]
                blk = np.where(band.T[None, None], blk, np.float32(NEG))
                out[:, :, ti, khalf * 64:(khalf + 1) * 64, qhalf * 64:(qhalf + 1) * 64] = blk
    return out


class Buf:
    __slots__ = ("w", "r", "dsem", "keep", "name")

    def __init__(self, name="", keep=False):
        self.w = None
        self.r = {}
        self.dsem = None
        self.keep = keep
        self.name = name


class Sem:
    _n = 0

    def __init__(self, h):
        self.h = h
        Sem._n += 1
        self.idx = Sem._n
        self.cnt = 0


class Sched:
    ROLL = 30000

    def __init__(self, nc, es):
        self.nc = nc
        self.es = es
        self.eng = {"pe": nc.tensor, "act": nc.scalar, "dve": nc.vector, "pool": nc.gpsimd, "sp": nc.sync}
        self.sem = {}
        self.cnt = {}
        self.waited = {e: {} for e in self.eng}
        self.nsem = 0
        for e in self.eng:
            self.sem[e] = self.newsem(e)
            self.cnt[e] = 0
        self.dsems = []
        self.dfree = []
        self.scope = []

    def release_scope(self):
        for b in self.scope:
            if not b.keep and b.dsem is not None:
                self.dfree.append(b.dsem)
                b.dsem = None
        self.scope = [b for b in self.scope if b.keep and False]

    def newsem(self, name):
        self.nsem += 1
        return Sem(self.es.enter_context(self.nc.semaphore(f"{name}_{self.nsem}")))

    def _deps(self, r, w):
        deps = []
        for b in r:
            if b.w is not None:
                deps.append(b.w)
        for b in w:
            if b.w is not None:
                deps.append(b.w)
            deps.extend(b.r.values())
        return deps

    def _wait(self, e, deps):
        eng = self.eng[e]
        wd = self.waited[e]
        for (se, sem, val) in deps:
            if se == "pe" and e == "pe":
                continue
            if wd.get(sem.idx, 0) >= val:
                continue
            eng.wait_ge(sem.h, val)
            wd[sem.idx] = val

    def op(self, e, fn, r=(), w=()):
        self._wait(e, self._deps(r, w))
        ins = fn(self.eng[e])
        if self.cnt[e] >= self.ROLL:
            self.sem[e] = self.newsem(e)
            self.cnt[e] = 0
        self.cnt[e] += 1
        ins.then_inc(self.sem[e].h, 1)
        tok = (e, self.sem[e], self.cnt[e])
        for b in r:
            b.r[(e, self.sem[e].idx)] = tok
        for b in w:
            b.w = tok
            b.r = {}
        return tok

    def dma(self, out, in_, rd, wr, q="sp"):
        rd_dram = rd.name.startswith("D:")
        wr_dram = wr.name.startswith("D:")
        assert rd_dram != wr_dram
        own = rd if wr_dram else wr
        deps = self._deps([] if rd_dram else [rd], [] if wr_dram else [wr])
        self._wait(q, deps)
        ins = self.eng[q].dma_start(out=out, in_=in_)
        if own.dsem is None:
            while self.dfree and self.dfree[-1].cnt > 40000:
                self.dfree.pop()
            if self.dfree:
                own.dsem = self.dfree.pop()
            else:
                own.dsem = self.newsem("d")
                self.dsems.append(own.dsem)
            self.scope.append(own)
        own.dsem.cnt += 16
        assert own.dsem.cnt < 65000, "DMA semaphore overflow"
        ins.then_inc(own.dsem.h, 16)
        tok = ("dma", own.dsem, own.dsem.cnt)
        if wr_dram:
            rd.r[("dma", own.dsem.idx)] = tok
        else:
            wr.w = tok
            wr.r = {}
        return tok

    def barrier(self):
        toks = [(e, self.sem[e], self.cnt[e]) for e in self.eng if self.cnt[e] > 0]
        toks += [("dma", d, d.cnt) for d in self.dsems if d.cnt > 0]
        for e in self.eng:
            self._wait(e, toks)
        self.release_scope()


class Ctx:
    pass


def build(N, DEPTH, stop_after=None):
    NT = N // 128
    NTOK = N + CTX
    NTT = NTOK // 128
    NKC = NTT
    nc = bass.Bass("TRN2", target_bir_lowering=False)
    es = ExitStack()
    K = Sched(nc, es)

    def din(name, shape, dt=F32):
        return nc.dram_tensor(name, list(shape), dt, kind="ExternalInput").ap()

    def dscr(name, shape, dt):
        return nc.dram_tensor(name, list(shape), dt, kind="Internal").ap()

    x_in = din("x", [N, D]); ctx_in = din("ctx", [CTX, D]); cc_in = din("cc", [128, 8, 2])
    w_ada = din("w_ada", [DEPTH, D, 6 * D]); b_ada = din("b_ada", [DEPTH, 6 * D])
    w_a = din("w_a", [DEPTH, D, WA_COLS])
    lamv = din("lamv", [DEPTH, 2, 2, 32]); dnw = din("dnw", [DEPTH, 64])
    plan, tkeys = _na_plan(N)
    NTAB = len(tkeys)
    natab = din("natab", [DEPTH, 6, NTAB, 128, 128])
    qnw = din("qnw", [DEPTH, 384])
    w_uq = din("w_uq", [DEPTH, 256, 384]); w_uqr = din("w_uqr", [DEPTH, 256, 384])
    w_ukn = din("w_ukn", [DEPTH, 128, 256]); w_ukv = din("w_ukv", [DEPTH, 128, 256])
    w_out = din("w_out", [DEPTH, D, D]); w_up = din("w_up", [DEPTH, D, 2 * DFF]); w_down = din("w_down", [DEPTH, DFF, D])
    lnp = din("lnp", [DEPTH, 4, D])
    convp = din("convp", [DEPTH, 128, 44, 4])
    ident_in = din("ident", [128, 128]); cosT_in = din("cosT", [128, N]); sinT_in = din("sinT", [128, N])
    y_out = nc.dram_tensor("y", [N, D], F32, kind="ExternalOutput").ap()

    xc_d = dscr("xc_d", [CTX, D], F32)
    x1_d = dscr("x1_d", [NTOK, D], F32)
    ada_d = dscr("ada_d", [DEPTH, 2, 2, D], F32)
    qa_d = dscr("qa_d", [4, 96, NTOK], BF16); ka_d = dscr("ka_d", [4, 96, NTOK], BF16)
    qn_d = dscr("qn_d", [384, NTOK], BF16); kn_d = dscr("kn_d", [384, NTOK], BF16)
    qm_d = dscr("qm_d", [4, 96, NTOK], BF16); km_d = dscr("km_d", [4, 96, NTOK], BF16)
    va_d = dscr("va_d", [NTOK, 6, 65], BF16); vn_d = dscr("vn_d", [NTOK, 6, 65], BF16); vm_d = dscr("vm_d", [NTOK, 4, 65], BF16)
    mix_d = dscr("mix_d", [8, 128, NTOK], BF16)
    h2_d = dscr("h2_d", [8, 128, NTOK], BF16)
    at_d = dscr("at_d", [22, 128, NTOK], BF16)
    B = {n: Buf("D:" + n, keep=True) for n in ["y", "xc", "x1", "ada", "qa", "ka", "qn", "kn", "qm", "km", "va", "vn", "vm", "mix", "h2", "at", "IN"]}

    uid = [0]

    def sb(st, name, shape, dt):
        uid[0] += 1
        return st.enter_context(nc.sbuf_tensor(f"s{uid[0]}_{name}", list(shape), dt))

    def ps(st, name, shape, dt=F32):
        uid[0] += 1
        return st.enter_context(nc.psum_tensor(f"p{uid[0]}_{name}", list(shape), dt))

    ident = sb(es, "ident", [128, 128], F32); b_ident = Buf("ident", True)
    epsb = sb(es, "epsb", [128, 1], F32); b_eps = Buf()
    modfm = sb(es, "modfm", [128, DEPTH, 48, 2], F32); b_mod = Buf()
    lam_sb = sb(es, "lam_sb", [128, DEPTH, 2], F32); b_lam = Buf()
    K.dma(ident[:], ident_in, B["IN"], b_ident)
    K.op("pool", lambda e: e.memset(epsb[:], LN_EPS), w=[b_eps])

    def x_tile_ap(l, t, final=False):
        if t < NT:
            src = x_in if l == 0 else y_out
            return src[t * 128:(t + 1) * 128, :], (B["IN"] if l == 0 else B["y"])
        src = ctx_in if l == 0 else xc_d
        return src[(t - NT) * 128:(t - NT + 1) * 128, :], (B["IN"] if l == 0 else B["xc"])

    def x_out_ap(t):
        if t < NT:
            return y_out[t * 128:(t + 1) * 128, :], B["y"]
        return xc_d[(t - NT) * 128:(t - NT + 1) * 128, :], B["xc"]

    def load_weight(st, name, src, nchunk, ncols, dst, dbuf, colblk=2048):
        stg = [sb(st, f"{name}_s{i}", [128, colblk], F32) for i in range(2)]
        sbf = [Buf() for _ in range(2)]
        i = 0
        for c in range(nchunk):
            for c0 in range(0, ncols, colblk):
                cw = min(colblk, ncols - c0)
                K.dma(stg[i % 2][:, :cw], src[c * 128:(c + 1) * 128, c0:c0 + cw], B["IN"], sbf[i % 2])
                s_ = stg[i % 2]
                K.op("pool", lambda e, s_=s_, c=c, c0=c0, cw=cw: e.tensor_copy(out=dst[:, c, c0:c0 + cw], in_=s_[:, :cw]),
                     r=[sbf[i % 2]], w=[dbuf])
                i += 1

    def rstd_op(out_ap, in_ap, scale, rbufs, wbuf):
        K.op("act", lambda e: e.activation(out=out_ap, in_=in_ap, func=AF.Ln, bias=epsb[:], scale=scale), r=rbufs + [b_eps], w=[wbuf])
        K.op("act", lambda e: e.activation(out=out_ap, in_=out_ap, func=AF.Exp, scale=-0.5), r=[wbuf], w=[wbuf])

    with ExitStack() as st:
        csb = sb(st, "csb", [128, 8, 2], F32); b_c = Buf()
        K.dma(csb[:], cc_in, B["IN"], b_c)
        K.op("act", lambda e: e.activation(out=csb[:], in_=csb[:], func=AF.Silu), r=[b_c], w=[b_c])
        ones2 = sb(st, "ones2", [1, 2], F32); b_o2 = Buf()
        K.op("pool", lambda e: e.memset(ones2[:], 1.0), w=[b_o2])
        wst = [sb(st, f"wst{i}", [128, 8, 512], F32) for i in range(2)]; b_wst = [Buf(), Buf()]
        bst = [sb(st, f"bst{i}", [1, 512], F32) for i in range(2)]; b_bst = [Buf(), Buf()]
        pfm = [ps(st, f"pfm{i}", [128, 4, 2]) for i in range(2)]; b_pfm = [Buf(), Buf()]
        prow = [ps(st, f"prow{i}", [2, 512]) for i in range(2)]; b_prow = [Buf(), Buf()]
        rsb = [sb(st, f"rsb{i}", [2, 512], F32) for i in range(2)]; b_rsb = [Buf(), Buf()]
        it = 0
        for l in range(DEPTH):
            for cb in range(12):
                j = it % 2
                it += 1
                K.dma(wst[j][:], w_ada[l, :, cb * 512:(cb + 1) * 512].rearrange("(c p) n -> p c n", p=128), B["IN"], b_wst[j])
                K.dma(bst[j][:], b_ada[l:l + 1, cb * 512:(cb + 1) * 512], B["IN"], b_bst[j])
                for cc in range(4):
                    for dc in range(8):
                        K.op("pe", lambda e, j=j, cc=cc, dc=dc: e.matmul(pfm[j][:, cc, :], lhsT=wst[j][:, dc, cc * 128:(cc + 1) * 128],
                                                                          rhs=csb[:, dc, :], start=(dc == 0), stop=False),
                             r=[b_wst[j], b_c], w=[b_pfm[j]])
                    K.op("pe", lambda e, j=j, cc=cc: e.matmul(pfm[j][:, cc, :], lhsT=bst[j][:, cc * 128:(cc + 1) * 128], rhs=ones2[:],
                                                              start=False, stop=True), r=[b_bst[j], b_o2], w=[b_pfm[j]])
                is_scale = cb in (2, 3, 8, 9)
                K.op("dve", lambda e, j=j, l=l, cb=cb, a=(1.0 if is_scale else 0.0): e.tensor_scalar_add(
                    out=modfm[:, l, cb * 4:(cb + 1) * 4, :], in0=pfm[j][:], scalar1=a), r=[b_pfm[j]], w=[b_mod])
                if cb in (4, 5, 10, 11):
                    for dc in range(8):
                        K.op("pe", lambda e, j=j, dc=dc: e.matmul(prow[j][:], lhsT=csb[:, dc, :], rhs=wst[j][:, dc, :], start=(dc == 0), stop=False),
                             r=[b_wst[j], b_c], w=[b_prow[j]])
                    K.op("pe", lambda e, j=j: e.matmul(prow[j][:], lhsT=ones2[:], rhs=bst[j][:], start=False, stop=True),
                         r=[b_bst[j], b_o2], w=[b_prow[j]])
                    K.op("act", lambda e, j=j: e.copy(out=rsb[j][:], in_=prow[j][:]), r=[b_prow[j]], w=[b_rsb[j]])
                    g = 0 if cb < 6 else 1
                    half = cb % 2
                    K.dma(ada_d[l, :, g, half * 512:(half + 1) * 512], rsb[j][:], b_rsb[j], B["ada"])
        lv = sb(st, "lv", [128, DEPTH, 2, 2, 32], F32); b_lv = Buf()
        K.dma(lv[:].rearrange("p l a c b -> p (l a c b)"), lamv.rearrange("l a c b -> (l a c b)").partition_broadcast(128), B["IN"], b_lv)
        lt = sb(st, "lt", [128, DEPTH, 2, 32], F32); b_lt = Buf()
        ls = sb(st, "ls", [128, DEPTH, 2], F32); b_ls = Buf()
        K.op("dve", lambda e: e.tensor_tensor(out=lt[:], in0=lv[:, :, 0, :, :], in1=lv[:, :, 1, :, :], op=ALU.mult), r=[b_lv], w=[b_lt])
        K.op("dve", lambda e: e.tensor_reduce(out=ls[:], in_=lt[:], axis=mybir.AxisListType.X, op=ALU.add), r=[b_lt], w=[b_ls])
        K.op("act", lambda e: e.activation(out=ls[:], in_=ls[:], func=AF.Exp), r=[b_ls], w=[b_ls])
        for l in range(DEPTH):
            lam_init = 0.8 - 0.6 * math.exp(-0.3 * l)
            K.op("dve", lambda e, l=l, li=lam_init: e.scalar_tensor_tensor(out=lam_sb[:, l, 0:1], in0=ls[:, l, 1:2], scalar=-li, in1=ls[:, l, 0:1],
                                                                            op0=ALU.add, op1=ALU.subtract), r=[b_ls], w=[b_lam])
        K.barrier()

    def phaseA(l):
        with ExitStack() as st:
            wa = sb(st, "wa", [128, 8, WA_COLS], BF16); b_wa = Buf()
            load_weight(st, "wa", w_a[l], 8, WA_COLS, wa, b_wa, colblk=1760)
            wq = sb(st, "wq", [128, 2, 384], BF16); b_wq = Buf()
            wqr = sb(st, "wqr", [128, 2, 384], BF16); b_wqr = Buf()
            wkn = sb(st, "wkn", [128, 1, 256], BF16); b_wkn = Buf()
            wkv = sb(st, "wkv", [128, 1, 256], BF16); b_wkv = Buf()
            load_weight(st, "wq", w_uq[l], 2, 384, wq, b_wq, colblk=384)
            load_weight(st, "wqr", w_uqr[l], 2, 384, wqr, b_wqr, colblk=384)
            load_weight(st, "wkn", w_ukn[l], 1, 256, wkn, b_wkn, colblk=256)
            load_weight(st, "wkv", w_ukv[l], 1, 256, wkv, b_wkv, colblk=256)
            gq = sb(st, "gq", [128, 384], F32); b_gq = Buf()
            K.dma(gq[:], qnw[l].partition_broadcast(128), B["IN"], b_gq)
            xt = [sb(st, f"xt{i}", [128, D], F32) for i in range(2)]; b_xt = [Buf(), Buf()]
            xn = [sb(st, f"xn{i}", [128, D], F32) for i in range(2)]; b_xn = [Buf(), Buf()]
            stt = sb(st, "stt", [128, 2, 6], F32); b_stt = Buf()
            mv = sb(st, "mv", [128, 2], F32); b_mv = Buf()
            rs = sb(st, "rs", [128, 1], F32); b_rs = Buf()
            nb = sb(st, "nb", [128, 1], F32); b_nb = Buf()
            hT = [sb(st, f"hT{i}", [128, 8, 512], BF16) for i in range(2)]; b_hT = [Buf(), Buf()]
            cs = [sb(st, f"cs{i}", [128, 512], F32) for i in range(2)]; b_cs = [Buf(), Buf()]
            sn = [sb(st, f"sn{i}", [128, 512], F32) for i in range(2)]; b_sn = [Buf(), Buf()]
            t1 = [sb(st, f"t1{i}", [128, 512], F32) for i in range(2)]; b_t1 = [Buf(), Buf()]
            t2 = [sb(st, f"t2{i}", [128, 512], F32) for i in range(2)]; b_t2 = [Buf(), Buf()]
            ob = [sb(st, f"ob{i}", [128, 512], BF16) for i in range(3)]; b_ob = [Buf() for _ in range(3)]
            vb = [sb(st, f"vb{i}", [128, 6, 65], BF16) for i in range(4)]; b_vb = [Buf() for _ in range(4)]
            cqs = sb(st, "cqs", [128, 384], F32); b_cqs = Buf()
            cqn = sb(st, "cqn", [128, 384], F32); b_cqn = Buf()
            junk = sb(st, "junk", [128, 384], F32); b_junk = Buf()
            ssq = sb(st, "ssq", [128, 2], F32); b_ssq = Buf()
            cT = [sb(st, f"cT{i}", [128, 3, 512], BF16) for i in range(2)]; b_cT = [Buf(), Buf()]
            ptr = [ps(st, f"ptr{i}", [128, 128]) for i in range(2)]; b_ptr = [Buf(), Buf()]
            pfm = [ps(st, f"pA{i}", [128, 512]) for i in range(4)]; b_pfm = [Buf() for _ in range(4)]
            for i in range(4):
                K.op("pool", lambda e, i=i: e.memset(vb[i][:], 1.0), w=[b_vb[i]])
            cnt = {"tr": 0, "pf": 0, "ob": 0, "vb": 0}

            def nxt(k, n):
                v = cnt[k] % n
                cnt[k] += 1
                return v

            def a_load(t):
                src, sbuf_ = x_tile_ap(l, t)
                K.dma(xt[t % 2][:], src, sbuf_, b_xt[t % 2])

            groups = [(g * 4, 4) for g in range(NT // 4)]
            if NT % 4:
                groups.append((NT // 4 * 4, NT % 4))
            groups.append((NT, 2))
            for gi, (t0, ntl) in enumerate(groups):
                is_ctx = t0 >= NT
                ntok = ntl * 128
                tok0 = t0 * 128
                mi = 1 if is_ctx else 0
                hb = gi % 2
                if not is_ctx:
                    K.dma(cs[hb][:, :ntok], cosT_in[:, tok0:tok0 + ntok], B["IN"], b_cs[hb])
                    K.dma(sn[hb][:, :ntok], sinT_in[:, tok0:tok0 + ntok], B["IN"], b_sn[hb])
                for ti in range(ntl):
                    t = t0 + ti
                    xb = t % 2
                    if t == 0:
                        a_load(0)
                    if t + 1 < NTT:
                        a_load(t + 1)
                    K.op("dve", lambda e, xb=xb: e.bn_stats(out=stt[:, 0, :], in_=xt[xb][:, 0:512]), r=[b_xt[xb]], w=[b_stt])
                    K.op("dve", lambda e, xb=xb: e.bn_stats(out=stt[:, 1, :], in_=xt[xb][:, 512:1024]), r=[b_xt[xb]], w=[b_stt])
                    K.op("dve", lambda e: e.bn_aggr(out=mv[:], in_=stt[:].rearrange("p c s -> p (c s)")), r=[b_stt], w=[b_mv])
                    rstd_op(rs[:], mv[:, 1:2], 1.0, [b_mv], b_rs)
                    K.op("dve", lambda e: e.scalar_tensor_tensor(out=nb[:], in0=mv[:, 0:1], scalar=-1.0, in1=rs[:], op0=ALU.mult, op1=ALU.mult),
                         r=[b_mv, b_rs], w=[b_nb])
                    K.op("act", lambda e, xb=xb: e.activation(out=xn[xb][:], in_=xt[xb][:], func=AF.Identity, bias=nb[:], scale=rs[:]),
                         r=[b_xt[xb], b_nb, b_rs], w=[b_xn[xb]])
                    for dc in range(8):
                        p = nxt("tr", 2)
                        K.op("pe", lambda e, p=p, xb=xb, dc=dc: e.transpose(out=ptr[p][:], in_=xn[xb][:, dc * 128:(dc + 1) * 128], identity=ident[:]),
                             r=[b_xn[xb], b_ident], w=[b_ptr[p]])
                        K.op("dve", lambda e, p=p, dc=dc, ti=ti: e.tensor_scalar(out=hT[hb][:, dc, ti * 128:(ti + 1) * 128], in0=ptr[p][:],
                                                                                 scalar1=modfm[:, l, 8 + dc, mi:mi + 1], scalar2=modfm[:, l, dc, mi:mi + 1],
                                                                                 op0=ALU.mult, op1=ALU.add),
                             r=[b_ptr[p], b_mod], w=[b_hT[hb]])

                def fm_mm(col0, m, pbuf):
                    for dc in range(8):
                        K.op("pe", lambda e, dc=dc: e.matmul(pfm[pbuf][:m, :ntok], lhsT=wa[:, dc, col0:col0 + m], rhs=hT[hb][:, dc, :ntok],
                                                              start=(dc == 0), stop=(dc == 7)), r=[b_wa, b_hT[hb]], w=[b_pfm[pbuf]])

                def rope_out(col0, colr, m, dst_aps, dbuf):
                    pa = nxt("pf", 4)
                    fm_mm(col0, m, pa)
                    o = nxt("ob", 3)
                    if is_ctx or colr is None:
                        K.op("act", lambda e: e.copy(out=ob[o][:m, :ntok], in_=pfm[pa][:m, :ntok]), r=[b_pfm[pa]], w=[b_ob[o]])
                    else:
                        pb = nxt("pf", 4)
                        fm_mm(colr, m, pb)
                        K.op("dve", lambda e: e.tensor_tensor(out=t1[hb][:m, :ntok], in0=pfm[pa][:m, :ntok], in1=cs[hb][:m, :ntok], op=ALU.mult),
                             r=[b_pfm[pa], b_cs[hb]], w=[b_t1[hb]])
                        K.op("dve", lambda e: e.tensor_tensor(out=t2[hb][:m, :ntok], in0=pfm[pb][:m, :ntok], in1=sn[hb][:m, :ntok], op=ALU.mult),
                             r=[b_pfm[pb], b_sn[hb]], w=[b_t2[hb]])
                        K.op("pool", lambda e: e.tensor_tensor(out=ob[o][:m, :ntok], in0=t1[hb][:m, :ntok], in1=t2[hb][:m, :ntok], op=ALU.add),
                             r=[b_t1[hb], b_t2[hb]], w=[b_ob[o]])
                    for d_ in dst_aps:
                        K.dma(d_, ob[o][:m, :ntok], b_ob[o], dbuf)

                for c in range(4):
                    rope_out(C_AQ + c * 96, C_AQR + c * 96, 96, [qa_d[c, :, tok0:tok0 + ntok]], B["qa"])
                    rope_out(C_AK + c * 96, C_AKR + c * 96, 96, [ka_d[c, :, tok0:tok0 + ntok]], B["ka"])
                for c in range(3):
                    rope_out(C_NQ + c * 128, None, 128, [qn_d[c * 128:(c + 1) * 128, tok0:tok0 + ntok]], B["qn"])
                    rope_out(C_NK + c * 128, None, 128, [kn_d[c * 128:(c + 1) * 128, tok0:tok0 + ntok]], B["kn"])
                rope_out(C_KR, C_KRR, 32, [km_d[h, 64:96, tok0:tok0 + ntok] for h in range(4)], B["km"])

                for ti in range(ntl):
                    t = t0 + ti
                    for (col0, dst, dbuf) in ((C_AV, va_d, B["va"]), (C_NV, vn_d, B["vn"])):
                        pa = nxt("pf", 4)
                        for dc in range(8):
                            K.op("pe", lambda e, dc=dc, pa=pa, col0=col0: e.matmul(pfm[pa][:, :384], lhsT=hT[hb][:, dc, ti * 128:(ti + 1) * 128],
                                                                                   rhs=wa[:, dc, col0:col0 + 384], start=(dc == 0), stop=(dc == 7)),
                                 r=[b_wa, b_hT[hb]], w=[b_pfm[pa]])
                        v = nxt("vb", 4)
                        K.op("act", lambda e, pa=pa, v=v: e.copy(out=vb[v][:, :, 0:64], in_=pfm[pa][:, :384].rearrange("p (h d) -> p h d", d=64)),
                             r=[b_pfm[pa]], w=[b_vb[v]])
                        K.dma(dst[t * 128:(t + 1) * 128, :, :], vb[v][:], b_vb[v], dbuf)
                    pa = nxt("pf", 4)
                    for dc in range(8):
                        K.op("pe", lambda e, dc=dc, pa=pa: e.matmul(pfm[pa][:, :384], lhsT=hT[hb][:, dc, ti * 128:(ti + 1) * 128],
                                                                    rhs=wa[:, dc, C_CQ:C_CQ + 384], start=(dc == 0), stop=(dc == 7)),
                             r=[b_wa, b_hT[hb]], w=[b_pfm[pa]])
                    K.op("act", lambda e, pa=pa: e.copy(out=cqs[:], in_=pfm[pa][:, :384]), r=[b_pfm[pa]], w=[b_cqs])
                    K.op("dve", lambda e: e.scalar_tensor_tensor(out=junk[:, 0:256], in0=cqs[:, 0:256], scalar=1.0, in1=cqs[:, 0:256], op0=ALU.mult, op1=ALU.mult,
                                                                 accum_out=ssq[:, 0:1]), r=[b_cqs], w=[b_junk, b_ssq])
                    K.op("dve", lambda e: e.scalar_tensor_tensor(out=junk[:, 256:384], in0=cqs[:, 256:384], scalar=1.0, in1=cqs[:, 256:384], op0=ALU.mult, op1=ALU.mult,
                                                                 accum_out=ssq[:, 1:2]), r=[b_cqs], w=[b_junk, b_ssq])
                    rstd_op(ssq[:, 0:1], ssq[:, 0:1], 1.0 / 256, [b_ssq], b_ssq)
                    rstd_op(ssq[:, 1:2], ssq[:, 1:2], 1.0 / 128, [b_ssq], b_ssq)
                    K.op("dve", lambda e: e.scalar_tensor_tensor(out=cqn[:, 0:256], in0=cqs[:, 0:256], scalar=ssq[:, 0:1], in1=gq[:, 0:256], op0=ALU.mult, op1=ALU.mult),
                         r=[b_cqs, b_ssq, b_gq], w=[b_cqn])
                    K.op("dve", lambda e: e.scalar_tensor_tensor(out=cqn[:, 256:384], in0=cqs[:, 256:384], scalar=ssq[:, 1:2], in1=gq[:, 256:384], op0=ALU.mult, op1=ALU.mult),
                         r=[b_cqs, b_ssq, b_gq], w=[b_cqn])
                    for c in range(3):
                        p = nxt("tr", 2)
                        K.op("pe", lambda e, p=p, c=c: e.transpose(out=ptr[p][:], in_=cqn[:, c * 128:(c + 1) * 128], identity=ident[:]),
                             r=[b_cqn, b_ident], w=[b_ptr[p]])
                        K.op("act", lambda e, p=p, c=c: e.copy(out=cT[hb][:, c, ti * 128:(ti + 1) * 128], in_=ptr[p][:]), r=[b_ptr[p]], w=[b_cT[hb]])
                for h in range(4):
                    pa = nxt("pf", 4)
                    for rc in range(2):
                        K.op("pe", lambda e, rc=rc, pa=pa: e.matmul(pfm[pa][:96, :ntok], lhsT=wq[:, rc, h * 96:(h + 1) * 96], rhs=cT[hb][:, rc, :ntok],
                                                                    start=(rc == 0), stop=(rc == 1)), r=[b_wq, b_cT[hb]], w=[b_pfm[pa]])
                    o = nxt("ob", 3)
                    if is_ctx:
                        K.op("act", lambda e, pa=pa, o=o: e.copy(out=ob[o][:96, :ntok], in_=pfm[pa][:96, :ntok]), r=[b_pfm[pa]], w=[b_ob[o]])
                    else:
                        pb = nxt("pf", 4)
                        for rc in range(2):
                            K.op("pe", lambda e, rc=rc, pb=pb: e.matmul(pfm[pb][:96, :ntok], lhsT=wqr[:, rc, h * 96:(h + 1) * 96], rhs=cT[hb][:, rc, :ntok],
                                                                        start=(rc == 0), stop=(rc == 1)), r=[b_wqr, b_cT[hb]], w=[b_pfm[pb]])
                        K.op("act", lambda e, pa=pa, o=o: e.copy(out=ob[o][0:64, :ntok], in_=pfm[pa][0:64, :ntok]), r=[b_pfm[pa]], w=[b_ob[o]])
                        K.op("dve", lambda e, pa=pa: e.tensor_tensor(out=t1[hb][64:96, :ntok], in0=pfm[pa][64:96, :ntok], in1=cs[hb][64:96, :ntok], op=ALU.mult),
                             r=[b_pfm[pa], b_cs[hb]], w=[b_t1[hb]])
                        K.op("dve", lambda e, pb=pb: e.tensor_tensor(out=t2[hb][64:96, :ntok], in0=pfm[pb][64:96, :ntok], in1=sn[hb][64:96, :ntok], op=ALU.mult),
                             r=[b_pfm[pb], b_sn[hb]], w=[b_t2[hb]])
                        K.op("pool", lambda e, o=o: e.tensor_tensor(out=ob[o][64:96, :ntok], in0=t1[hb][64:96, :ntok], in1=t2[hb][64:96, :ntok], op=ALU.add),
                             r=[b_t1[hb], b_t2[hb]], w=[b_ob[o]])
                    K.dma(qm_d[h, :, tok0:tok0 + ntok], ob[o][:96, :ntok], b_ob[o], B["qm"])
                    pa = nxt("pf", 4)
                    K.op("pe", lambda e, pa=pa: e.matmul(pfm[pa][:64, :ntok], lhsT=wkn[:, 0, h * 64:(h + 1) * 64], rhs=cT[hb][:, 2, :ntok], start=True, stop=True),
                         r=[b_wkn, b_cT[hb]], w=[b_pfm[pa]])
                    o = nxt("ob", 3)
                    K.op("act", lambda e, pa=pa, o=o: e.copy(out=ob[o][:64, :ntok], in_=pfm[pa][:64, :ntok]), r=[b_pfm[pa]], w=[b_ob[o]])
                    K.dma(km_d[h, 0:64, tok0:tok0 + ntok], ob[o][:64, :ntok], b_ob[o], B["km"])
                for ti in range(ntl):
                    t = t0 + ti
                    pa = nxt("pf", 4)
                    K.op("pe", lambda e, pa=pa: e.matmul(pfm[pa][:, :256], lhsT=cT[hb][:, 2, ti * 128:(ti + 1) * 128], rhs=wkv[:, 0, :], start=True, stop=True),
                         r=[b_wkv, b_cT[hb]], w=[b_pfm[pa]])
                    v = nxt("vb", 4)
                    K.op("act", lambda e, pa=pa, v=v: e.copy(out=vb[v][:, 0:4, 0:64], in_=pfm[pa][:, :256].rearrange("p (h d) -> p h d", d=64)),
                         r=[b_pfm[pa]], w=[b_vb[v]])
                    K.dma(vm_d[t * 128:(t + 1) * 128, :, :], vb[v][:, 0:4, :], b_vb[v], B["vm"])
            K.barrier()


    def phaseB(l, do_ctx):
        lam_init = 0.8 - 0.6 * math.exp(-0.3 * l)
        for kind in DBG_KINDS:
            with ExitStack() as st:
                if kind == "a":
                    NH, q_d, k_d, v_d, bq, bk, bv, scale = 6, qa_d, ka_d, va_d, B["qa"], B["ka"], B["va"], DA_SCALE
                elif kind == "n":
                    NH, q_d, k_d, v_d, bq, bk, bv, scale = 6, qn_d, kn_d, vn_d, B["qn"], B["kn"], B["vn"], NA_SCALE
                else:
                    NH, q_d, k_d, v_d, bq, bk, bv, scale = 4, qm_d, km_d, vm_d, B["qm"], B["km"], B["vm"], MLA_SCALE
                NCH = 3 if kind == "n" else 4
                kT = sb(st, "kT", [128, NCH, NTOK], BF16); b_kT = Buf()
                vv = sb(st, "vv", [128, NKC, NH, 65], BF16); b_vv = Buf()
                if kind != "n":
                    for h in range(4):
                        K.dma(kT[:96, h, :], k_d[h], bk, b_kT)
                else:
                    for c in range(3):
                        K.dma(kT[:, c, :], k_d[c * 128:(c + 1) * 128, :], bk, b_kT)
                for c0 in range(0, NKC, 8):
                    c1 = min(NKC, c0 + 8)
                    K.dma(vv[:, c0:c1, :, :], v_d[c0 * 128:c1 * 128, :, :].rearrange("(c p) h d -> p c h d", p=128), bv, b_vv)
                NSLOT = {"a": 12, "n": 6, "m": 4}[kind]
                qT = [sb(st, f"qT{i}", [128, NSLOT, 512], BF16) for i in range(2)]; b_qT = [Buf(), Buf()]
                if kind != "m":
                    for i in range(2):
                        K.op("pool", lambda e, i=i: e.memset(qT[i][:], 0.0), w=[b_qT[i]])
                pT = [sb(st, f"pT{i}", [128, 1024], BF16) for i in range(3)]; b_pT = [[Buf(), Buf()] for _ in range(3)]
                osb = [[sb(st, f"osb{p_}{i}", [65, 512], F32) for i in range(2)] for p_ in range(2)]; b_osb = [[Buf(), Buf()], [Buf(), Buf()]]
                mixt = [sb(st, f"mixt{i}", [128, 4, 128], F32) for i in range(2)]; b_mixt = [Buf(), Buf()]
                pending = []
                att_no = [0]

                def defer(fn):
                    pending.append([1, fn])

                def tick():
                    for it in pending:
                        it[0] -= 1
                    while pending and pending[0][0] <= 0:
                        pending.pop(0)[1]()
                mixo = [sb(st, f"mixo{i}", [128, 512], BF16) for i in range(2)]; b_mixo = [Buf(), Buf()]
                rc_ = sb(st, "rc_", [128, 2], F32); b_rc = Buf()
                o1 = sb(st, "o1", [128, 64], F32); b_o1 = Buf()
                o2 = sb(st, "o2", [128, 64], F32); b_o2 = Buf()
                jk = sb(st, "jk", [128, 64], F32); b_jk = Buf()
                s2 = sb(st, "s2", [128, 1], F32); b_s2 = Buf()
                sc2 = [ps(st, f"sc{i}", [128, 1024]) for i in range(2)]
                sc = [[sc2[i][:, s_ * 512:(s_ + 1) * 512] for s_ in range(2)] for i in range(2)]; b_sc = [[Buf(), Buf()], [Buf(), Buf()]]
                acc = [ps(st, f"acc{i}", [65, 512]) for i in range(2)]; b_acc = [Buf(), Buf()]
                pmisc = ps(st, "pmisc", [128, 512])
                _bp = Buf()
                ptk = [pmisc[:, 0:65], pmisc[:, 128:193]]; b_ptk = [_bp, _bp]
                ptm_t = ps(st, "ptm", [128, 128]); ptm = ptm_t[:]; b_ptm = Buf()
                cnt = {"sc": 0, "pT": 0, "mixo": 0, "q": 0}

                def nxt(k, n):
                    v = cnt[k] % n
                    cnt[k] += 1
                    return v

                if kind == "a":
                    dn = sb(st, "dn", [128, 64], F32); b_dn = Buf()
                    K.dma(dn[:], dnw[l].partition_broadcast(128), B["IN"], b_dn)
                    K.op("dve", lambda e: e.tensor_scalar_mul(out=dn[:], in0=dn[:], scalar1=1.0 - lam_init), r=[b_dn], w=[b_dn])
                if kind == "n":
                    tab = sb(st, "tab", [128, 6, NTAB, 128], F32); b_tab = Buf()
                    for h in range(6):
                        K.dma(tab[:, h, :, :], natab[l, h].rearrange("t k q -> k t q"), B["IN"], b_tab)
                    sbias = [sb(st, f"sbias{i}", [128, 256], F32) for i in range(2)]; b_sbias = [Buf(), Buf()]

                def attend(qb, qoff, nq, streams, kcs, tmap=None, hpair=0):
                    nk = len(kcs)
                    pend = None
                    one_bank = 2 * nq <= 512
                    for i, kc in enumerate(kcs):
                        sbi = nxt("sc", 2)
                        for s, (ch, nr, slot, vh) in enumerate(streams):
                            dst = sc[sbi][0][:, s * nq:(s + 1) * nq] if one_bank else sc[sbi][s][:, :nq]
                            K.op("pe", lambda e, s=s, ch=ch, nr=nr, slot=slot, kc=kc, sbi=sbi: e.matmul(
                                dst, lhsT=kT[:nr, ch, kc * 128:(kc + 1) * 128],
                                rhs=qT[qb][:nr, slot, qoff:qoff + nq], start=True, stop=True), r=[b_kT, b_qT[qb]],
                                w=[b_sc[sbi][0 if one_bank else s]])
                        pi = nxt("pT", 3)
                        if tmap is not None and kc in tmap:
                            assert one_bank
                            ti_ = tmap[kc]
                            K.op("dve", lambda e, sbi=sbi, ti_=ti_: e.scalar_tensor_tensor(
                                out=sbias[sbi][:, :2 * nq].rearrange("p (s q) -> p s q", s=2), in0=sc[sbi][0][:, :2 * nq].rearrange("p (s q) -> p s q", s=2),
                                scalar=scale, in1=tab[:, 2 * hpair:2 * hpair + 2, ti_, :], op0=ALU.mult, op1=ALU.add),
                                r=[b_sc[sbi][0], b_tab], w=[b_sbias[sbi]])
                            K.op("act", lambda e, sbi=sbi, pi=pi: e.activation(out=pT[pi][:, :2 * nq], in_=sbias[sbi][:, :2 * nq], func=AF.Exp),
                                 r=[b_sbias[sbi]], w=b_pT[pi])
                        elif one_bank:
                            K.op("act", lambda e, sbi=sbi, pi=pi: e.activation(out=pT[pi][:, :2 * nq], in_=sc[sbi][0][:, :2 * nq], func=AF.Exp, scale=scale),
                                 r=[b_sc[sbi][0]], w=b_pT[pi])
                        elif nq == 512 and MERGE_EXP:
                            K.op("act", lambda e, sbi=sbi, pi=pi: e.activation(out=pT[pi][:, :1024], in_=sc2[sbi][:, :1024], func=AF.Exp, scale=scale),
                                 r=b_sc[sbi], w=b_pT[pi])
                        else:
                            for s in range(2):
                                K.op("act", lambda e, sbi=sbi, pi=pi, s=s: e.activation(out=pT[pi][:, s * nq:(s + 1) * nq], in_=sc[sbi][s][:, :nq], func=AF.Exp, scale=scale),
                                     r=[b_sc[sbi][s]], w=[b_pT[pi][s]])
                        if pend is not None:
                            pend()

                        def mk(i=i, kc=kc, pi=pi):
                            for s, (ch, nr, slot, vh) in enumerate(streams):
                                K.op("pe", lambda e, s=s, vh=vh: e.matmul(acc[s][:, :nq], lhsT=vv[:, kc, vh, :], rhs=pT[pi][:, s * nq:(s + 1) * nq],
                                                                          start=(i == 0), stop=(i == nk - 1)), r=[b_vv, b_pT[pi][s]], w=[b_acc[s]])
                        pend = mk
                    pend()
                    par = att_no[0] % 2
                    att_no[0] += 1
                    for s in range(2):
                        K.op("act", lambda e, s=s: e.copy(out=osb[par][s][:, :nq], in_=acc[s][:, :nq]), r=[b_acc[s]], w=[b_osb[par][s]])
                    tick()
                    return par

                def fin(par, nq, slot0, diff_j, mo):
                    for qi in range(nq // 128):
                        slot = slot0 + qi
                        for s in range(2):
                            K.op("pe", lambda e, s=s: e.transpose(out=ptk[s], in_=osb[par][s][:65, qi * 128:(qi + 1) * 128], identity=ident[:65, :65]),
                                 r=[b_osb[par][s], b_ident], w=[b_ptk[s]])
                        for s in range(2):
                            K.op("dve", lambda e, s=s: e.reciprocal(out=rc_[:, s:s + 1], in_=ptk[s][:, 64:65]), r=[b_ptk[s]], w=[b_rc])
                        if diff_j is None:
                            for s in range(2):
                                K.op("dve", lambda e, s=s: e.tensor_scalar(out=mixt[mo][:, slot, s * 64:(s + 1) * 64], in0=ptk[s][:, 0:64], scalar1=rc_[:, s:s + 1],
                                                                            scalar2=None, op0=ALU.mult), r=[b_ptk[s], b_rc], w=[b_mixt[mo]])
                        else:
                            j = diff_j
                            K.op("dve", lambda e: e.tensor_scalar(out=o1[:], in0=ptk[0][:, 0:64], scalar1=rc_[:, 0:1], scalar2=None, op0=ALU.mult),
                                 r=[b_ptk[0], b_rc], w=[b_o1])
                            K.op("dve", lambda e: e.tensor_scalar(out=o2[:], in0=ptk[1][:, 0:64], scalar1=rc_[:, 1:2], scalar2=None, op0=ALU.mult),
                                 r=[b_ptk[1], b_rc], w=[b_o2])
                            K.op("dve", lambda e: e.scalar_tensor_tensor(out=o1[:], in0=o2[:], scalar=lam_sb[:, l, 0:1], in1=o1[:], op0=ALU.mult, op1=ALU.add),
                                 r=[b_o2, b_lam, b_o1], w=[b_o1])
                            K.op("pool", lambda e: e.memset(s2[:], 0.0), w=[b_s2])
                            K.op("dve", lambda e: e.scalar_tensor_tensor(out=jk[:], in0=o1[:], scalar=1.0, in1=o1[:], op0=ALU.mult, op1=ALU.mult, accum_out=s2[:]),
                                 r=[b_o1], w=[b_jk, b_s2])
                            rstd_op(s2[:], s2[:], 1.0 / 64, [b_s2], b_s2)
                            K.op("dve", lambda e: e.scalar_tensor_tensor(out=mixt[mo][:, slot, j * 64:(j + 1) * 64], in0=o1[:], scalar=s2[:, 0:1], in1=dn[:],
                                                                         op0=ALU.mult, op1=ALU.mult), r=[b_o1, b_s2, b_dn], w=[b_mixt[mo]])

                def flush(slot, mo, col):
                    K.op("pe", lambda e: e.transpose(out=ptm, in_=mixt[mo][:, slot, :], identity=ident[:]), r=[b_mixt[mo], b_ident], w=[b_ptm])
                    K.op("act", lambda e: e.copy(out=mixo[mo][:, col:col + 128], in_=ptm), r=[b_ptm], w=[b_mixo[mo]])

                def unit_done(par, nq, slot0, diff_j, mo, flush_slots, dma_args):
                    def stage1():
                        fin(par, nq, slot0, diff_j, mo)
                        if flush_slots:
                            def stage2():
                                for (slot, col) in flush_slots:
                                    flush(slot, mo, col)
                                if dma_args is not None:
                                    cg_, q0_, nq_ = dma_args
                                    K.dma(mix_d[cg_, :, q0_:q0_ + nq_], mixo[mo][:, :nq_], b_mixo[mo], B["mix"])
                            defer(stage2)
                    defer(stage1)

                qblocks = [(g * 512, min(512, N - g * 512)) for g in range((N + 511) // 512)]
                if do_ctx:
                    qblocks.append((N, CTX))
                def q_load(bi):
                    q0, nq = qblocks[bi]
                    qb = bi % 2
                    if kind == "m":
                        for h in range(4):
                            K.dma(qT[qb][:96, h, :nq], q_d[h, :, q0:q0 + nq], bq, b_qT[qb])
                    elif kind == "a":
                        for g in range(12):
                            r0 = (g % 3) * 32
                            K.dma(qT[qb][r0:r0 + 32, g, :nq], q_d[g // 3, r0:r0 + 32, q0:q0 + nq], bq, b_qT[qb])
                    else:
                        for h in range(6):
                            r0 = (h % 2) * 64
                            K.dma(qT[qb][r0:r0 + 64, h, :nq], q_d[(h // 2) * 128 + r0:(h // 2) * 128 + r0 + 64, q0:q0 + nq], bq, b_qT[qb])

                q_load(0)
                for bi, (q0, nq) in enumerate(qblocks):
                    is_ctx = q0 >= N
                    qb = bi % 2
                    if bi + 1 < len(qblocks):
                        q_load(bi + 1)
                    allk = list(range(NT, NT + 2)) if is_ctx else list(range(NKC))
                    for c in range(3 if kind != "m" else 2):
                        mo = nxt("mixo", 2)
                        cg = c + (0 if kind == "a" else 3 if kind == "n" else 6)
                        allslots = [(qi, qi * 128) for qi in range(nq // 128)]
                        if kind == "a":
                            for j in range(2):
                                hh_ = 2 * c + j
                                par = attend(qb, 0, nq, [((2 * hh_) // 3, 96, 2 * hh_, hh_), ((2 * hh_ + 1) // 3, 96, 2 * hh_ + 1, hh_)], allk)
                                unit_done(par, nq, 0, j, mo, allslots if j == 1 else None, (cg, q0, nq))
                        elif kind == "m":
                            par = attend(qb, 0, nq, [(2 * c, 96, 2 * c, 2 * c), (2 * c + 1, 96, 2 * c + 1, 2 * c + 1)], allk)
                            unit_done(par, nq, 0, None, mo, allslots, (cg, q0, nq))
                        else:
                            stn = [(c, 128, 2 * c, 2 * c), (c, 128, 2 * c + 1, 2 * c + 1)]
                            if is_ctx:
                                par = attend(qb, 0, nq, stn, allk)
                                unit_done(par, nq, 0, None, mo, allslots, (cg, q0, nq))
                            else:
                                for qi in range(nq // 128):
                                    qp = (q0 + qi * 128) // 128
                                    tmap = {kp: ti_ for (kp, ti_) in plan[qp]}
                                    kcs = [kp for (kp, ti_) in plan[qp]] + [NT, NT + 1]
                                    par = attend(qb, qi * 128, 128, stn, kcs, tmap, c)
                                    unit_done(par, 128, qi, None, mo, [(qi, qi * 128)], (cg, q0, nq) if qi == nq // 128 - 1 else None)
                while pending:
                    tick()
                K.barrier()

    def ln_norm(src, b_src, dst, b_dst, tmp):
        stt, mv, rs, nb, b_stt, b_mv, b_rs, b_nb = tmp
        K.op("dve", lambda e: e.bn_stats(out=stt[:, 0, :], in_=src[:, 0:512]), r=[b_src], w=[b_stt])
        K.op("dve", lambda e: e.bn_stats(out=stt[:, 1, :], in_=src[:, 512:1024]), r=[b_src], w=[b_stt])
        K.op("dve", lambda e: e.bn_aggr(out=mv[:], in_=stt[:].rearrange("p c s -> p (c s)")), r=[b_stt], w=[b_mv])
        rstd_op(rs[:], mv[:, 1:2], 1.0, [b_mv], b_rs)
        K.op("dve", lambda e: e.scalar_tensor_tensor(out=nb[:], in0=mv[:, 0:1], scalar=-1.0, in1=rs[:], op0=ALU.mult, op1=ALU.mult),
             r=[b_mv, b_rs], w=[b_nb])
        K.op("act", lambda e: e.activation(out=dst[:], in_=src[:], func=AF.Identity, bias=nb[:], scale=rs[:]), r=[b_src, b_nb, b_rs], w=[b_dst])

    def mk_tmp(st, pfx):
        return (sb(st, pfx + "stt", [128, 2, 6], F32), sb(st, pfx + "mv", [128, 2], F32), sb(st, pfx + "rs", [128, 1], F32), sb(st, pfx + "nb", [128, 1], F32),
                Buf(), Buf(), Buf(), Buf())

    def postnorm(py, b_py, xres, b_xres, gate, b_gate, g_bc, b_bc, b_gb, z, b_z, xn, b_xn, tmp):
        for hh in range(2):
            K.op("dve", lambda e, hh=hh: e.tensor_tensor(out=z[:, hh * 512:(hh + 1) * 512], in0=py[hh][:], in1=gate[:, hh * 512:(hh + 1) * 512], op=ALU.mult),
                 r=[b_py[hh], b_gate], w=[b_z])
        K.op("dve", lambda e: e.scalar_tensor_tensor(out=z[:], in0=xres[:], scalar=ALPHA, in1=z[:], op0=ALU.mult, op1=ALU.add), r=[b_xres, b_z], w=[b_z])
        ln_norm(z, b_z, xn, b_xn, tmp)
        K.op("dve", lambda e: e.tensor_tensor(out=xn[:], in0=xn[:], in1=g_bc[:], op=ALU.mult), r=[b_xn, b_gb[0]], w=[b_xn])
        K.op("pool", lambda e: e.tensor_tensor(out=z[:], in0=xn[:], in1=b_bc[:], op=ALU.add), r=[b_xn, b_gb[1]], w=[b_z])

    def bc_load(st, name, src_row, dbuf_src):
        t = sb(st, name, [128, D], F32)
        b = Buf()
        K.dma(t[:], src_row.partition_broadcast(128), dbuf_src, b)
        return t, b

    def tok_tiles(do_ctx):
        return list(range(NT)) + ([NT, NT + 1] if do_ctx else [])

    def phaseC1(l, do_ctx):
        with ExitStack() as st:
            wo = sb(st, "wo", [128, 8, D], BF16); b_wo = Buf()
            load_weight(st, "wo", w_out[l], 8, D, wo, b_wo, colblk=1024)
            gate = [bc_load(st, f"gate{m}", ada_d[l, m, 0, :], B["ada"]) for m in range(2)]
            g_bc, b_g = bc_load(st, "g_bc", lnp[l, 0, :], B["IN"])
            b_bc, b_b = bc_load(st, "b_bc", lnp[l, 1, :], B["IN"])
            b_gb = (b_g, b_b)
            mT = [sb(st, f"mT{i}", [128, 8, 128], BF16) for i in range(2)]; b_mT = [Buf(), Buf()]
            xr = [sb(st, f"xr{i}", [128, D], F32) for i in range(2)]; b_xr = [Buf(), Buf()]
            z = [sb(st, f"z{i}", [128, D], F32) for i in range(2)]; b_z = [Buf(), Buf()]
            xn = sb(st, "xn", [128, D], F32); b_xn = Buf()
            xn2 = sb(st, "xn2", [128, D], F32); b_xn2 = Buf()
            h2 = [sb(st, f"h2{i}", [128, 8, 128], BF16) for i in range(2)]; b_h2 = [Buf(), Buf()]
            tmp = mk_tmp(st, "c1")
            py = [ps(st, f"py{i}", [128, 512]) for i in range(2)]; b_py = [Buf(), Buf()]
            ptr = [ps(st, f"ptr{i}", [128, 128]) for i in range(2)]; b_ptr = [Buf(), Buf()]
            ntr = 0
            mi_of = lambda t: 1 if t >= NT else 0

            def c1_load(t):
                i = t % 2
                K.dma(mT[i][:], mix_d[:, :, t * 128:(t + 1) * 128].rearrange("c p t -> p c t"), B["mix"], b_mT[i])
                src, sbuf_ = x_tile_ap(l, t)
                K.dma(xr[i][:], src, sbuf_, b_xr[i])

            tl = tok_tiles(do_ctx)
            c1_load(tl[0])
            for ti_, t in enumerate(tl):
                i = t % 2
                mi = mi_of(t)
                if ti_ + 1 < len(tl):
                    c1_load(tl[ti_ + 1])
                for hh in range(2):
                    for fc in range(8):
                        K.op("pe", lambda e, hh=hh, fc=fc: e.matmul(py[hh][:], lhsT=mT[i][:, fc, :], rhs=wo[:, fc, hh * 512:(hh + 1) * 512],
                                                                    start=(fc == 0), stop=(fc == 7)), r=[b_mT[i], b_wo], w=[b_py[hh]])
                postnorm(py, b_py, xr[i], b_xr[i], gate[mi][0], gate[mi][1], g_bc, b_bc, b_gb, z[i], b_z[i], xn, b_xn, tmp)
                K.dma(x1_d[t * 128:(t + 1) * 128, :], z[i][:], b_z[i], B["x1"])
                ln_norm(z[i], b_z[i], xn2, b_xn2, tmp)
                for dc in range(8):
                    p = ntr % 2
                    ntr += 1
                    K.op("pe", lambda e, p=p, dc=dc: e.transpose(out=ptr[p][:], in_=xn2[:, dc * 128:(dc + 1) * 128], identity=ident[:]),
                         r=[b_xn2, b_ident], w=[b_ptr[p]])
                    K.op("dve", lambda e, p=p, dc=dc: e.tensor_scalar(out=h2[i][:, dc, :], in0=ptr[p][:], scalar1=modfm[:, l, 32 + dc, mi:mi + 1],
                                                                      scalar2=modfm[:, l, 24 + dc, mi:mi + 1], op0=ALU.mult, op1=ALU.add),
                         r=[b_ptr[p], b_mod], w=[b_h2[i]])
                K.dma(h2_d[:, :, t * 128:(t + 1) * 128].rearrange("c p t -> p c t"), h2[i][:], b_h2[i], B["h2"])
            K.barrier()

    def phaseC2(l, do_ctx):
        with ExitStack() as st:
            wu = sb(st, "wu", [128, 8, 2 * DFF], BF16); b_wu = Buf()
            load_weight(st, "wu", w_up[l], 8, 2 * DFF, wu, b_wu, colblk=1408)
            cp = sb(st, "cp", [128, 44, 4], F32); b_cp = Buf()
            K.dma(cp[:], convp[l], B["IN"], b_cp)
            hg = [sb(st, f"hg{i}", [128, 8, 514], BF16) for i in range(2)]; b_hg = [Buf(), Buf()]
            u = [[sb(st, f"u{g}{i}", [128, 514], F32) for i in range(2)] for g in range(2)]; b_u = [[Buf(), Buf()], [Buf(), Buf()]]
            cv = [[sb(st, f"cv{g}{i}", [128, 512], F32) for i in range(2)] for g in range(2)]; b_cv = [[Buf(), Buf()], [Buf(), Buf()]]
            ao = [sb(st, f"ao{i}", [128, 512], BF16) for i in range(2)]; b_ao = [Buf(), Buf()]
            pum = [[ps(st, f"pum{g}{i}", [128, 512]) for i in range(2)] for g in range(2)]; b_pum = [[Buf(), Buf()], [Buf(), Buf()]]
            puh_t = ps(st, "puh", [128, 512])
            puh = [[puh_t[:, (g * 2 + i) * 8:(g * 2 + i) * 8 + 2] for i in range(2)] for g in range(2)]; _bh = Buf(); b_puh = [[_bh, _bh], [_bh, _bh]]
            seqs = [(0, N)] + ([(N, N + CTX)] if do_ctx else [])
            gi = 0
            it = 0
            for (s0, s1) in seqs:
                for tok0 in range(s0, s1, 512):
                    ntok = min(512, s1 - tok0)
                    hb = gi % 2
                    gi += 1
                    lo = tok0 - 1 if tok0 > s0 else tok0
                    hi = tok0 + ntok + 1 if tok0 + ntok < s1 else tok0 + ntok
                    if lo == tok0 or hi == tok0 + ntok:
                        K.op("pool", lambda e, hb=hb: e.memset(hg[hb][:], 0.0), w=[b_hg[hb]])
                    K.dma(hg[hb][:, :, lo - tok0 + 1:hi - tok0 + 1], h2_d[:, :, lo:hi].rearrange("c p t -> p c t"), B["h2"], b_hg[hb])
                    for j in range(22):
                        i = it % 2
                        it += 1
                        for g in range(2):
                            col0 = g * DFF + j * 128
                            for dc in range(8):
                                K.op("pe", lambda e, g=g, dc=dc, col0=col0: e.matmul(pum[g][i][:, :ntok], lhsT=wu[:, dc, col0:col0 + 128], rhs=hg[hb][:, dc, 0:ntok],
                                                                                     start=(dc == 0), stop=(dc == 7)), r=[b_wu, b_hg[hb]], w=[b_pum[g][i]])
                            for dc in range(8):
                                K.op("pe", lambda e, g=g, dc=dc, col0=col0: e.matmul(puh[g][i], lhsT=wu[:, dc, col0:col0 + 128], rhs=hg[hb][:, dc, ntok:ntok + 2],
                                                                                     start=(dc == 0), stop=(dc == 7)), r=[b_wu, b_hg[hb]], w=[b_puh[g][i]])
                            K.op("act", lambda e, g=g: e.copy(out=u[g][i][:, 0:ntok], in_=pum[g][i][:, :ntok]), r=[b_pum[g][i]], w=[b_u[g][i]])
                            K.op("act", lambda e, g=g: e.copy(out=u[g][i][:, ntok:ntok + 2], in_=puh[g][i]), r=[b_puh[g][i]], w=[b_u[g][i]])
                            ch = g * 22 + j
                            eng = "dve"
                            K.op(eng, lambda e, g=g, ch=ch: e.tensor_scalar(out=cv[g][i][:, :ntok], in0=u[g][i][:, 0:ntok], scalar1=cp[:, ch, 0:1], scalar2=cp[:, ch, 3:4],
                                                                             op0=ALU.mult, op1=ALU.add), r=[b_u[g][i], b_cp], w=[b_cv[g][i]])
                            for k in (1, 2):
                                K.op(eng, lambda e, g=g, ch=ch, k=k: e.scalar_tensor_tensor(out=cv[g][i][:, :ntok], in0=u[g][i][:, k:k + ntok], scalar=cp[:, ch, k:k + 1],
                                                                                            in1=cv[g][i][:, :ntok], op0=ALU.mult, op1=ALU.add),
                                     r=[b_u[g][i], b_cp, b_cv[g][i]], w=[b_cv[g][i]])
                        K.op("act", lambda e: e.activation(out=cv[0][i][:, :ntok], in_=cv[0][i][:, :ntok], func=AF.Silu), r=[b_cv[0][i]], w=[b_cv[0][i]])
                        K.op("dve", lambda e: e.tensor_tensor(out=ao[i][:, :ntok], in0=cv[0][i][:, :ntok], in1=cv[1][i][:, :ntok], op=ALU.mult),
                             r=[b_cv[0][i], b_cv[1][i]], w=[b_ao[i]])
                        K.dma(at_d[j, :, tok0:tok0 + ntok], ao[i][:, :ntok], b_ao[i], B["at"])
            K.barrier()

    def phaseC3(l, do_ctx):
        with ExitStack() as st:
            wd = sb(st, "wd", [128, 22, D], BF16); b_wd = Buf()
            load_weight(st, "wd", w_down[l], 22, D, wd, b_wd, colblk=1024)
            gate = [bc_load(st, f"gate{m}", ada_d[l, m, 1, :], B["ada"]) for m in range(2)]
            g_bc, b_g = bc_load(st, "g_bc", lnp[l, 2, :], B["IN"])
            b_bc, b_b = bc_load(st, "b_bc", lnp[l, 3, :], B["IN"])
            b_gb = (b_g, b_b)
            aT = [sb(st, f"aT{i}", [128, 22, 128], BF16) for i in range(2)]; b_aT = [Buf(), Buf()]
            xr = [sb(st, f"xr{i}", [128, D], F32) for i in range(2)]; b_xr = [Buf(), Buf()]
            z = [sb(st, f"z{i}", [128, D], F32) for i in range(2)]; b_z = [Buf(), Buf()]
            xn = sb(st, "xn", [128, D], F32); b_xn = Buf()
            tmp = mk_tmp(st, "c3")
            py = [ps(st, f"py{i}", [128, 512]) for i in range(2)]; b_py = [Buf(), Buf()]
            def c3_load(t):
                i = t % 2
                K.dma(aT[i][:], at_d[:, :, t * 128:(t + 1) * 128].rearrange("c p t -> p c t"), B["at"], b_aT[i])
                K.dma(xr[i][:], x1_d[t * 128:(t + 1) * 128, :], B["x1"], b_xr[i])

            tl = tok_tiles(do_ctx)
            c3_load(tl[0])
            for ti_, t in enumerate(tl):
                i = t % 2
                mi = 1 if t >= NT else 0
                if ti_ + 1 < len(tl):
                    c3_load(tl[ti_ + 1])
                for hh in range(2):
                    for fc in range(22):
                        K.op("pe", lambda e, hh=hh, fc=fc: e.matmul(py[hh][:], lhsT=aT[i][:, fc, :], rhs=wd[:, fc, hh * 512:(hh + 1) * 512],
                                                                    start=(fc == 0), stop=(fc == 21)), r=[b_aT[i], b_wd], w=[b_py[hh]])
                postnorm(py, b_py, xr[i], b_xr[i], gate[mi][0], gate[mi][1], g_bc, b_bc, b_gb, z[i], b_z[i], xn, b_xn, tmp)
                dst, dbuf = x_out_ap(t)
                K.dma(dst, z[i][:], b_z[i], dbuf)
            K.barrier()

    for l in range(DEPTH):
        do_ctx = l < DEPTH - 1
        for nm, fn in (("A", lambda: phaseA(l)), ("B", lambda: phaseB(l, do_ctx)), ("C1", lambda: phaseC1(l, do_ctx)),
                       ("C2", lambda: phaseC2(l, do_ctx)), ("C3", lambda: phaseC3(l, do_ctx))):
            if stop_after is not None and stop_after == "0":
                break
            fn()
            if stop_after == nm:
                break
        if stop_after is not None:
            break
    K.barrier()
    es.close()
    return nc


_NC_CACHE = {}


def _host_inputs(N, DEPTH, x, c, ctx, c_ctx, w_ada, b_ada, w_in, lam_q1, lam_k1, lam_q2, lam_k2, diff_norm_w, na_rpb,
                 mla_q_norm_w, mla_kv_norm_w, w_uq, w_ukv, w_out, ln1_g, ln1_b, w_up, conv_w, conv_b, w_down, ln2_g, ln2_b):
    f = lambda a: np.ascontiguousarray(np.asarray(a, dtype=np.float32))
    L = DEPTH
    w_in = f(w_in)[:L]
    plan, tkeys = _na_plan(N)
    cosT, sinT = _rope_tables(N)
    w_ukv = f(w_ukv)[:L].reshape(L, 128, 4, 128)
    shared = {
        "w_ada": f(w_ada)[:L], "b_ada": f(b_ada)[:L],
        "w_a": np.ascontiguousarray(w_in[:, :, _win_cols()]),
        "lamv": np.ascontiguousarray(np.stack([np.stack([f(lam_q1)[:L], f(lam_q2)[:L]], axis=1), np.stack([f(lam_k1)[:L], f(lam_k2)[:L]], axis=1)], axis=1)),
        "dnw": f(diff_norm_w)[:L],
        "natab": _na_tables(f(na_rpb)[:L], tkeys),
        "qnw": np.ascontiguousarray(np.concatenate([f(mla_q_norm_w)[:L], f(mla_kv_norm_w)[:L]], axis=1)),
        "w_uq": f(w_uq)[:L], "w_uqr": np.ascontiguousarray(f(w_uq)[:L][:, :, _wuq_rot_cols()]),
        "w_ukn": np.ascontiguousarray(w_ukv[:, :, :, 0:64].reshape(L, 128, 256)),
        "w_ukv": np.ascontiguousarray(w_ukv[:, :, :, 64:128].reshape(L, 128, 256)),
        "w_out": f(w_out)[:L], "w_up": f(w_up)[:L], "w_down": f(w_down)[:L],
        "lnp": np.ascontiguousarray(np.stack([f(ln1_g)[:L], f(ln1_b)[:L], f(ln2_g)[:L], f(ln2_b)[:L]], axis=1)),
        "convp": np.ascontiguousarray(np.concatenate([f(conv_w)[:L], f(conv_b)[:L][:, None, :]], axis=1).reshape(L, 4, 44, 128).transpose(0, 3, 2, 1)),
        "ident": np.eye(128, dtype=np.float32), "cosT": cosT, "sinT": sinT,
    }
    x = f(x); ctx = f(ctx); c = f(c); c_ctx = f(c_ctx)
    maps = []
    for b in range(x.shape[0]):
        m = dict(shared)
        m["x"] = x[b]
        m["ctx"] = ctx[b]
        m["cc"] = np.ascontiguousarray(np.stack([c[b], c_ctx], axis=0).reshape(2, 8, 128).transpose(2, 1, 0))
        maps.append(m)
    return maps


def run(N, DEPTH, inputs, stop_after=None):
    key = (N, DEPTH, stop_after)
    if key not in _NC_CACHE:
        _NC_CACHE[key] = build(N, DEPTH, stop_after)
    nc = _NC_CACHE[key]
    maps = _host_inputs(N, DEPTH, **inputs)
    res = run_bass_kernel_spmd(nc, maps, core_ids=list(range(len(maps))))
    return np.stack([np.asarray(r["y"], dtype=np.float32) for r in res.results], axis=0)


def kernel(**inputs):
    N = inputs["x"].shape[1]
    return run(N, DEPTH_FULL, inputs)
```

```python
import math
from contextlib import ExitStack

import numpy as np
import concourse.bass as bass
import concourse.mybir as mybir
from concourse.bass_utils import run_bass_kernel_spmd

F32 = mybir.dt.float32
BF16 = mybir.dt.bfloat16
AF = mybir.ActivationFunctionType
ALU = mybir.AluOpType

D = 1024
CTX = 256
GW = 64
DFF = 2816
NEG = -30000.0
LN_EPS = 1e-6
DEPTH_FULL = 4
DBG_KINDS = ("a", "n", "m")
MERGE_EXP = True
ALPHA = (2 * DEPTH_FULL) ** 0.25
DA_SCALE = 32 ** -0.5
NA_SCALE = 64 ** -0.5
MLA_SCALE = 96 ** -0.5

O_AQ, O_AK, O_AV, O_NQ, O_NK, O_NV, O_CQ, O_CKV, O_KR = 0, 384, 768, 1152, 1536, 1920, 2304, 2560, 2688


def _rot_src(n):
    f = np.arange(n)
    j = f % 16
    return np.where(j < 8, f + 8, f - 8)


def _win_cols():
    aq = np.arange(384)
    cols = []
    cols.append(O_AQ + aq)
    cols.append(O_AQ + _rot_src(384))
    cols.append(O_AK + aq)
    cols.append(O_AK + _rot_src(384))
    cols.append(O_NQ + aq)
    cols.append(O_NK + aq)
    cols.append(O_KR + np.arange(32))
    cols.append(O_KR + _rot_src(32))
    cols.append(O_AV + aq)
    cols.append(O_NV + aq)
    cols.append(O_CQ + np.arange(384))
    return np.concatenate(cols)


WA_COLS = 3520
C_AQ, C_AQR, C_AK, C_AKR, C_NQ, C_NK, C_KR, C_KRR, C_AV, C_NV, C_CQ = 0, 384, 768, 1152, 1536, 1920, 2304, 2336, 2368, 2752, 3136


def _wuq_rot_cols():
    c = np.arange(384)
    h, f = c // 96, c % 96
    r = f - 64
    rr = np.where(r % 16 < 8, r + 8, r - 8)
    return np.where(f < 64, c, h * 96 + 64 + rr)


def _rope_tables(n):
    t = np.arange(n)
    row = (t // GW).astype(np.float32)
    col = (t % GW).astype(np.float32)
    inv = (10000.0 ** (-np.arange(0, 16, 2, dtype=np.float32) / 16)).astype(np.float32)
    cosT = np.zeros((128, n), np.float32)
    sinT = np.zeros((128, n), np.float32)
    for p in range(128):
        f = p % 32
        pos = row if f < 16 else col
        j = f % 16
        ang = (pos * inv[j % 8]).astype(np.float32)
        cosT[p] = np.cos(ang)
        sinT[p] = np.sin(ang) * (-1.0 if j < 8 else 1.0)
    return cosT, sinT


def _na_plan(n):
    rows = n // GW
    kh = min(8, rows)
    uniq = {}
    plan = []
    for qp in range(rows // 2):
        r0a = min(max(2 * qp - kh // 2, 0), rows - kh)
        r0b = min(max(2 * qp + 1 - kh // 2, 0), rows - kh)
        lo, hi = r0a // 2, (r0b + kh - 1) // 2
        ent = []
        for kp in range(lo, hi + 1):
            key = []
            for khalf in range(2):
                for qhalf in range(2):
                    kr, qr = 2 * kp + khalf, 2 * qp + qhalf
                    r0 = min(max(qr - kh // 2, 0), rows - kh)
                    key.append(kr - qr + 7 if (r0 <= kr < r0 + kh) else -1)
            key = tuple(key)
            if key not in uniq:
                uniq[key] = len(uniq)
            ent.append((kp, uniq[key]))
        plan.append(ent)
    return plan, list(uniq.keys())


def _na_tables(rpb, keys):
    L = rpb.shape[0]
    c = np.arange(GW)
    c0 = np.clip(c - 8, 0, GW - 16)
    band = (c[None, :] >= c0[:, None]) & (c[None, :] < c0[:, None] + 16)
    dc = np.clip(c[None, :] - c[:, None], -15, 15) + 15
    out = np.full((L, 6, len(keys), 128, 128), NEG, np.float32)
    for ti, key in enumerate(keys):
        for khalf in range(2):
            for qhalf in range(2):
                a = key[khalf * 2 + qhalf]
                if a < 0:
                    continue
                blk = rpb[:, :, a, :][:, :, dc.T]
                blk = np.where(band.T[None, None], blk, np.float32(NEG))
                out[:, :, ti, khalf * 64:(khalf + 1) * 64, qhalf * 64:(qhalf + 1) * 64] = blk
    return out


class Buf:
    __slots__ = ("w", "r", "dsem", "keep", "name")

    def __init__(self, name="", keep=False):
        self.w = None
        self.r = {}
        self.dsem = None
        self.keep = keep
        self.name = name


class Sem:
    _n = 0

    def __init__(self, h):
        self.h = h
        Sem._n += 1
        self.idx = Sem._n
        self.cnt = 0


class Sched:
    ROLL = 30000

    def __init__(self, nc, es):
        self.nc = nc
        self.es = es
        self.eng = {"pe": nc.tensor, "act": nc.scalar, "dve": nc.vector, "pool": nc.gpsimd, "sp": nc.sync}
        self.sem = {}
        self.cnt = {}
        self.waited = {e: {} for e in self.eng}
        self.nsem = 0
        for e in self.eng:
            self.sem[e] = self.newsem(e)
            self.cnt[e] = 0
        self.dsems = []
        self.dfree = []
        self.scope = []

    def release_scope(self):
        for b in self.scope:
            if not b.keep and b.dsem is not None:
                self.dfree.append(b.dsem)
                b.dsem = None
        self.scope = [b for b in self.scope if b.keep and False]

    def newsem(self, name):
        self.nsem += 1
        return Sem(self.es.enter_context(self.nc.semaphore(f"{name}_{self.nsem}")))

    def _deps(self, r, w):
        deps = []
        for b in r:
            if b.w is not None:
                deps.append(b.w)
        for b in w:
            if b.w is not None:
                deps.append(b.w)
            deps.extend(b.r.values())
        return deps

    def _wait(self, e, deps):
        eng = self.eng[e]
        wd = self.waited[e]
        for (se, sem, val) in deps:
            if se == "pe" and e == "pe":
                continue
            if wd.get(sem.idx, 0) >= val:
                continue
            eng.wait_ge(sem.h, val)
            wd[sem.idx] = val

    def op(self, e, fn, r=(), w=()):
        self._wait(e, self._deps(r, w))
        ins = fn(self.eng[e])
        if self.cnt[e] >= self.ROLL:
            self.sem[e] = self.newsem(e)
            self.cnt[e] = 0
        self.cnt[e] += 1
        ins.then_inc(self.sem[e].h, 1)
        tok = (e, self.sem[e], self.cnt[e])
        for b in r:
            b.r[(e, self.sem[e].idx)] = tok
        for b in w:
            b.w = tok
            b.r = {}
        return tok

    def dma(self, out, in_, rd, wr, q="sp"):
        rd_dram = rd.name.startswith("D:")
        wr_dram = wr.name.startswith("D:")
        assert rd_dram != wr_dram
        own = rd if wr_dram else wr
        deps = self._deps([] if rd_dram else [rd], [] if wr_dram else [wr])
        self._wait(q, deps)
        ins = self.eng[q].dma_start(out=out, in_=in_)
        if own.dsem is None:
            while self.dfree and self.dfree[-1].cnt > 40000:
                self.dfree.pop()
            if self.dfree:
                own.dsem = self.dfree.pop()
            else:
                own.dsem = self.newsem("d")
                self.dsems.append(own.dsem)
            self.scope.append(own)
        own.dsem.cnt += 16
        assert own.dsem.cnt < 65000, "DMA semaphore overflow"
        ins.then_inc(own.dsem.h, 16)
        tok = ("dma", own.dsem, own.dsem.cnt)
        if wr_dram:
            rd.r[("dma", own.dsem.idx)] = tok
        else:
            wr.w = tok
            wr.r = {}
        return tok

    def barrier(self):
        toks = [(e, self.sem[e], self.cnt[e]) for e in self.eng if self.cnt[e] > 0]
        toks += [("dma", d, d.cnt) for d in self.dsems if d.cnt > 0]
        for e in self.eng:
            self._wait(e, toks)
        self.release_scope()


class Ctx:
    pass


def build(N, DEPTH, stop_after=None):
    NT = N // 128
    NTOK = N + CTX
    NTT = NTOK // 128
    NKC = NTT
    nc = bass.Bass("TRN2", target_bir_lowering=False)
    es = ExitStack()
    K = Sched(nc, es)

    def din(name, shape, dt=F32):
        return nc.dram_tensor(name, list(shape), dt, kind="ExternalInput").ap()

    def dscr(name, shape, dt):
        return nc.dram_tensor(name, list(shape), dt, kind="Internal").ap()

    x_in = din("x", [N, D]); ctx_in = din("ctx", [CTX, D]); cc_in = din("cc", [128, 8, 2])
    w_ada = din("w_ada", [DEPTH, D, 6 * D]); b_ada = din("b_ada", [DEPTH, 6 * D])
    w_a = din("w_a", [DEPTH, D, WA_COLS])
    lamv = din("lamv", [DEPTH, 2, 2, 32]); dnw = din("dnw", [DEPTH, 64])
    plan, tkeys = _na_plan(N)
    NTAB = len(tkeys)
    natab = din("natab", [DEPTH, 6, NTAB, 128, 128])
    qnw = din("qnw", [DEPTH, 384])
    w_uq = din("w_uq", [DEPTH, 256, 384]); w_uqr = din("w_uqr", [DEPTH, 256, 384])
    w_ukn = din("w_ukn", [DEPTH, 128, 256]); w_ukv = din("w_ukv", [DEPTH, 128, 256])
    w_out = din("w_out", [DEPTH, D, D]); w_up = din("w_up", [DEPTH, D, 2 * DFF]); w_down = din("w_down", [DEPTH, DFF, D])
    lnp = din("lnp", [DEPTH, 4, D])
    convp = din("convp", [DEPTH, 128, 44, 4])
    ident_in = din("ident", [128, 128]); cosT_in = din("cosT", [128, N]); sinT_in = din("sinT", [128, N])
    y_out = nc.dram_tensor("y", [N, D], F32, kind="ExternalOutput").ap()

    xc_d = dscr("xc_d", [CTX, D], F32)
    x1_d = dscr("x1_d", [NTOK, D], F32)
    ada_d = dscr("ada_d", [DEPTH, 2, 2, D], F32)
    qa_d = dscr("qa_d", [4, 96, NTOK], BF16); ka_d = dscr("ka_d", [4, 96, NTOK], BF16)
    qn_d = dscr("qn_d", [384, NTOK], BF16); kn_d = dscr("kn_d", [384, NTOK], BF16)
    qm_d = dscr("qm_d", [4, 96, NTOK], BF16); km_d = dscr("km_d", [4, 96, NTOK], BF16)
    va_d = dscr("va_d", [NTOK, 6, 65], BF16); vn_d = dscr("vn_d", [NTOK, 6, 65], BF16); vm_d = dscr("vm_d", [NTOK, 4, 65], BF16)
    mix_d = dscr("mix_d", [8, 128, NTOK], BF16)
    h2_d = dscr("h2_d", [8, 128, NTOK], BF16)
    at_d = dscr("at_d", [22, 128, NTOK], BF16)
    B = {n: Buf("D:" + n, keep=True) for n in ["y", "xc", "x1", "ada", "qa", "ka", "qn", "kn", "qm", "km", "va", "vn", "vm", "mix", "h2", "at", "IN"]}

    uid = [0]

    def sb(st, name, shape, dt):
        uid[0] += 1
        return st.enter_context(nc.sbuf_tensor(f"s{uid[0]}_{name}", list(shape), dt))

    def ps(st, name, shape, dt=F32):
        uid[0] += 1
        return st.enter_context(nc.psum_tensor(f"p{uid[0]}_{name}", list(shape), dt))

    ident = sb(es, "ident", [128, 128], F32); b_ident = Buf("ident", True)
    epsb = sb(es, "epsb", [128, 1], F32); b_eps = Buf()
    modfm = sb(es, "modfm", [128, DEPTH, 48, 2], F32); b_mod = Buf()
    lam_sb = sb(es, "lam_sb", [128, DEPTH, 2], F32); b_lam = Buf()
    K.dma(ident[:], ident_in, B["IN"], b_ident)
    K.op("pool", lambda e: e.memset(epsb[:], LN_EPS), w=[b_eps])

    def x_tile_ap(l, t, final=False):
        if t < NT:
            src = x_in if l == 0 else y_out
            return src[t * 128:(t + 1) * 128, :], (B["IN"] if l == 0 else B["y"])
        src = ctx_in if l == 0 else xc_d
        return src[(t - NT) * 128:(t - NT + 1) * 128, :], (B["IN"] if l == 0 else B["xc"])

    def x_out_ap(t):
        if t < NT:
            return y_out[t * 128:(t + 1) * 128, :], B["y"]
        return xc_d[(t - NT) * 128:(t - NT + 1) * 128, :], B["xc"]

    def load_weight(st, name, src, nchunk, ncols, dst, dbuf, colblk=2048):
        stg = [sb(st, f"{name}_s{i}", [128, colblk], F32) for i in range(2)]
        sbf = [Buf() for _ in range(2)]
        i = 0
        for c in range(nchunk):
            for c0 in range(0, ncols, colblk):
                cw = min(colblk, ncols - c0)
                K.dma(stg[i % 2][:, :cw], src[c * 128:(c + 1) * 128, c0:c0 + cw], B["IN"], sbf[i % 2])
                s_ = stg[i % 2]
                K.op("pool", lambda e, s_=s_, c=c, c0=c0, cw=cw: e.tensor_copy(out=dst[:, c, c0:c0 + cw], in_=s_[:, :cw]),
                     r=[sbf[i % 2]], w=[dbuf])
                i += 1

    def rstd_op(out_ap, in_ap, scale, rbufs, wbuf):
        K.op("act", lambda e: e.activation(out=out_ap, in_=in_ap, func=AF.Ln, bias=epsb[:], scale=scale), r=rbufs + [b_eps], w=[wbuf])
        K.op("act", lambda e: e.activation(out=out_ap, in_=out_ap, func=AF.Exp, scale=-0.5), r=[wbuf], w=[wbuf])

    with ExitStack() as st:
        csb = sb(st, "csb", [128, 8, 2], F32); b_c = Buf()
        K.dma(csb[:], cc_in, B["IN"], b_c)
        K.op("act", lambda e: e.activation(out=csb[:], in_=csb[:], func=AF.Silu), r=[b_c], w=[b_c])
        ones2 = sb(st, "ones2", [1, 2], F32); b_o2 = Buf()
        K.op("pool", lambda e: e.memset(ones2[:], 1.0), w=[b_o2])
        wst = [sb(st, f"wst{i}", [128, 8, 512], F32) for i in range(2)]; b_wst = [Buf(), Buf()]
        bst = [sb(st, f"bst{i}", [1, 512], F32) for i in range(2)]; b_bst = [Buf(), Buf()]
        pfm = [ps(st, f"pfm{i}", [128, 4, 2]) for i in range(2)]; b_pfm = [Buf(), Buf()]
        prow = [ps(st, f"prow{i}", [2, 512]) for i in range(2)]; b_prow = [Buf(), Buf()]
        rsb = [sb(st, f"rsb{i}", [2, 512], F32) for i in range(2)]; b_rsb = [Buf(), Buf()]
        it = 0
        for l in range(DEPTH):
            for cb in range(12):
                j = it % 2
                it += 1
                K.dma(wst[j][:], w_ada[l, :, cb * 512:(cb + 1) * 512].rearrange("(c p) n -> p c n", p=128), B["IN"], b_wst[j])
                K.dma(bst[j][:], b_ada[l:l + 1, cb * 512:(cb + 1) * 512], B["IN"], b_bst[j])
                for cc in range(4):
                    for dc in range(8):
                        K.op("pe", lambda e, j=j, cc=cc, dc=dc: e.matmul(pfm[j][:, cc, :], lhsT=wst[j][:, dc, cc * 128:(cc + 1) * 128],
                                                                          rhs=csb[:, dc, :], start=(dc == 0), stop=False),
                             r=[b_wst[j], b_c], w=[b_pfm[j]])
                    K.op("pe", lambda e, j=j, cc=cc: e.matmul(pfm[j][:, cc, :], lhsT=bst[j][:, cc * 128:(cc + 1) * 128], rhs=ones2[:],
                                                              start=False, stop=True), r=[b_bst[j], b_o2], w=[b_pfm[j]])
                is_scale = cb in (2, 3, 8, 9)
                K.op("dve", lambda e, j=j, l=l, cb=cb, a=(1.0 if is_scale else 0.0): e.tensor_scalar_add(
                    out=modfm[:, l, cb * 4:(cb + 1) * 4, :], in0=pfm[j][:], scalar1=a), r=[b_pfm[j]], w=[b_mod])
                if cb in (4, 5, 10, 11):
                    for dc in range(8):
                        K.op("pe", lambda e, j=j, dc=dc: e.matmul(prow[j][:], lhsT=csb[:, dc, :], rhs=wst[j][:, dc, :], start=(dc == 0), stop=False),
                             r=[b_wst[j], b_c], w=[b_prow[j]])
                    K.op("pe", lambda e, j=j: e.matmul(prow[j][:], lhsT=ones2[:], rhs=bst[j][:], start=False, stop=True),
                         r=[b_bst[j], b_o2], w=[b_prow[j]])
                    K.op("act", lambda e, j=j: e.copy(out=rsb[j][:], in_=prow[j][:]), r=[b_prow[j]], w=[b_rsb[j]])
                    g = 0 if cb < 6 else 1
                    half = cb % 2
                    K.dma(ada_d[l, :, g, half * 512:(half + 1) * 512], rsb[j][:], b_rsb[j], B["ada"])
        lv = sb(st, "lv", [128, DEPTH, 2, 2, 32], F32); b_lv = Buf()
        K.dma(lv[:].rearrange("p l a c b -> p (l a c b)"), lamv.rearrange("l a c b -> (l a c b)").partition_broadcast(128), B["IN"], b_lv)
        lt = sb(st, "lt", [128, DEPTH, 2, 32], F32); b_lt = Buf()
        ls = sb(st, "ls", [128, DEPTH, 2], F32); b_ls = Buf()
        K.op("dve", lambda e: e.tensor_tensor(out=lt[:], in0=lv[:, :, 0, :, :], in1=lv[:, :, 1, :, :], op=ALU.mult), r=[b_lv], w=[b_lt])
        K.op("dve", lambda e: e.tensor_reduce(out=ls[:], in_=lt[:], axis=mybir.AxisListType.X, op=ALU.add), r=[b_lt], w=[b_ls])
        K.op("act", lambda e: e.activation(out=ls[:], in_=ls[:], func=AF.Exp), r=[b_ls], w=[b_ls])
        for l in range(DEPTH):
            lam_init = 0.8 - 0.6 * math.exp(-0.3 * l)
            K.op("dve", lambda e, l=l, li=lam_init: e.scalar_tensor_tensor(out=lam_sb[:, l, 0:1], in0=ls[:, l, 1:2], scalar=-li, in1=ls[:, l, 0:1],
                                                                            op0=ALU.add, op1=ALU.subtract), r=[b_ls], w=[b_lam])
        K.barrier()

    def phaseA(l):
        with ExitStack() as st:
            wa = sb(st, "wa", [128, 8, WA_COLS], BF16); b_wa = Buf()
            load_weight(st, "wa", w_a[l], 8, WA_COLS, wa, b_wa, colblk=1760)
            wq = sb(st, "wq", [128, 2, 384], BF16); b_wq = Buf()
            wqr = sb(st, "wqr", [128, 2, 384], BF16); b_wqr = Buf()
            wkn = sb(st, "wkn", [128, 1, 256], BF16); b_wkn = Buf()
            wkv = sb(st, "wkv", [128, 1, 256], BF16); b_wkv = Buf()
            load_weight(st, "wq", w_uq[l], 2, 384, wq, b_wq, colblk=384)
            load_weight(st, "wqr", w_uqr[l], 2, 384, wqr, b_wqr, colblk=384)
            load_weight(st, "wkn", w_ukn[l], 1, 256, wkn, b_wkn, colblk=256)
            load_weight(st, "wkv", w_ukv[l], 1, 256, wkv, b_wkv, colblk=256)
            gq = sb(st, "gq", [128, 384], F32); b_gq = Buf()
            K.dma(gq[:], qnw[l].partition_broadcast(128), B["IN"], b_gq)
            xt = [sb(st, f"xt{i}", [128, D], F32) for i in range(2)]; b_xt = [Buf(), Buf()]
            xn = [sb(st, f"xn{i}", [128, D], F32) for i in range(2)]; b_xn = [Buf(), Buf()]
            stt = sb(st, "stt", [128, 2, 6], F32); b_stt = Buf()
            mv = sb(st, "mv", [128, 2], F32); b_mv = Buf()
            rs = sb(st, "rs", [128, 1], F32); b_rs = Buf()
            nb = sb(st, "nb", [128, 1], F32); b_nb = Buf()
            hT = [sb(st, f"hT{i}", [128, 8, 512], BF16) for i in range(2)]; b_hT = [Buf(), Buf()]
            cs = [sb(st, f"cs{i}", [128, 512], F32) for i in range(2)]; b_cs = [Buf(), Buf()]
            sn = [sb(st, f"sn{i}", [128, 512], F32) for i in range(2)]; b_sn = [Buf(), Buf()]
            t1 = [sb(st, f"t1{i}", [128, 512], F32) for i in range(2)]; b_t1 = [Buf(), Buf()]
            t2 = [sb(st, f"t2{i}", [128, 512], F32) for i in range(2)]; b_t2 = [Buf(), Buf()]
            ob = [sb(st, f"ob{i}", [128, 512], BF16) for i in range(3)]; b_ob = [Buf() for _ in range(3)]
            vb = [sb(st, f"vb{i}", [128, 6, 65], BF16) for i in range(4)]; b_vb = [Buf() for _ in range(4)]
            cqs = sb(st, "cqs", [128, 384], F32); b_cqs = Buf()
            cqn = sb(st, "cqn", [128, 384], F32); b_cqn = Buf()
            junk = sb(st, "junk", [128, 384], F32); b_junk = Buf()
            ssq = sb(st, "ssq", [128, 2], F32); b_ssq = Buf()
            cT = [sb(st, f"cT{i}", [128, 3, 512], BF16) for i in range(2)]; b_cT = [Buf(), Buf()]
            ptr = [ps(st, f"ptr{i}", [128, 128]) for i in range(2)]; b_ptr = [Buf(), Buf()]
            pfm = [ps(st, f"pA{i}", [128, 512]) for i in range(4)]; b_pfm = [Buf() for _ in range(4)]
            for i in range(4):
                K.op("pool", lambda e, i=i: e.memset(vb[i][:], 1.0), w=[b_vb[i]])
            cnt = {"tr": 0, "pf": 0, "ob": 0, "vb": 0}

            def nxt(k, n):
                v = cnt[k] % n
                cnt[k] += 1
                return v

            def a_load(t):
                src, sbuf_ = x_tile_ap(l, t)
                K.dma(xt[t % 2][:], src, sbuf_, b_xt[t % 2])

            groups = [(g * 4, 4) for g in range(NT // 4)]
            if NT % 4:
                groups.append((NT // 4 * 4, NT % 4))
            groups.append((NT, 2))
            for gi, (t0, ntl) in enumerate(groups):
                is_ctx = t0 >= NT
                ntok = ntl * 128
                tok0 = t0 * 128
                mi = 1 if is_ctx else 0
                hb = gi % 2
                if not is_ctx:
                    K.dma(cs[hb][:, :ntok], cosT_in[:, tok0:tok0 + ntok], B["IN"], b_cs[hb])
                    K.dma(sn[hb][:, :ntok], sinT_in[:, tok0:tok0 + ntok], B["IN"], b_sn[hb])
                for ti in range(ntl):
                    t = t0 + ti
                    xb = t % 2
                    if t == 0:
                        a_load(0)
                    if t + 1 < NTT:
                        a_load(t + 1)
                    K.op("dve", lambda e, xb=xb: e.bn_stats(out=stt[:, 0, :], in_=xt[xb][:, 0:512]), r=[b_xt[xb]], w=[b_stt])
                    K.op("dve", lambda e, xb=xb: e.bn_stats(out=stt[:, 1, :], in_=xt[xb][:, 512:1024]), r=[b_xt[xb]], w=[b_stt])
                    K.op("dve", lambda e: e.bn_aggr(out=mv[:], in_=stt[:].rearrange("p c s -> p (c s)")), r=[b_stt], w=[b_mv])
                    rstd_op(rs[:], mv[:, 1:2], 1.0, [b_mv], b_rs)
                    K.op("dve", lambda e: e.scalar_tensor_tensor(out=nb[:], in0=mv[:, 0:1], scalar=-1.0, in1=rs[:], op0=ALU.mult, op1=ALU.mult),
                         r=[b_mv, b_rs], w=[b_nb])
                    K.op("act", lambda e, xb=xb: e.activation(out=xn[xb][:], in_=xt[xb][:], func=AF.Identity, bias=nb[:], scale=rs[:]),
                         r=[b_xt[xb], b_nb, b_rs], w=[b_xn[xb]])
                    for dc in range(8):
                        p = nxt("tr", 2)
                        K.op("pe", lambda e, p=p, xb=xb, dc=dc: e.transpose(out=ptr[p][:], in_=xn[xb][:, dc * 128:(dc + 1) * 128], identity=ident[:]),
                             r=[b_xn[xb], b_ident], w=[b_ptr[p]])
                        K.op("dve", lambda e, p=p, dc=dc, ti=ti: e.tensor_scalar(out=hT[hb][:, dc, ti * 128:(ti + 1) * 128], in0=ptr[p][:],
                                                                                 scalar1=modfm[:, l, 8 + dc, mi:mi + 1], scalar2=modfm[:, l, dc, mi:mi + 1],
                                                                                 op0=ALU.mult, op1=ALU.add),
                             r=[b_ptr[p], b_mod], w=[b_hT[hb]])

                def fm_mm(col0, m, pbuf):
                    for dc in range(8):
                        K.op("pe", lambda e, dc=dc: e.matmul(pfm[pbuf][:m, :ntok], lhsT=wa[:, dc, col0:col0 + m], rhs=hT[hb][:, dc, :ntok],
                                                              start=(dc == 0), stop=(dc == 7)), r=[b_wa, b_hT[hb]], w=[b_pfm[pbuf]])

                def rope_out(col0, colr, m, dst_aps, dbuf):
                    pa = nxt("pf", 4)
                    fm_mm(col0, m, pa)
                    o = nxt("ob", 3)
                    if is_ctx or colr is None:
                        K.op("act", lambda e: e.copy(out=ob[o][:m, :ntok], in_=pfm[pa][:m, :ntok]), r=[b_pfm[pa]], w=[b_ob[o]])
                    else:
                        pb = nxt("pf", 4)
                        fm_mm(colr, m, pb)
                        K.op("dve", lambda e: e.tensor_tensor(out=t1[hb][:m, :ntok], in0=pfm[pa][:m, :ntok], in1=cs[hb][:m, :ntok], op=ALU.mult),
                             r=[b_pfm[pa], b_cs[hb]], w=[b_t1[hb]])
                        K.op("dve", lambda e: e.tensor_tensor(out=t2[hb][:m, :ntok], in0=pfm[pb][:m, :ntok], in1=sn[hb][:m, :ntok], op=ALU.mult),
                             r=[b_pfm[pb], b_sn[hb]], w=[b_t2[hb]])
                        K.op("pool", lambda e: e.tensor_tensor(out=ob[o][:m, :ntok], in0=t1[hb][:m, :ntok], in1=t2[hb][:m, :ntok], op=ALU.add),
                             r=[b_t1[hb], b_t2[hb]], w=[b_ob[o]])
                    for d_ in dst_aps:
                        K.dma(d_, ob[o][:m, :ntok], b_ob[o], dbuf)

                for c in range(4):
                    rope_out(C_AQ + c * 96, C_AQR + c * 96, 96, [qa_d[c, :, tok0:tok0 + ntok]], B["qa"])
                    rope_out(C_AK + c * 96, C_AKR + c * 96, 96, [ka_d[c, :, tok0:tok0 + ntok]], B["ka"])
                for c in range(3):
                    rope_out(C_NQ + c * 128, None, 128, [qn_d[c * 128:(c + 1) * 128, tok0:tok0 + ntok]], B["qn"])
                    rope_out(C_NK + c * 128, None, 128, [kn_d[c * 128:(c + 1) * 128, tok0:tok0 + ntok]], B["kn"])
                rope_out(C_KR, C_KRR, 32, [km_d[h, 64:96, tok0:tok0 + ntok] for h in range(4)], B["km"])

                for ti in range(ntl):
                    t = t0 + ti
                    for (col0, dst, dbuf) in ((C_AV, va_d, B["va"]), (C_NV, vn_d, B["vn"])):
                        pa = nxt("pf", 4)
                        for dc in range(8):
                            K.op("pe", lambda e, dc=dc, pa=pa, col0=col0: e.matmul(pfm[pa][:, :384], lhsT=hT[hb][:, dc, ti * 128:(ti + 1) * 128],
                                                                                   rhs=wa[:, dc, col0:col0 + 384], start=(dc == 0), stop=(dc == 7)),
                                 r=[b_wa, b_hT[hb]], w=[b_pfm[pa]])
                        v = nxt("vb", 4)
                        K.op("act", lambda e, pa=pa, v=v: e.copy(out=vb[v][:, :, 0:64], in_=pfm[pa][:, :384].rearrange("p (h d) -> p h d", d=64)),
                             r=[b_pfm[pa]], w=[b_vb[v]])
                        K.dma(dst[t * 128:(t + 1) * 128, :, :], vb[v][:], b_vb[v], dbuf)
                    pa = nxt("pf", 4)
                    for dc in range(8):
                        K.op("pe", lambda e, dc=dc, pa=pa: e.matmul(pfm[pa][:, :384], lhsT=hT[hb][:, dc, ti * 128:(ti + 1) * 128],
                                                                    rhs=wa[:, dc, C_CQ:C_CQ + 384], start=(dc == 0), stop=(dc == 7)),
                             r=[b_wa, b_hT[hb]], w=[b_pfm[pa]])
                    K.op("act", lambda e, pa=pa: e.copy(out=cqs[:], in_=pfm[pa][:, :384]), r=[b_pfm[pa]], w=[b_cqs])
                    K.op("dve", lambda e: e.scalar_tensor_tensor(out=junk[:, 0:256], in0=cqs[:, 0:256], scalar=1.0, in1=cqs[:, 0:256], op0=ALU.mult, op1=ALU.mult,
                                                                 accum_out=ssq[:, 0:1]), r=[b_cqs], w=[b_junk, b_ssq])
                    K.op("dve", lambda e: e.scalar_tensor_tensor(out=junk[:, 256:384], in0=cqs[:, 256:384], scalar=1.0, in1=cqs[:, 256:384], op0=ALU.mult, op1=ALU.mult,
                                                                 accum_out=ssq[:, 1:2]), r=[b_cqs], w=[b_junk, b_ssq])
                    rstd_op(ssq[:, 0:1], ssq[:, 0:1], 1.0 / 256, [b_ssq], b_ssq)
                    rstd_op(ssq[:, 1:2], ssq[:, 1:2], 1.0 / 128, [b_ssq], b_ssq)
                    K.op("dve", lambda e: e.scalar_tensor_tensor(out=cqn[:, 0:256], in0=cqs[:, 0:256], scalar=ssq[:, 0:1], in1=gq[:, 0:256], op0=ALU.mult, op1=ALU.mult),
                         r=[b_cqs, b_ssq, b_gq], w=[b_cqn])
                    K.op("dve", lambda e: e.scalar_tensor_tensor(out=cqn[:, 256:384], in0=cqs[:, 256:384], scalar=ssq[:, 1:2], in1=gq[:, 256:384], op0=ALU.mult, op1=ALU.mult),
                         r=[b_cqs, b_ssq, b_gq], w=[b_cqn])
                    for c in range(3):
                        p = nxt("tr", 2)
                        K.op("pe", lambda e, p=p, c=c: e.transpose(out=ptr[p][:], in_=cqn[:, c * 128:(c + 1) * 128], identity=ident[:]),
                             r=[b_cqn, b_ident], w=[b_ptr[p]])
                        K.op("act", lambda e, p=p, c=c: e.copy(out=cT[hb][:, c, ti * 128:(ti + 1) * 128], in_=ptr[p][:]), r=[b_ptr[p]], w=[b_cT[hb]])
                for h in range(4):
                    pa = nxt("pf", 4)
                    for rc in range(2):
                        K.op("pe", lambda e, rc=rc, pa=pa: e.matmul(pfm[pa][:96, :ntok], lhsT=wq[:, rc, h * 96:(h + 1) * 96], rhs=cT[hb][:, rc, :ntok],
                                                                    start=(rc == 0), stop=(rc == 1)), r=[b_wq, b_cT[hb]], w=[b_pfm[pa]])
                    o = nxt("ob", 3)
                    if is_ctx:
                        K.op("act", lambda e, pa=pa, o=o: e.copy(out=ob[o][:96, :ntok], in_=pfm[pa][:96, :ntok]), r=[b_pfm[pa]], w=[b_ob[o]])
                    else:
                        pb = nxt("pf", 4)
                        for rc in range(2):
                            K.op("pe", lambda e, rc=rc, pb=pb: e.matmul(pfm[pb][:96, :ntok], lhsT=wqr[:, rc, h * 96:(h + 1) * 96], rhs=cT[hb][:, rc, :ntok],
                                                                        start=(rc == 0), stop=(rc == 1)), r=[b_wqr, b_cT[hb]], w=[b_pfm[pb]])
                        K.op("act", lambda e, pa=pa, o=o: e.copy(out=ob[o][0:64, :ntok], in_=pfm[pa][0:64, :ntok]), r=[b_pfm[pa]], w=[b_ob[o]])
                        K.op("dve", lambda e, pa=pa: e.tensor_tensor(out=t1[hb][64:96, :ntok], in0=pfm[pa][64:96, :ntok], in1=cs[hb][64:96, :ntok], op=ALU.mult),
                             r=[b_pfm[pa], b_cs[hb]], w=[b_t1[hb]])
                        K.op("dve", lambda e, pb=pb: e.tensor_tensor(out=t2[hb][64:96, :ntok], in0=pfm[pb][64:96, :ntok], in1=sn[hb][64:96, :ntok], op=ALU.mult),
                             r=[b_pfm[pb], b_sn[hb]], w=[b_t2[hb]])
                        K.op("pool", lambda e, o=o: e.tensor_tensor(out=ob[o][64:96, :ntok], in0=t1[hb][64:96, :ntok], in1=t2[hb][64:96, :ntok], op=ALU.add),
                             r=[b_t1[hb], b_t2[hb]], w=[b_ob[o]])
                    K.dma(qm_d[h, :, tok0:tok0 + ntok], ob[o][:96, :ntok], b_ob[o], B["qm"])
                    pa = nxt("pf", 4)
                    K.op("pe", lambda e, pa=pa: e.matmul(pfm[pa][:64, :ntok], lhsT=wkn[:, 0, h * 64:(h + 1) * 64], rhs=cT[hb][:, 2, :ntok], start=True, stop=True),
                         r=[b_wkn, b_cT[hb]], w=[b_pfm[pa]])
                    o = nxt("ob", 3)
                    K.op("act", lambda e, pa=pa, o=o: e.copy(out=ob[o][:64, :ntok], in_=pfm[pa][:64, :ntok]), r=[b_pfm[pa]], w=[b_ob[o]])
                    K.dma(km_d[h, 0:64, tok0:tok0 + ntok], ob[o][:64, :ntok], b_ob[o], B["km"])
                for ti in range(ntl):
                    t = t0 + ti
                    pa = nxt("pf", 4)
                    K.op("pe", lambda e, pa=pa: e.matmul(pfm[pa][:, :256], lhsT=cT[hb][:, 2, ti * 128:(ti + 1) * 128], rhs=wkv[:, 0, :], start=True, stop=True),
                         r=[b_wkv, b_cT[hb]], w=[b_pfm[pa]])
                    v = nxt("vb", 4)
                    K.op("act", lambda e, pa=pa, v=v: e.copy(out=vb[v][:, 0:4, 0:64], in_=pfm[pa][:, :256].rearrange("p (h d) -> p h d", d=64)),
                         r=[b_pfm[pa]], w=[b_vb[v]])
                    K.dma(vm_d[t * 128:(t + 1) * 128, :, :], vb[v][:, 0:4, :], b_vb[v], B["vm"])
            K.barrier()


    def phaseB(l, do_ctx):
        lam_init = 0.8 - 0.6 * math.exp(-0.3 * l)
        for kind in DBG_KINDS:
            with ExitStack() as st:
                if kind == "a":
                    NH, q_d, k_d, v_d, bq, bk, bv, scale = 6, qa_d, ka_d, va_d, B["qa"], B["ka"], B["va"], DA_SCALE
                elif kind == "n":
                    NH, q_d, k_d, v_d, bq, bk, bv, scale = 6, qn_d, kn_d, vn_d, B["qn"], B["kn"], B["vn"], NA_SCALE
                else:
                    NH, q_d, k_d, v_d, bq, bk, bv, scale = 4, qm_d, km_d, vm_d, B["qm"], B["km"], B["vm"], MLA_SCALE
                NCH = 3 if kind == "n" else 4
                kT = sb(st, "kT", [128, NCH, NTOK], BF16); b_kT = Buf()
                vv = sb(st, "vv", [128, NKC, NH, 65], BF16); b_vv = Buf()
                if kind != "n":
                    for h in range(4):
                        K.dma(kT[:96, h, :], k_d[h], bk, b_kT)
                else:
                    for c in range(3):
                        K.dma(kT[:, c, :], k_d[c * 128:(c + 1) * 128, :], bk, b_kT)
                for c0 in range(0, NKC, 8):
                    c1 = min(NKC, c0 + 8)
                    K.dma(vv[:, c0:c1, :, :], v_d[c0 * 128:c1 * 128, :, :].rearrange("(c p) h d -> p c h d", p=128), bv, b_vv)
                NSLOT = {"a": 12, "n": 6, "m": 4}[kind]
                qT = [sb(st, f"qT{i}", [128, NSLOT, 512], BF16) for i in range(2)]; b_qT = [Buf(), Buf()]
                if kind != "m":
                    for i in range(2):
                        K.op("pool", lambda e, i=i: e.memset(qT[i][:], 0.0), w=[b_qT[i]])
                pT = [sb(st, f"pT{i}", [128, 1024], BF16) for i in range(3)]; b_pT = [[Buf(), Buf()] for _ in range(3)]
                osb = [[sb(st, f"osb{p_}{i}", [65, 512], F32) for i in range(2)] for p_ in range(2)]; b_osb = [[Buf(), Buf()], [Buf(), Buf()]]
                mixt = [sb(st, f"mixt{i}", [128, 4, 128], F32) for i in range(2)]; b_mixt = [Buf(), Buf()]
                pending = []
                att_no = [0]

                def defer(fn):
                    pending.append([1, fn])

                def tick():
                    for it in pending:
                        it[0] -= 1
                    while pending and pending[0][0] <= 0:
                        pending.pop(0)[1]()
                mixo = [sb(st, f"mixo{i}", [128, 512], BF16) for i in range(2)]; b_mixo = [Buf(), Buf()]
                rc_ = sb(st, "rc_", [128, 2], F32); b_rc = Buf()
                o1 = sb(st, "o1", [128, 64], F32); b_o1 = Buf()
                o2 = sb(st, "o2", [128, 64], F32); b_o2 = Buf()
                jk = sb(st, "jk", [128, 64], F32); b_jk = Buf()
                s2 = sb(st, "s2", [128, 1], F32); b_s2 = Buf()
                sc2 = [ps(st, f"sc{i}", [128, 1024]) for i in range(2)]
                sc = [[sc2[i][:, s_ * 512:(s_ + 1) * 512] for s_ in range(2)] for i in range(2)]; b_sc = [[Buf(), Buf()], [Buf(), Buf()]]
                acc = [ps(st, f"acc{i}", [65, 512]) for i in range(2)]; b_acc = [Buf(), Buf()]
                pmisc = ps(st, "pmisc", [128, 512])
                _bp = Buf()
                ptk = [pmisc[:, 0:65], pmisc[:, 128:193]]; b_ptk = [_bp, _bp]
                ptm_t = ps(st, "ptm", [128, 128]); ptm = ptm_t[:]; b_ptm = Buf()
                cnt = {"sc": 0, "pT": 0, "mixo": 0, "q": 0}

                def nxt(k, n):
                    v = cnt[k] % n
                    cnt[k] += 1
                    return v

                if kind == "a":
                    dn = sb(st, "dn", [128, 64], F32); b_dn = Buf()
                    K.dma(dn[:], dnw[l].partition_broadcast(128), B["IN"], b_dn)
                    K.op("dve", lambda e: e.tensor_scalar_mul(out=dn[:], in0=dn[:], scalar1=1.0 - lam_init), r=[b_dn], w=[b_dn])
                if kind == "n":
                    tab = sb(st, "tab", [128, 6, NTAB, 128], F32); b_tab = Buf()
                    for h in range(6):
                        K.dma(tab[:, h, :, :], natab[l, h].rearrange("t k q -> k t q"), B["IN"], b_tab)
                    sbias = [sb(st, f"sbias{i}", [128, 256], F32) for i in range(2)]; b_sbias = [Buf(), Buf()]

                def attend(qb, qoff, nq, streams, kcs, tmap=None, hpair=0):
                    nk = len(kcs)
                    pend = None
                    one_bank = 2 * nq <= 512
                    for i, kc in enumerate(kcs):
                        sbi = nxt("sc", 2)
                        for s, (ch, nr, slot, vh) in enumerate(streams):
                            dst = sc[sbi][0][:, s * nq:(s + 1) * nq] if one_bank else sc[sbi][s][:, :nq]
                            K.op("pe", lambda e, s=s, ch=ch, nr=nr, slot=slot, kc=kc, sbi=sbi: e.matmul(
                                dst, lhsT=kT[:nr, ch, kc * 128:(kc + 1) * 128],
                                rhs=qT[qb][:nr, slot, qoff:qoff + nq], start=True, stop=True), r=[b_kT, b_qT[qb]],
                                w=[b_sc[sbi][0 if one_bank else s]])
                        pi = nxt("pT", 3)
                        if tmap is not None and kc in tmap:
                            assert one_bank
                            ti_ = tmap[kc]
                            K.op("dve", lambda e, sbi=sbi, ti_=ti_: e.scalar_tensor_tensor(
                                out=sbias[sbi][:, :2 * nq].rearrange("p (s q) -> p s q", s=2), in0=sc[sbi][0][:, :2 * nq].rearrange("p (s q) -> p s q", s=2),
                                scalar=scale, in1=tab[:, 2 * hpair:2 * hpair + 2, ti_, :], op0=ALU.mult, op1=ALU.add),
                                r=[b_sc[sbi][0], b_tab], w=[b_sbias[sbi]])
                            K.op("act", lambda e, sbi=sbi, pi=pi: e.activation(out=pT[pi][:, :2 * nq], in_=sbias[sbi][:, :2 * nq], func=AF.Exp),
                                 r=[b_sbias[sbi]], w=b_pT[pi])
                        elif one_bank:
                            K.op("act", lambda e, sbi=sbi, pi=pi: e.activation(out=pT[pi][:, :2 * nq], in_=sc[sbi][0][:, :2 * nq], func=AF.Exp, scale=scale),
                                 r=[b_sc[sbi][0]], w=b_pT[pi])
                        elif nq == 512 and MERGE_EXP:
                            K.op("act", lambda e, sbi=sbi, pi=pi: e.activation(out=pT[pi][:, :1024], in_=sc2[sbi][:, :1024], func=AF.Exp, scale=scale),
                                 r=b_sc[sbi], w=b_pT[pi])
                        else:
                            for s in range(2):
                                K.op("act", lambda e, sbi=sbi, pi=pi, s=s: e.activation(out=pT[pi][:, s * nq:(s + 1) * nq], in_=sc[sbi][s][:, :nq], func=AF.Exp, scale=scale),
                                     r=[b_sc[sbi][s]], w=[b_pT[pi][s]])
                        if pend is not None:
                            pend()

                        def mk(i=i, kc=kc, pi=pi):
                            for s, (ch, nr, slot, vh) in enumerate(streams):
                                K.op("pe", lambda e, s=s, vh=vh: e.matmul(acc[s][:, :nq], lhsT=vv[:, kc, vh, :], rhs=pT[pi][:, s * nq:(s + 1) * nq],
                                                                          start=(i == 0), stop=(i == nk - 1)), r=[b_vv, b_pT[pi][s]], w=[b_acc[s]])
                        pend = mk
                    pend()
                    par = att_no[0] % 2
                    att_no[0] += 1
                    for s in range(2):
                        K.op("act", lambda e, s=s: e.copy(out=osb[par][s][:, :nq], in_=acc[s][:, :nq]), r=[b_acc[s]], w=[b_osb[par][s]])
                    tick()
                    return par

                def fin(par, nq, slot0, diff_j, mo):
                    for qi in range(nq // 128):
                        slot = slot0 + qi
                        for s in range(2):
                            K.op("pe", lambda e, s=s: e.transpose(out=ptk[s], in_=osb[par][s][:65, qi * 128:(qi + 1) * 128], identity=ident[:65, :65]),
                                 r=[b_osb[par][s], b_ident], w=[b_ptk[s]])
                        for s in range(2):
                            K.op("dve", lambda e, s=s: e.reciprocal(out=rc_[:, s:s + 1], in_=ptk[s][:, 64:65]), r=[b_ptk[s]], w=[b_rc])
                        if diff_j is None:
                            for s in range(2):
                                K.op("dve", lambda e, s=s: e.tensor_scalar(out=mixt[mo][:, slot, s * 64:(s + 1) * 64], in0=ptk[s][:, 0:64], scalar1=rc_[:, s:s + 1],
                                                                            scalar2=None, op0=ALU.mult), r=[b_ptk[s], b_rc], w=[b_mixt[mo]])
                        else:
                            j = diff_j
                            K.op("dve", lambda e: e.tensor_scalar(out=o1[:], in0=ptk[0][:, 0:64], scalar1=rc_[:, 0:1], scalar2=None, op0=ALU.mult),
                                 r=[b_ptk[0], b_rc], w=[b_o1])
                            K.op("dve", lambda e: e.tensor_scalar(out=o2[:], in0=ptk[1][:, 0:64], scalar1=rc_[:, 1:2], scalar2=None, op0=ALU.mult),
                                 r=[b_ptk[1], b_rc], w=[b_o2])
                            K.op("dve", lambda e: e.scalar_tensor_tensor(out=o1[:], in0=o2[:], scalar=lam_sb[:, l, 0:1], in1=o1[:], op0=ALU.mult, op1=ALU.add),
                                 r=[b_o2, b_lam, b_o1], w=[b_o1])
                            K.op("pool", lambda e: e.memset(s2[:], 0.0), w=[b_s2])
                            K.op("dve", lambda e: e.scalar_tensor_tensor(out=jk[:], in0=o1[:], scalar=1.0, in1=o1[:], op0=ALU.mult, op1=ALU.mult, accum_out=s2[:]),
                                 r=[b_o1], w=[b_jk, b_s2])
                            rstd_op(s2[:], s2[:], 1.0 / 64, [b_s2], b_s2)
                            K.op("dve", lambda e: e.scalar_tensor_tensor(out=mixt[mo][:, slot, j * 64:(j + 1) * 64], in0=o1[:], scalar=s2[:, 0:1], in1=dn[:],
                                                                         op0=ALU.mult, op1=ALU.mult), r=[b_o1, b_s2, b_dn], w=[b_mixt[mo]])

                def flush(slot, mo, col):
                    K.op("pe", lambda e: e.transpose(out=ptm, in_=mixt[mo][:, slot, :], identity=ident[:]), r=[b_mixt[mo], b_ident], w=[b_ptm])
                    K.op("act", lambda e: e.copy(out=mixo[mo][:, col:col + 128], in_=ptm), r=[b_ptm], w=[b_mixo[mo]])

                def unit_done(par, nq, slot0, diff_j, mo, flush_slots, dma_args):
                    def stage1():
                        fin(par, nq, slot0, diff_j, mo)
                        if flush_slots:
                            def stage2():
                                for (slot, col) in flush_slots:
                                    flush(slot, mo, col)
                                if dma_args is not None:
                                    cg_, q0_, nq_ = dma_args
                                    K.dma(mix_d[cg_, :, q0_:q0_ + nq_], mixo[mo][:, :nq_], b_mixo[mo], B["mix"])
                            defer(stage2)
                    defer(stage1)

                qblocks = [(g * 512, min(512, N - g * 512)) for g in range((N + 511) // 512)]
                if do_ctx:
                    qblocks.append((N, CTX))
                def q_load(bi):
                    q0, nq = qblocks[bi]
                    qb = bi % 2
                    if kind == "m":
                        for h in range(4):
                            K.dma(qT[qb][:96, h, :nq], q_d[h, :, q0:q0 + nq], bq, b_qT[qb])
                    elif kind == "a":
                        for g in range(12):
                            r0 = (g % 3) * 32
                            K.dma(qT[qb][r0:r0 + 32, g, :nq], q_d[g // 3, r0:r0 + 32, q0:q0 + nq], bq, b_qT[qb])
                    else:
                        for h in range(6):
                            r0 = (h % 2) * 64
                            K.dma(qT[qb][r0:r0 + 64, h, :nq], q_d[(h // 2) * 128 + r0:(h // 2) * 128 + r0 + 64, q0:q0 + nq], bq, b_qT[qb])

                q_load(0)
                for bi, (q0, nq) in enumerate(qblocks):
                    is_ctx = q0 >= N
                    qb = bi % 2
                    if bi + 1 < len(qblocks):
                        q_load(bi + 1)
                    allk = list(range(NT, NT + 2)) if is_ctx else list(range(NKC))
                    for c in range(3 if kind != "m" else 2):
                        mo = nxt("mixo", 2)
                        cg = c + (0 if kind == "a" else 3 if kind == "n" else 6)
                        allslots = [(qi, qi * 128) for qi in range(nq // 128)]
                        if kind == "a":
                            for j in range(2):
                                hh_ = 2 * c + j
                                par = attend(qb, 0, nq, [((2 * hh_) // 3, 96, 2 * hh_, hh_), ((2 * hh_ + 1) // 3, 96, 2 * hh_ + 1, hh_)], allk)
                                unit_done(par, nq, 0, j, mo, allslots if j == 1 else None, (cg, q0, nq))
                        elif kind == "m":
                            par = attend(qb, 0, nq, [(2 * c, 96, 2 * c, 2 * c), (2 * c + 1, 96, 2 * c + 1, 2 * c + 1)], allk)
                            unit_done(par, nq, 0, None, mo, allslots, (cg, q0, nq))
                        else:
                            stn = [(c, 128, 2 * c, 2 * c), (c, 128, 2 * c + 1, 2 * c + 1)]
                            if is_ctx:
                                par = attend(qb, 0, nq, stn, allk)
                                unit_done(par, nq, 0, None, mo, allslots, (cg, q0, nq))
                            else:
                                for qi in range(nq // 128):
                                    qp = (q0 + qi * 128) // 128
                                    tmap = {kp: ti_ for (kp, ti_) in plan[qp]}
                                    kcs = [kp for (kp, ti_) in plan[qp]] + [NT, NT + 1]
                                    par = attend(qb, qi * 128, 128, stn, kcs, tmap, c)
                                    unit_done(par, 128, qi, None, mo, [(qi, qi * 128)], (cg, q0, nq) if qi == nq // 128 - 1 else None)
                while pending:
                    tick()
                K.barrier()

    def ln_norm(src, b_src, dst, b_dst, tmp):
        stt, mv, rs, nb, b_stt, b_mv, b_rs, b_nb = tmp
        K.op("dve", lambda e: e.bn_stats(out=stt[:, 0, :], in_=src[:, 0:512]), r=[b_src], w=[b_stt])
        K.op("dve", lambda e: e.bn_stats(out=stt[:, 1, :], in_=src[:, 512:1024]), r=[b_src], w=[b_stt])
        K.op("dve", lambda e: e.bn_aggr(out=mv[:], in_=stt[:].rearrange("p c s -> p (c s)")), r=[b_stt], w=[b_mv])
        rstd_op(rs[:], mv[:, 1:2], 1.0, [b_mv], b_rs)
        K.op("dve", lambda e: e.scalar_tensor_tensor(out=nb[:], in0=mv[:, 0:1], scalar=-1.0, in1=rs[:], op0=ALU.mult, op1=ALU.mult),
             r=[b_mv, b_rs], w=[b_nb])
        K.op("act", lambda e: e.activation(out=dst[:], in_=src[:], func=AF.Identity, bias=nb[:], scale=rs[:]), r=[b_src, b_nb, b_rs], w=[b_dst])

    def mk_tmp(st, pfx):
        return (sb(st, pfx + "stt", [128, 2, 6], F32), sb(st, pfx + "mv", [128, 2], F32), sb(st, pfx + "rs", [128, 1], F32), sb(st, pfx + "nb", [128, 1], F32),
                Buf(), Buf(), Buf(), Buf())

    def postnorm(py, b_py, xres, b_xres, gate, b_gate, g_bc, b_bc, b_gb, z, b_z, xn, b_xn, tmp):
        for hh in range(2):
            K.op("dve", lambda e, hh=hh: e.tensor_tensor(out=z[:, hh * 512:(hh + 1) * 512], in0=py[hh][:], in1=gate[:, hh * 512:(hh + 1) * 512], op=ALU.mult),
                 r=[b_py[hh], b_gate], w=[b_z])
        K.op("dve", lambda e: e.scalar_tensor_tensor(out=z[:], in0=xres[:], scalar=ALPHA, in1=z[:], op0=ALU.mult, op1=ALU.add), r=[b_xres, b_z], w=[b_z])
        ln_norm(z, b_z, xn, b_xn, tmp)
        K.op("dve", lambda e: e.tensor_tensor(out=xn[:], in0=xn[:], in1=g_bc[:], op=ALU.mult), r=[b_xn, b_gb[0]], w=[b_xn])
        K.op("pool", lambda e: e.tensor_tensor(out=z[:], in0=xn[:], in1=b_bc[:], op=ALU.add), r=[b_xn, b_gb[1]], w=[b_z])

    def bc_load(st, name, src_row, dbuf_src):
        t = sb(st, name, [128, D], F32)
        b = Buf()
        K.dma(t[:], src_row.partition_broadcast(128), dbuf_src, b)
        return t, b

    def tok_tiles(do_ctx):
        return list(range(NT)) + ([NT, NT + 1] if do_ctx else [])

    def phaseC1(l, do_ctx):
        with ExitStack() as st:
            wo = sb(st, "wo", [128, 8, D], BF16); b_wo = Buf()
            load_weight(st, "wo", w_out[l], 8, D, wo, b_wo, colblk=1024)
            gate = [bc_load(st, f"gate{m}", ada_d[l, m, 0, :], B["ada"]) for m in range(2)]
            g_bc, b_g = bc_load(st, "g_bc", lnp[l, 0, :], B["IN"])
            b_bc, b_b = bc_load(st, "b_bc", lnp[l, 1, :], B["IN"])
            b_gb = (b_g, b_b)
            mT = [sb(st, f"mT{i}", [128, 8, 128], BF16) for i in range(2)]; b_mT = [Buf(), Buf()]
            xr = [sb(st, f"xr{i}", [128, D], F32) for i in range(2)]; b_xr = [Buf(), Buf()]
            z = [sb(st, f"z{i}", [128, D], F32) for i in range(2)]; b_z = [Buf(), Buf()]
            xn = sb(st, "xn", [128, D], F32); b_xn = Buf()
            xn2 = sb(st, "xn2", [128, D], F32); b_xn2 = Buf()
            h2 = [sb(st, f"h2{i}", [128, 8, 128], BF16) for i in range(2)]; b_h2 = [Buf(), Buf()]
            tmp = mk_tmp(st, "c1")
            py = [ps(st, f"py{i}", [128, 512]) for i in range(2)]; b_py = [Buf(), Buf()]
            ptr = [ps(st, f"ptr{i}", [128, 128]) for i in range(2)]; b_ptr = [Buf(), Buf()]
            ntr = 0
            mi_of = lambda t: 1 if t >= NT else 0

            def c1_load(t):
                i = t % 2
                K.dma(mT[i][:], mix_d[:, :, t * 128:(t + 1) * 128].rearrange("c p t -> p c t"), B["mix"], b_mT[i])
                src, sbuf_ = x_tile_ap(l, t)
                K.dma(xr[i][:], src, sbuf_, b_xr[i])

            tl = tok_tiles(do_ctx)
            c1_load(tl[0])
            for ti_, t in enumerate(tl):
                i = t % 2
                mi = mi_of(t)
                if ti_ + 1 < len(tl):
                    c1_load(tl[ti_ + 1])
                for hh in range(2):
                    for fc in range(8):
                        K.op("pe", lambda e, hh=hh, fc=fc: e.matmul(py[hh][:], lhsT=mT[i][:, fc, :], rhs=wo[:, fc, hh * 512:(hh + 1) * 512],
                                                                    start=(fc == 0), stop=(fc == 7)), r=[b_mT[i], b_wo], w=[b_py[hh]])
                postnorm(py, b_py, xr[i], b_xr[i], gate[mi][0], gate[mi][1], g_bc, b_bc, b_gb, z[i], b_z[i], xn, b_xn, tmp)
                K.dma(x1_d[t * 128:(t + 1) * 128, :], z[i][:], b_z[i], B["x1"])
                ln_norm(z[i], b_z[i], xn2, b_xn2, tmp)
                for dc in range(8):
                    p = ntr % 2
                    ntr += 1
                    K.op("pe", lambda e, p=p, dc=dc: e.transpose(out=ptr[p][:], in_=xn2[:, dc * 128:(dc + 1) * 128], identity=ident[:]),
                         r=[b_xn2, b_ident], w=[b_ptr[p]])
                    K.op("dve", lambda e, p=p, dc=dc: e.tensor_scalar(out=h2[i][:, dc, :], in0=ptr[p][:], scalar1=modfm[:, l, 32 + dc, mi:mi + 1],
                                                                      scalar2=modfm[:, l, 24 + dc, mi:mi + 1], op0=ALU.mult, op1=ALU.add),
                         r=[b_ptr[p], b_mod], w=[b_h2[i]])
                K.dma(h2_d[:, :, t * 128:(t + 1) * 128].rearrange("c p t -> p c t"), h2[i][:], b_h2[i], B["h2"])
            K.barrier()

    def phaseC2(l, do_ctx):
        with ExitStack() as st:
            wu = sb(st, "wu", [128, 8, 2 * DFF], BF16); b_wu = Buf()
            load_weight(st, "wu", w_up[l], 8, 2 * DFF, wu, b_wu, colblk=1408)
            cp = sb(st, "cp", [128, 44, 4], F32); b_cp = Buf()
            K.dma(cp[:], convp[l], B["IN"], b_cp)
            hg = [sb(st, f"hg{i}", [128, 8, 514], BF16) for i in range(2)]; b_hg = [Buf(), Buf()]
            u = [[sb(st, f"u{g}{i}", [128, 514], F32) for i in range(2)] for g in range(2)]; b_u = [[Buf(), Buf()], [Buf(), Buf()]]
            cv = [[sb(st, f"cv{g}{i}", [128, 512], F32) for i in range(2)] for g in range(2)]; b_cv = [[Buf(), Buf()], [Buf(), Buf()]]
            ao = [sb(st, f"ao{i}", [128, 512], BF16) for i in range(2)]; b_ao = [Buf(), Buf()]
            pum = [[ps(st, f"pum{g}{i}", [128, 512]) for i in range(2)] for g in range(2)]; b_pum = [[Buf(), Buf()], [Buf(), Buf()]]
            puh_t = ps(st, "puh", [128, 512])
            puh = [[puh_t[:, (g * 2 + i) * 8:(g * 2 + i) * 8 + 2] for i in range(2)] for g in range(2)]; _bh = Buf(); b_puh = [[_bh, _bh], [_bh, _bh]]
            seqs = [(0, N)] + ([(N, N + CTX)] if do_ctx else [])
            glist = [(s0, s1, tok0) for (s0, s1) in seqs for tok0 in range(s0, s1, 512)]

            def c2_load(gi_):
                s0, s1, tok0 = glist[gi_]
                ntok = min(512, s1 - tok0)
                hb = gi_ % 2
                lo = tok0 - 1 if tok0 > s0 else tok0
                hi = tok0 + ntok + 1 if tok0 + ntok < s1 else tok0 + ntok
                if lo == tok0 or hi == tok0 + ntok:
                    K.op("pool", lambda e, hb=hb: e.memset(hg[hb][:], 0.0), w=[b_hg[hb]])
                K.dma(hg[hb][:, :, lo - tok0 + 1:hi - tok0 + 1], h2_d[:, :, lo:hi].rearrange("c p t -> p c t"), B["h2"], b_hg[hb])

            it = 0
            c2_load(0)
            for gi, (s0, s1, tok0) in enumerate(glist):
                if True:
                    ntok = min(512, s1 - tok0)
                    hb = gi % 2
                    if gi + 1 < len(glist):
                        c2_load(gi + 1)
                    for j in range(22):
                        i = it % 2
                        it += 1
                        for g in range(2):
                            col0 = g * DFF + j * 128
                            for dc in range(8):
                                K.op("pe", lambda e, g=g, dc=dc, col0=col0: e.matmul(pum[g][i][:, :ntok], lhsT=wu[:, dc, col0:col0 + 128], rhs=hg[hb][:, dc, 0:ntok],
                                                                                     start=(dc == 0), stop=(dc == 7)), r=[b_wu, b_hg[hb]], w=[b_pum[g][i]])
                            for dc in range(8):
                                K.op("pe", lambda e, g=g, dc=dc, col0=col0: e.matmul(puh[g][i], lhsT=wu[:, dc, col0:col0 + 128], rhs=hg[hb][:, dc, ntok:ntok + 2],
                                                                                     start=(dc == 0), stop=(dc == 7)), r=[b_wu, b_hg[hb]], w=[b_puh[g][i]])
                            K.op("act", lambda e, g=g: e.copy(out=u[g][i][:, 0:ntok], in_=pum[g][i][:, :ntok]), r=[b_pum[g][i]], w=[b_u[g][i]])
                            K.op("act", lambda e, g=g: e.copy(out=u[g][i][:, ntok:ntok + 2], in_=puh[g][i]), r=[b_puh[g][i]], w=[b_u[g][i]])
                        eng = "dve"
                        for g in range(2):
                            ch = g * 22 + j
                            K.op(eng, lambda e, g=g, ch=ch: e.tensor_scalar(out=cv[g][i][:, :ntok], in0=u[g][i][:, 0:ntok], scalar1=cp[:, ch, 0:1], scalar2=cp[:, ch, 3:4],
                                                                             op0=ALU.mult, op1=ALU.add), r=[b_u[g][i], b_cp], w=[b_cv[g][i]])
                        for k in (1, 2):
                            for g in range(2):
                                ch = g * 22 + j
                                K.op(eng, lambda e, g=g, ch=ch, k=k: e.scalar_tensor_tensor(out=cv[g][i][:, :ntok], in0=u[g][i][:, k:k + ntok], scalar=cp[:, ch, k:k + 1],
                                                                                            in1=cv[g][i][:, :ntok], op0=ALU.mult, op1=ALU.add),
                                     r=[b_u[g][i], b_cp, b_cv[g][i]], w=[b_cv[g][i]])
                        K.op("act", lambda e: e.activation(out=cv[0][i][:, :ntok], in_=cv[0][i][:, :ntok], func=AF.Silu), r=[b_cv[0][i]], w=[b_cv[0][i]])
                        K.op("dve", lambda e: e.tensor_tensor(out=ao[i][:, :ntok], in0=cv[0][i][:, :ntok], in1=cv[1][i][:, :ntok], op=ALU.mult),
                             r=[b_cv[0][i], b_cv[1][i]], w=[b_ao[i]])
                        K.dma(at_d[j, :, tok0:tok0 + ntok], ao[i][:, :ntok], b_ao[i], B["at"])
            K.barrier()

    def phaseC3(l, do_ctx):
        with ExitStack() as st:
            wd = sb(st, "wd", [128, 22, D], BF16); b_wd = Buf()
            load_weight(st, "wd", w_down[l], 22, D, wd, b_wd, colblk=1024)
            gate = [bc_load(st, f"gate{m}", ada_d[l, m, 1, :], B["ada"]) for m in range(2)]
            g_bc, b_g = bc_load(st, "g_bc", lnp[l, 2, :], B["IN"])
            b_bc, b_b = bc_load(st, "b_bc", lnp[l, 3, :], B["IN"])
            b_gb = (b_g, b_b)
            aT = [sb(st, f"aT{i}", [128, 22, 128], BF16) for i in range(2)]; b_aT = [Buf(), Buf()]
            xr = [sb(st, f"xr{i}", [128, D], F32) for i in range(2)]; b_xr = [Buf(), Buf()]
            z = [sb(st, f"z{i}", [128, D], F32) for i in range(2)]; b_z = [Buf(), Buf()]
            xn = sb(st, "xn", [128, D], F32); b_xn = Buf()
            tmp = mk_tmp(st, "c3")
            py = [ps(st, f"py{i}", [128, 512]) for i in range(2)]; b_py = [Buf(), Buf()]
            def c3_load(t):
                i = t % 2
                K.dma(aT[i][:], at_d[:, :, t * 128:(t + 1) * 128].rearrange("c p t -> p c t"), B["at"], b_aT[i])
                K.dma(xr[i][:], x1_d[t * 128:(t + 1) * 128, :], B["x1"], b_xr[i])

            tl = tok_tiles(do_ctx)
            c3_load(tl[0])
            for ti_, t in enumerate(tl):
                i = t % 2
                mi = 1 if t >= NT else 0
                if ti_ + 1 < len(tl):
                    c3_load(tl[ti_ + 1])
                for hh in range(2):
                    for fc in range(22):
                        K.op("pe", lambda e, hh=hh, fc=fc: e.matmul(py[hh][:], lhsT=aT[i][:, fc, :], rhs=wd[:, fc, hh * 512:(hh + 1) * 512],
                                                                    start=(fc == 0), stop=(fc == 21)), r=[b_aT[i], b_wd], w=[b_py[hh]])
                postnorm(py, b_py, xr[i], b_xr[i], gate[mi][0], gate[mi][1], g_bc, b_bc, b_gb, z[i], b_z[i], xn, b_xn, tmp)
                dst, dbuf = x_out_ap(t)
                K.dma(dst, z[i][:], b_z[i], dbuf)
            K.barrier()

    for l in range(DEPTH):
        do_ctx = l < DEPTH - 1
        for nm, fn in (("A", lambda: phaseA(l)), ("B", lambda: phaseB(l, do_ctx)), ("C1", lambda: phaseC1(l, do_ctx)),
                       ("C2", lambda: phaseC2(l, do_ctx)), ("C3", lambda: phaseC3(l, do_ctx))):
            if stop_after is not None and stop_after == "0":
                break
            fn()
            if stop_after == nm:
                break
        if stop_after is not None:
            break
    K.barrier()
    es.close()
    return nc


_NC_CACHE = {}


def _host_inputs(N, DEPTH, x, c, ctx, c_ctx, w_ada, b_ada, w_in, lam_q1, lam_k1, lam_q2, lam_k2, diff_norm_w, na_rpb,
                 mla_q_norm_w, mla_kv_norm_w, w_uq, w_ukv, w_out, ln1_g, ln1_b, w_up, conv_w, conv_b, w_down, ln2_g, ln2_b):
    f = lambda a: np.ascontiguousarray(np.asarray(a, dtype=np.float32))
    L = DEPTH
    w_in = f(w_in)[:L]
    plan, tkeys = _na_plan(N)
    cosT, sinT = _rope_tables(N)
    w_ukv = f(w_ukv)[:L].reshape(L, 128, 4, 128)
    shared = {
        "w_ada": f(w_ada)[:L], "b_ada": f(b_ada)[:L],
        "w_a": np.ascontiguousarray(w_in[:, :, _win_cols()]),
        "lamv": np.ascontiguousarray(np.stack([np.stack([f(lam_q1)[:L], f(lam_q2)[:L]], axis=1), np.stack([f(lam_k1)[:L], f(lam_k2)[:L]], axis=1)], axis=1)),
        "dnw": f(diff_norm_w)[:L],
        "natab": _na_tables(f(na_rpb)[:L], tkeys),
        "qnw": np.ascontiguousarray(np.concatenate([f(mla_q_norm_w)[:L], f(mla_kv_norm_w)[:L]], axis=1)),
        "w_uq": f(w_uq)[:L], "w_uqr": np.ascontiguousarray(f(w_uq)[:L][:, :, _wuq_rot_cols()]),
        "w_ukn": np.ascontiguousarray(w_ukv[:, :, :, 0:64].reshape(L, 128, 256)),
        "w_ukv": np.ascontiguousarray(w_ukv[:, :, :, 64:128].reshape(L, 128, 256)),
        "w_out": f(w_out)[:L], "w_up": f(w_up)[:L], "w_down": f(w_down)[:L],
        "lnp": np.ascontiguousarray(np.stack([f(ln1_g)[:L], f(ln1_b)[:L], f(ln2_g)[:L], f(ln2_b)[:L]], axis=1)),
        "convp": np.ascontiguousarray(np.concatenate([f(conv_w)[:L], f(conv_b)[:L][:, None, :]], axis=1).reshape(L, 4, 44, 128).transpose(0, 3, 2, 1)),
        "ident": np.eye(128, dtype=np.float32), "cosT": cosT, "sinT": sinT,
    }
    x = f(x); ctx = f(ctx); c = f(c); c_ctx = f(c_ctx)
    maps = []
    for b in range(x.shape[0]):
        m = dict(shared)
        m["x"] = x[b]
        m["ctx"] = ctx[b]
        m["cc"] = np.ascontiguousarray(np.stack([c[b], c_ctx], axis=0).reshape(2, 8, 128).transpose(2, 1, 0))
        maps.append(m)
    return maps


def run(N, DEPTH, inputs, stop_after=None):
    key = (N, DEPTH, stop_after)
    if key not in _NC_CACHE:
        _NC_CACHE[key] = build(N, DEPTH, stop_after)
    nc = _NC_CACHE[key]
    maps = _host_inputs(N, DEPTH, **inputs)
    res = run_bass_kernel_spmd(nc, maps, core_ids=list(range(len(maps))))
    return np.stack([np.asarray(r["y"], dtype=np.float32) for r in res.results], axis=0)


def kernel(**inputs):
    N = inputs["x"].shape[1]
    return run(N, DEPTH_FULL, inputs)
```

```python
import math
from contextlib import ExitStack

import numpy as np
import concourse.bass as bass
import concourse.mybir as mybir
from concourse.bass_utils import run_bass_kernel_spmd

F32 = mybir.dt.float32
BF16 = mybir.dt.bfloat16
AF = mybir.ActivationFunctionType
ALU = mybir.AluOpType

D = 1024
CTX = 256
GW = 64
DFF = 2816
NEG = -30000.0
LN_EPS = 1e-6
DEPTH_FULL = 4
DBG_KINDS = ("a", "n", "m")
MERGE_EXP = True
ALPHA = (2 * DEPTH_FULL) ** 0.25
DA_SCALE = 32 ** -0.5
NA_SCALE = 64 ** -0.5
MLA_SCALE = 96 ** -0.5

O_AQ, O_AK, O_AV, O_NQ, O_NK, O_NV, O_CQ, O_CKV, O_KR = 0, 384, 768, 1152, 1536, 1920, 2304, 2560, 2688


def _rot_src(n):
    f = np.arange(n)
    j = f % 16
    return np.where(j < 8, f + 8, f - 8)


def _win_cols():
    aq = np.arange(384)
    cols = []
    cols.append(O_AQ + aq)
    cols.append(O_AQ + _rot_src(384))
    cols.append(O_AK + aq)
    cols.append(O_AK + _rot_src(384))
    cols.append(O_NQ + aq)
    cols.append(O_NK + aq)
    cols.append(O_KR + np.arange(32))
    cols.append(O_KR + _rot_src(32))
    cols.append(O_AV + aq)
    cols.append(O_NV + aq)
    cols.append(O_CQ + np.arange(384))
    return np.concatenate(cols)


WA_COLS = 3520
C_AQ, C_AQR, C_AK, C_AKR, C_NQ, C_NK, C_KR, C_KRR, C_AV, C_NV, C_CQ = 0, 384, 768, 1152, 1536, 1920, 2304, 2336, 2368, 2752, 3136


def _wuq_rot_cols():
    c = np.arange(384)
    h, f = c // 96, c % 96
    r = f - 64
    rr = np.where(r % 16 < 8, r + 8, r - 8)
    return np.where(f < 64, c, h * 96 + 64 + rr)


def _rope_tables(n):
    t = np.arange(n)
    row = (t // GW).astype(np.float32)
    col = (t % GW).astype(np.float32)
    inv = (10000.0 ** (-np.arange(0, 16, 2, dtype=np.float32) / 16)).astype(np.float32)
    cosT = np.zeros((128, n), np.float32)
    sinT = np.zeros((128, n), np.float32)
    for p in range(128):
        f = p % 32
        pos = row if f < 16 else col
        j = f % 16
        ang = (pos * inv[j % 8]).astype(np.float32)
        cosT[p] = np.cos(ang)
        sinT[p] = np.sin(ang) * (-1.0 if j < 8 else 1.0)
    return cosT, sinT


def _na_plan(n):
    rows = n // GW
    kh = min(8, rows)
    uniq = {}
    plan = []
    for qp in range(rows // 2):
        r0a = min(max(2 * qp - kh // 2, 0), rows - kh)
        r0b = min(max(2 * qp + 1 - kh // 2, 0), rows - kh)
        lo, hi = r0a // 2, (r0b + kh - 1) // 2
        ent = []
        for kp in range(lo, hi + 1):
            key = []
            for khalf in range(2):
                for qhalf in range(2):
                    kr, qr = 2 * kp + khalf, 2 * qp + qhalf
                    r0 = min(max(qr - kh // 2, 0), rows - kh)
                    key.append(kr - qr + 7 if (r0 <= kr < r0 + kh) else -1)
            key = tuple(key)
            if key not in uniq:
                uniq[key] = len(uniq)
            ent.append((kp, uniq[key]))
        plan.append(ent)
    return plan, list(uniq.keys())


def _na_tables(rpb, keys):
    L = rpb.shape[0]
    c = np.arange(GW)
    c0 = np.clip(c - 8, 0, GW - 16)
    band = (c[None, :] >= c0[:, None]) & (c[None, :] < c0[:, None] + 16)
    dc = np.clip(c[None, :] - c[:, None], -15, 15) + 15
    out = np.full((L, 6, len(keys), 128, 128), NEG, np.float32)
    for ti, key in enumerate(keys):
        for khalf in range(2):
            for qhalf in range(2):
                a = key[khalf * 2 + qhalf]
                if a < 0:
                    continue
                blk = rpb[:, :, a, :][:, :, dc.T]
                blk = np.where(band.T[None, None], blk, np.float32(NEG))
                out[:, :, ti, khalf * 64:(khalf + 1) * 64, qhalf * 64:(qhalf + 1) * 64] = blk
    return out


class Buf:
    __slots__ = ("w", "r", "dsem", "keep", "name")

    def __init__(self, name="", keep=False):
        self.w = None
        self.r = {}
        self.dsem = None
        self.keep = keep
        self.name = name


class Sem:
    _n = 0

    def __init__(self, h):
        self.h = h
        Sem._n += 1
        self.idx = Sem._n
        self.cnt = 0


class Sched:
    ROLL = 30000

    def __init__(self, nc, es):
        self.nc = nc
        self.es = es
        self.eng = {"pe": nc.tensor, "act": nc.scalar, "dve": nc.vector, "pool": nc.gpsimd, "sp": nc.sync}
        self.sem = {}
        self.cnt = {}
        self.waited = {e: {} for e in self.eng}
        self.nsem = 0
        for e in self.eng:
            self.sem[e] = self.newsem(e)
            self.cnt[e] = 0
        self.dsems = []
        self.dfree = []
        self.scope = []

    def release_scope(self):
        for b in self.scope:
            if not b.keep and b.dsem is not None:
                self.dfree.append(b.dsem)
                b.dsem = None
        self.scope = [b for b in self.scope if b.keep and False]

    def newsem(self, name):
        self.nsem += 1
        return Sem(self.es.enter_context(self.nc.semaphore(f"{name}_{self.nsem}")))

    def _deps(self, r, w):
        deps = []
        for b in r:
            if b.w is not None:
                deps.append(b.w)
        for b in w:
            if b.w is not None:
                deps.append(b.w)
            deps.extend(b.r.values())
        return deps

    def _wait(self, e, deps):
        eng = self.eng[e]
        wd = self.waited[e]
        for (se, sem, val) in deps:
            if se == "pe" and e == "pe":
                continue
            if wd.get(sem.idx, 0) >= val:
                continue
            eng.wait_ge(sem.h, val)
            wd[sem.idx] = val

    def op(self, e, fn, r=(), w=()):
        self._wait(e, self._deps(r, w))
        ins = fn(self.eng[e])
        if self.cnt[e] >= self.ROLL:
            self.sem[e] = self.newsem(e)
            self.cnt[e] = 0
        self.cnt[e] += 1
        ins.then_inc(self.sem[e].h, 1)
        tok = (e, self.sem[e], self.cnt[e])
        for b in r:
            b.r[(e, self.sem[e].idx)] = tok
        for b in w:
            b.w = tok
            b.r = {}
        return tok

    def dma(self, out, in_, rd, wr, q="sp"):
        rd_dram = rd.name.startswith("D:")
        wr_dram = wr.name.startswith("D:")
        assert rd_dram != wr_dram
        own = rd if wr_dram else wr
        deps = self._deps([] if rd_dram else [rd], [] if wr_dram else [wr])
        self._wait(q, deps)
        ins = self.eng[q].dma_start(out=out, in_=in_)
        if own.dsem is None:
            while self.dfree and self.dfree[-1].cnt > 40000:
                self.dfree.pop()
            if self.dfree:
                own.dsem = self.dfree.pop()
            else:
                own.dsem = self.newsem("d")
                self.dsems.append(own.dsem)
            self.scope.append(own)
        own.dsem.cnt += 16
        assert own.dsem.cnt < 65000, "DMA semaphore overflow"
        ins.then_inc(own.dsem.h, 16)
        tok = ("dma", own.dsem, own.dsem.cnt)
        if wr_dram:
            rd.r[("dma", own.dsem.idx)] = tok
        else:
            wr.w = tok
            wr.r = {}
        return tok

    def barrier(self):
        toks = [(e, self.sem[e], self.cnt[e]) for e in self.eng if self.cnt[e] > 0]
        toks += [("dma", d, d.cnt) for d in self.dsems if d.cnt > 0]
        for e in self.eng:
            self._wait(e, toks)
        self.release_scope()


class Ctx:
    pass


def build(N, DEPTH, stop_after=None):
    NT = N // 128
    NTOK = N + CTX
    NTT = NTOK // 128
    NKC = NTT
    nc = bass.Bass("TRN2", target_bir_lowering=False)
    es = ExitStack()
    K = Sched(nc, es)

    def din(name, shape, dt=F32):
        return nc.dram_tensor(name, list(shape), dt, kind="ExternalInput").ap()

    def dscr(name, shape, dt):
        return nc.dram_tensor(name, list(shape), dt, kind="Internal").ap()

    x_in = din("x", [N, D]); ctx_in = din("ctx", [CTX, D]); cc_in = din("cc", [128, 8, 2])
    w_ada = din("w_ada", [DEPTH, D, 6 * D]); b_ada = din("b_ada", [DEPTH, 6 * D])
    w_a = din("w_a", [DEPTH, D, WA_COLS])
    lamv = din("lamv", [DEPTH, 2, 2, 32]); dnw = din("dnw", [DEPTH, 64])
    plan, tkeys = _na_plan(N)
    NTAB = len(tkeys)
    natab = din("natab", [DEPTH, 6, NTAB, 128, 128])
    qnw = din("qnw", [DEPTH, 384])
    w_uq = din("w_uq", [DEPTH, 256, 384]); w_uqr = din("w_uqr", [DEPTH, 256, 384])
    w_ukn = din("w_ukn", [DEPTH, 128, 256]); w_ukv = din("w_ukv", [DEPTH, 128, 256])
    w_out = din("w_out", [DEPTH, D, D]); w_up = din("w_up", [DEPTH, D, 2 * DFF]); w_down = din("w_down", [DEPTH, DFF, D])
    lnp = din("lnp", [DEPTH, 4, D])
    convp = din("convp", [DEPTH, 128, 44, 4])
    ident_in = din("ident", [128, 128]); cosT_in = din("cosT", [128, N]); sinT_in = din("sinT", [128, N])
    y_out = nc.dram_tensor("y", [N, D], F32, kind="ExternalOutput").ap()

    xc_d = dscr("xc_d", [CTX, D], F32)
    x1_d = dscr("x1_d", [NTOK, D], F32)
    ada_d = dscr("ada_d", [DEPTH, 2, 2, D], F32)
    qa_d = dscr("qa_d", [4, 96, NTOK], BF16); ka_d = dscr("ka_d", [4, 96, NTOK], BF16)
    qn_d = dscr("qn_d", [384, NTOK], BF16); kn_d = dscr("kn_d", [384, NTOK], BF16)
    qm_d = dscr("qm_d", [4, 96, NTOK], BF16); km_d = dscr("km_d", [4, 96, NTOK], BF16)
    va_d = dscr("va_d", [NTOK, 6, 65], BF16); vn_d = dscr("vn_d", [NTOK, 6, 65], BF16); vm_d = dscr("vm_d", [NTOK, 4, 65], BF16)
    mix_d = dscr("mix_d", [8, 128, NTOK], BF16)
    h2_d = dscr("h2_d", [8, 128, NTOK], BF16)
    at_d = dscr("at_d", [22, 128, NTOK], BF16)
    B = {n: Buf("D:" + n, keep=True) for n in ["y", "xc", "x1", "ada", "qa", "ka", "qn", "kn", "qm", "km", "va", "vn", "vm", "mix", "h2", "at", "IN"]}

    uid = [0]

    def sb(st, name, shape, dt):
        uid[0] += 1
        return st.enter_context(nc.sbuf_tensor(f"s{uid[0]}_{name}", list(shape), dt))

    def ps(st, name, shape, dt=F32):
        uid[0] += 1
        return st.enter_context(nc.psum_tensor(f"p{uid[0]}_{name}", list(shape), dt))

    ident = sb(es, "ident", [128, 128], F32); b_ident = Buf("ident", True)
    epsb = sb(es, "epsb", [128, 1], F32); b_eps = Buf()
    modfm = sb(es, "modfm", [128, DEPTH, 48, 2], F32); b_mod = Buf()
    lam_sb = sb(es, "lam_sb", [128, DEPTH, 2], F32); b_lam = Buf()
    K.dma(ident[:], ident_in, B["IN"], b_ident)
    K.op("pool", lambda e: e.memset(epsb[:], LN_EPS), w=[b_eps])

    def x_tile_ap(l, t, final=False):
        if t < NT:
            src = x_in if l == 0 else y_out
            return src[t * 128:(t + 1) * 128, :], (B["IN"] if l == 0 else B["y"])
        src = ctx_in if l == 0 else xc_d
        return src[(t - NT) * 128:(t - NT + 1) * 128, :], (B["IN"] if l == 0 else B["xc"])

    def x_out_ap(t):
        if t < NT:
            return y_out[t * 128:(t + 1) * 128, :], B["y"]
        return xc_d[(t - NT) * 128:(t - NT + 1) * 128, :], B["xc"]

    def load_weight(st, name, src, nchunk, ncols, dst, dbuf, colblk=2048):
        stg = [sb(st, f"{name}_s{i}", [128, colblk], F32) for i in range(2)]
        sbf = [Buf() for _ in range(2)]
        i = 0
        for c in range(nchunk):
            for c0 in range(0, ncols, colblk):
                cw = min(colblk, ncols - c0)
                K.dma(stg[i % 2][:, :cw], src[c * 128:(c + 1) * 128, c0:c0 + cw], B["IN"], sbf[i % 2])
                s_ = stg[i % 2]
                K.op("pool", lambda e, s_=s_, c=c, c0=c0, cw=cw: e.tensor_copy(out=dst[:, c, c0:c0 + cw], in_=s_[:, :cw]),
                     r=[sbf[i % 2]], w=[dbuf])
                i += 1

    def rstd_op(out_ap, in_ap, scale, rbufs, wbuf):
        K.op("act", lambda e: e.activation(out=out_ap, in_=in_ap, func=AF.Ln, bias=epsb[:], scale=scale), r=rbufs + [b_eps], w=[wbuf])
        K.op("act", lambda e: e.activation(out=out_ap, in_=out_ap, func=AF.Exp, scale=-0.5), r=[wbuf], w=[wbuf])

    with ExitStack() as st:
        csb = sb(st, "csb", [128, 8, 2], F32); b_c = Buf()
        K.dma(csb[:], cc_in, B["IN"], b_c)
        K.op("act", lambda e: e.activation(out=csb[:], in_=csb[:], func=AF.Silu), r=[b_c], w=[b_c])
        ones2 = sb(st, "ones2", [1, 2], F32); b_o2 = Buf()
        K.op("pool", lambda e: e.memset(ones2[:], 1.0), w=[b_o2])
        wst = [sb(st, f"wst{i}", [128, 8, 512], F32) for i in range(2)]; b_wst = [Buf(), Buf()]
        bst = [sb(st, f"bst{i}", [1, 512], F32) for i in range(2)]; b_bst = [Buf(), Buf()]
        pfm = [ps(st, f"pfm{i}", [128, 4, 2]) for i in range(2)]; b_pfm = [Buf(), Buf()]
        prow = [ps(st, f"prow{i}", [2, 512]) for i in range(2)]; b_prow = [Buf(), Buf()]
        rsb = [sb(st, f"rsb{i}", [2, 512], F32) for i in range(2)]; b_rsb = [Buf(), Buf()]
        it = 0
        for l in range(DEPTH):
            for cb in range(12):
                j = it % 2
                it += 1
                K.dma(wst[j][:], w_ada[l, :, cb * 512:(cb + 1) * 512].rearrange("(c p) n -> p c n", p=128), B["IN"], b_wst[j])
                K.dma(bst[j][:], b_ada[l:l + 1, cb * 512:(cb + 1) * 512], B["IN"], b_bst[j])
                for cc in range(4):
                    for dc in range(8):
                        K.op("pe", lambda e, j=j, cc=cc, dc=dc: e.matmul(pfm[j][:, cc, :], lhsT=wst[j][:, dc, cc * 128:(cc + 1) * 128],
                                                                          rhs=csb[:, dc, :], start=(dc == 0), stop=False),
                             r=[b_wst[j], b_c], w=[b_pfm[j]])
                    K.op("pe", lambda e, j=j, cc=cc: e.matmul(pfm[j][:, cc, :], lhsT=bst[j][:, cc * 128:(cc + 1) * 128], rhs=ones2[:],
                                                              start=False, stop=True), r=[b_bst[j], b_o2], w=[b_pfm[j]])
                is_scale = cb in (2, 3, 8, 9)
                K.op("dve", lambda e, j=j, l=l, cb=cb, a=(1.0 if is_scale else 0.0): e.tensor_scalar_add(
                    out=modfm[:, l, cb * 4:(cb + 1) * 4, :], in0=pfm[j][:], scalar1=a), r=[b_pfm[j]], w=[b_mod])
                if cb in (4, 5, 10, 11):
                    for dc in range(8):
                        K.op("pe", lambda e, j=j, dc=dc: e.matmul(prow[j][:], lhsT=csb[:, dc, :], rhs=wst[j][:, dc, :], start=(dc == 0), stop=False),
                             r=[b_wst[j], b_c], w=[b_prow[j]])
                    K.op("pe", lambda e, j=j: e.matmul(prow[j][:], lhsT=ones2[:], rhs=bst[j][:], start=False, stop=True),
                         r=[b_bst[j], b_o2], w=[b_prow[j]])
                    K.op("act", lambda e, j=j: e.copy(out=rsb[j][:], in_=prow[j][:]), r=[b_prow[j]], w=[b_rsb[j]])
                    g = 0 if cb < 6 else 1
                    half = cb % 2
                    K.dma(ada_d[l, :, g, half * 512:(half + 1) * 512], rsb[j][:], b_rsb[j], B["ada"])
        lv = sb(st, "lv", [128, DEPTH, 2, 2, 32], F32); b_lv = Buf()
        K.dma(lv[:].rearrange("p l a c b -> p (l a c b)"), lamv.rearrange("l a c b -> (l a c b)").partition_broadcast(128), B["IN"], b_lv)
        lt = sb(st, "lt", [128, DEPTH, 2, 32], F32); b_lt = Buf()
        ls = sb(st, "ls", [128, DEPTH, 2], F32); b_ls = Buf()
        K.op("dve", lambda e: e.tensor_tensor(out=lt[:], in0=lv[:, :, 0, :, :], in1=lv[:, :, 1, :, :], op=ALU.mult), r=[b_lv], w=[b_lt])
        K.op("dve", lambda e: e.tensor_reduce(out=ls[:], in_=lt[:], axis=mybir.AxisListType.X, op=ALU.add), r=[b_lt], w=[b_ls])
        K.op("act", lambda e: e.activation(out=ls[:], in_=ls[:], func=AF.Exp), r=[b_ls], w=[b_ls])
        for l in range(DEPTH):
            lam_init = 0.8 - 0.6 * math.exp(-0.3 * l)
            K.op("dve", lambda e, l=l, li=lam_init: e.scalar_tensor_tensor(out=lam_sb[:, l, 0:1], in0=ls[:, l, 1:2], scalar=-li, in1=ls[:, l, 0:1],
                                                                            op0=ALU.add, op1=ALU.subtract), r=[b_ls], w=[b_lam])
        K.barrier()

    def phaseA(l):
        with ExitStack() as st:
            wa = sb(st, "wa", [128, 8, WA_COLS], BF16); b_wa = Buf()
            load_weight(st, "wa", w_a[l], 8, WA_COLS, wa, b_wa, colblk=1760)
            wq = sb(st, "wq", [128, 2, 384], BF16); b_wq = Buf()
            wqr = sb(st, "wqr", [128, 2, 384], BF16); b_wqr = Buf()
            wkn = sb(st, "wkn", [128, 1, 256], BF16); b_wkn = Buf()
            wkv = sb(st, "wkv", [128, 1, 256], BF16); b_wkv = Buf()
            load_weight(st, "wq", w_uq[l], 2, 384, wq, b_wq, colblk=384)
            load_weight(st, "wqr", w_uqr[l], 2, 384, wqr, b_wqr, colblk=384)
            load_weight(st, "wkn", w_ukn[l], 1, 256, wkn, b_wkn, colblk=256)
            load_weight(st, "wkv", w_ukv[l], 1, 256, wkv, b_wkv, colblk=256)
            gq = sb(st, "gq", [128, 384], F32); b_gq = Buf()
            K.dma(gq[:], qnw[l].partition_broadcast(128), B["IN"], b_gq)
            xt = [sb(st, f"xt{i}", [128, D], F32) for i in range(2)]; b_xt = [Buf(), Buf()]
            xn = [sb(st, f"xn{i}", [128, D], F32) for i in range(2)]; b_xn = [Buf(), Buf()]
            stt = sb(st, "stt", [128, 2, 6], F32); b_stt = Buf()
            mv = sb(st, "mv", [128, 2], F32); b_mv = Buf()
            rs = sb(st, "rs", [128, 1], F32); b_rs = Buf()
            nb = sb(st, "nb", [128, 1], F32); b_nb = Buf()
            hT = [sb(st, f"hT{i}", [128, 8, 512], BF16) for i in range(2)]; b_hT = [Buf(), Buf()]
            cs = [sb(st, f"cs{i}", [128, 512], F32) for i in range(2)]; b_cs = [Buf(), Buf()]
            sn = [sb(st, f"sn{i}", [128, 512], F32) for i in range(2)]; b_sn = [Buf(), Buf()]
            t1 = [sb(st, f"t1{i}", [128, 512], F32) for i in range(2)]; b_t1 = [Buf(), Buf()]
            t2 = [sb(st, f"t2{i}", [128, 512], F32) for i in range(2)]; b_t2 = [Buf(), Buf()]
            ob = [sb(st, f"ob{i}", [128, 512], BF16) for i in range(3)]; b_ob = [Buf() for _ in range(3)]
            vb = [sb(st, f"vb{i}", [128, 6, 65], BF16) for i in range(4)]; b_vb = [Buf() for _ in range(4)]
            cqs = sb(st, "cqs", [128, 384], F32); b_cqs = Buf()
            cqn = sb(st, "cqn", [128, 384], F32); b_cqn = Buf()
            junk = sb(st, "junk", [128, 384], F32); b_junk = Buf()
            ssq = sb(st, "ssq", [128, 2], F32); b_ssq = Buf()
            cT = [sb(st, f"cT{i}", [128, 3, 512], BF16) for i in range(2)]; b_cT = [Buf(), Buf()]
            ptr = [ps(st, f"ptr{i}", [128, 128]) for i in range(2)]; b_ptr = [Buf(), Buf()]
            pfm = [ps(st, f"pA{i}", [128, 512]) for i in range(4)]; b_pfm = [Buf() for _ in range(4)]
            for i in range(4):
                K.op("pool", lambda e, i=i: e.memset(vb[i][:], 1.0), w=[b_vb[i]])
            cnt = {"tr": 0, "pf": 0, "ob": 0, "vb": 0}

            def nxt(k, n):
                v = cnt[k] % n
                cnt[k] += 1
                return v

            def a_load(t):
                src, sbuf_ = x_tile_ap(l, t)
                K.dma(xt[t % 2][:], src, sbuf_, b_xt[t % 2])

            groups = [(g * 4, 4) for g in range(NT // 4)]
            if NT % 4:
                groups.append((NT // 4 * 4, NT % 4))
            groups.append((NT, 2))
            for gi, (t0, ntl) in enumerate(groups):
                is_ctx = t0 >= NT
                ntok = ntl * 128
                tok0 = t0 * 128
                mi = 1 if is_ctx else 0
                hb = gi % 2
                if not is_ctx:
                    K.dma(cs[hb][:, :ntok], cosT_in[:, tok0:tok0 + ntok], B["IN"], b_cs[hb])
                    K.dma(sn[hb][:, :ntok], sinT_in[:, tok0:tok0 + ntok], B["IN"], b_sn[hb])
                for ti in range(ntl):
                    t = t0 + ti
                    xb = t % 2
                    if t == 0:
                        a_load(0)
                    if t + 1 < NTT:
                        a_load(t + 1)
                    K.op("dve", lambda e, xb=xb: e.bn_stats(out=stt[:, 0, :], in_=xt[xb][:, 0:512]), r=[b_xt[xb]], w=[b_stt])
                    K.op("dve", lambda e, xb=xb: e.bn_stats(out=stt[:, 1, :], in_=xt[xb][:, 512:1024]), r=[b_xt[xb]], w=[b_stt])
                    K.op("dve", lambda e: e.bn_aggr(out=mv[:], in_=stt[:].rearrange("p c s -> p (c s)")), r=[b_stt], w=[b_mv])
                    rstd_op(rs[:], mv[:, 1:2], 1.0, [b_mv], b_rs)
                    K.op("dve", lambda e: e.scalar_tensor_tensor(out=nb[:], in0=mv[:, 0:1], scalar=-1.0, in1=rs[:], op0=ALU.mult, op1=ALU.mult),
                         r=[b_mv, b_rs], w=[b_nb])
                    K.op("act", lambda e, xb=xb: e.activation(out=xn[xb][:], in_=xt[xb][:], func=AF.Identity, bias=nb[:], scale=rs[:]),
                         r=[b_xt[xb], b_nb, b_rs], w=[b_xn[xb]])
                    for dc in range(8):
                        p = nxt("tr", 2)
                        K.op("pe", lambda e, p=p, xb=xb, dc=dc: e.transpose(out=ptr[p][:], in_=xn[xb][:, dc * 128:(dc + 1) * 128], identity=ident[:]),
                             r=[b_xn[xb], b_ident], w=[b_ptr[p]])
                        K.op("dve", lambda e, p=p, dc=dc, ti=ti: e.tensor_scalar(out=hT[hb][:, dc, ti * 128:(ti + 1) * 128], in0=ptr[p][:],
                                                                                 scalar1=modfm[:, l, 8 + dc, mi:mi + 1], scalar2=modfm[:, l, dc, mi:mi + 1],
                                                                                 op0=ALU.mult, op1=ALU.add),
                             r=[b_ptr[p], b_mod], w=[b_hT[hb]])

                def fm_mm(col0, m, pbuf):
                    for dc in range(8):
                        K.op("pe", lambda e, dc=dc: e.matmul(pfm[pbuf][:m, :ntok], lhsT=wa[:, dc, col0:col0 + m], rhs=hT[hb][:, dc, :ntok],
                                                              start=(dc == 0), stop=(dc == 7)), r=[b_wa, b_hT[hb]], w=[b_pfm[pbuf]])

                def rope_out(col0, colr, m, dst_aps, dbuf):
                    pa = nxt("pf", 4)
                    fm_mm(col0, m, pa)
                    o = nxt("ob", 3)
                    if is_ctx or colr is None:
                        K.op("act", lambda e: e.copy(out=ob[o][:m, :ntok], in_=pfm[pa][:m, :ntok]), r=[b_pfm[pa]], w=[b_ob[o]])
                    else:
                        pb = nxt("pf", 4)
                        fm_mm(colr, m, pb)
                        K.op("dve", lambda e: e.tensor_tensor(out=t1[hb][:m, :ntok], in0=pfm[pa][:m, :ntok], in1=cs[hb][:m, :ntok], op=ALU.mult),
                             r=[b_pfm[pa], b_cs[hb]], w=[b_t1[hb]])
                        K.op("dve", lambda e: e.tensor_tensor(out=t2[hb][:m, :ntok], in0=pfm[pb][:m, :ntok], in1=sn[hb][:m, :ntok], op=ALU.mult),
                             r=[b_pfm[pb], b_sn[hb]], w=[b_t2[hb]])
                        K.op("pool", lambda e: e.tensor_tensor(out=ob[o][:m, :ntok], in0=t1[hb][:m, :ntok], in1=t2[hb][:m, :ntok], op=ALU.add),
                             r=[b_t1[hb], b_t2[hb]], w=[b_ob[o]])
                    for d_ in dst_aps:
                        K.dma(d_, ob[o][:m, :ntok], b_ob[o], dbuf)

                for c in range(4):
                    rope_out(C_AQ + c * 96, C_AQR + c * 96, 96, [qa_d[c, :, tok0:tok0 + ntok]], B["qa"])
                    rope_out(C_AK + c * 96, C_AKR + c * 96, 96, [ka_d[c, :, tok0:tok0 + ntok]], B["ka"])
                for c in range(3):
                    rope_out(C_NQ + c * 128, None, 128, [qn_d[c * 128:(c + 1) * 128, tok0:tok0 + ntok]], B["qn"])
                    rope_out(C_NK + c * 128, None, 128, [kn_d[c * 128:(c + 1) * 128, tok0:tok0 + ntok]], B["kn"])
                rope_out(C_KR, C_KRR, 32, [km_d[h, 64:96, tok0:tok0 + ntok] for h in range(4)], B["km"])

                for ti in range(ntl):
                    t = t0 + ti
                    for (col0, dst, dbuf) in ((C_AV, va_d, B["va"]), (C_NV, vn_d, B["vn"])):
                        pa = nxt("pf", 4)
                        for dc in range(8):
                            K.op("pe", lambda e, dc=dc, pa=pa, col0=col0: e.matmul(pfm[pa][:, :384], lhsT=hT[hb][:, dc, ti * 128:(ti + 1) * 128],
                                                                                   rhs=wa[:, dc, col0:col0 + 384], start=(dc == 0), stop=(dc == 7)),
                                 r=[b_wa, b_hT[hb]], w=[b_pfm[pa]])
                        v = nxt("vb", 4)
                        K.op("act", lambda e, pa=pa, v=v: e.copy(out=vb[v][:, :, 0:64], in_=pfm[pa][:, :384].rearrange("p (h d) -> p h d", d=64)),
                             r=[b_pfm[pa]], w=[b_vb[v]])
                        K.dma(dst[t * 128:(t + 1) * 128, :, :], vb[v][:], b_vb[v], dbuf)
                    pa = nxt("pf", 4)
                    for dc in range(8):
                        K.op("pe", lambda e, dc=dc, pa=pa: e.matmul(pfm[pa][:, :384], lhsT=hT[hb][:, dc, ti * 128:(ti + 1) * 128],
                                                                    rhs=wa[:, dc, C_CQ:C_CQ + 384], start=(dc == 0), stop=(dc == 7)),
                             r=[b_wa, b_hT[hb]], w=[b_pfm[pa]])
                    K.op("act", lambda e, pa=pa: e.copy(out=cqs[:], in_=pfm[pa][:, :384]), r=[b_pfm[pa]], w=[b_cqs])
                    K.op("dve", lambda e: e.scalar_tensor_tensor(out=junk[:, 0:256], in0=cqs[:, 0:256], scalar=1.0, in1=cqs[:, 0:256], op0=ALU.mult, op1=ALU.mult,
                                                                 accum_out=ssq[:, 0:1]), r=[b_cqs], w=[b_junk, b_ssq])
                    K.op("dve", lambda e: e.scalar_tensor_tensor(out=junk[:, 256:384], in0=cqs[:, 256:384], scalar=1.0, in1=cqs[:, 256:384], op0=ALU.mult, op1=ALU.mult,
                                                                 accum_out=ssq[:, 1:2]), r=[b_cqs], w=[b_junk, b_ssq])
                    rstd_op(ssq[:, 0:1], ssq[:, 0:1], 1.0 / 256, [b_ssq], b_ssq)
                    rstd_op(ssq[:, 1:2], ssq[:, 1:2], 1.0 / 128, [b_ssq], b_ssq)
                    K.op("dve", lambda e: e.scalar_tensor_tensor(out=cqn[:, 0:256], in0=cqs[:, 0:256], scalar=ssq[:, 0:1], in1=gq[:, 0:256], op0=ALU.mult, op1=ALU.mult),
                         r=[b_cqs, b_ssq, b_gq], w=[b_cqn])
                    K.op("dve", lambda e: e.scalar_tensor_tensor(out=cqn[:, 256:384], in0=cqs[:, 256:384], scalar=ssq[:, 1:2], in1=gq[:, 256:384], op0=ALU.mult, op1=ALU.mult),
                         r=[b_cqs, b_ssq, b_gq], w=[b_cqn])
                    for c in range(3):
                        p = nxt("tr", 2)
                        K.op("pe", lambda e, p=p, c=c: e.transpose(out=ptr[p][:], in_=cqn[:, c * 128:(c + 1) * 128], identity=ident[:]),
                             r=[b_cqn, b_ident], w=[b_ptr[p]])
                        K.op("act", lambda e, p=p, c=c: e.copy(out=cT[hb][:, c, ti * 128:(ti + 1) * 128], in_=ptr[p][:]), r=[b_ptr[p]], w=[b_cT[hb]])
                for h in range(4):
                    pa = nxt("pf", 4)
                    for rc in range(2):
                        K.op("pe", lambda e, rc=rc, pa=pa: e.matmul(pfm[pa][:96, :ntok], lhsT=wq[:, rc, h * 96:(h + 1) * 96], rhs=cT[hb][:, rc, :ntok],
                                                                    start=(rc == 0), stop=(rc == 1)), r=[b_wq, b_cT[hb]], w=[b_pfm[pa]])
                    o = nxt("ob", 3)
                    if is_ctx:
                        K.op("act", lambda e, pa=pa, o=o: e.copy(out=ob[o][:96, :ntok], in_=pfm[pa][:96, :ntok]), r=[b_pfm[pa]], w=[b_ob[o]])
                    else:
                        pb = nxt("pf", 4)
                        for rc in range(2):
                            K.op("pe", lambda e, rc=rc, pb=pb: e.matmul(pfm[pb][:96, :ntok], lhsT=wqr[:, rc, h * 96:(h + 1) * 96], rhs=cT[hb][:, rc, :ntok],
                                                                        start=(rc == 0), stop=(rc == 1)), r=[b_wqr, b_cT[hb]], w=[b_pfm[pb]])
                        K.op("act", lambda e, pa=pa, o=o: e.copy(out=ob[o][0:64, :ntok], in_=pfm[pa][0:64, :ntok]), r=[b_pfm[pa]], w=[b_ob[o]])
                        K.op("dve", lambda e, pa=pa: e.tensor_tensor(out=t1[hb][64:96, :ntok], in0=pfm[pa][64:96, :ntok], in1=cs[hb][64:96, :ntok], op=ALU.mult),
                             r=[b_pfm[pa], b_cs[hb]], w=[b_t1[hb]])
                        K.op("dve", lambda e, pb=pb: e.tensor_tensor(out=t2[hb][64:96, :ntok], in0=pfm[pb][64:96, :ntok], in1=sn[hb][64:96, :ntok], op=ALU.mult),
                             r=[b_pfm[pb], b_sn[hb]], w=[b_t2[hb]])
                        K.op("pool", lambda e, o=o: e.tensor_tensor(out=ob[o][64:96, :ntok], in0=t1[hb][64:96, :ntok], in1=t2[hb][64:96, :ntok], op=ALU.add),
                             r=[b_t1[hb], b_t2[hb]], w=[b_ob[o]])
                    K.dma(qm_d[h, :, tok0:tok0 + ntok], ob[o][:96, :ntok], b_ob[o], B["qm"])
                    pa = nxt("pf", 4)
                    K.op("pe", lambda e, pa=pa: e.matmul(pfm[pa][:64, :ntok], lhsT=wkn[:, 0, h * 64:(h + 1) * 64], rhs=cT[hb][:, 2, :ntok], start=True, stop=True),
                         r=[b_wkn, b_cT[hb]], w=[b_pfm[pa]])
                    o = nxt("ob", 3)
                    K.op("act", lambda e, pa=pa, o=o: e.copy(out=ob[o][:64, :ntok], in_=pfm[pa][:64, :ntok]), r=[b_pfm[pa]], w=[b_ob[o]])
                    K.dma(km_d[h, 0:64, tok0:tok0 + ntok], ob[o][:64, :ntok], b_ob[o], B["km"])
                for ti in range(ntl):
                    t = t0 + ti
                    pa = nxt("pf", 4)
                    K.op("pe", lambda e, pa=pa: e.matmul(pfm[pa][:, :256], lhsT=cT[hb][:, 2, ti * 128:(ti + 1) * 128], rhs=wkv[:, 0, :], start=True, stop=True),
                         r=[b_wkv, b_cT[hb]], w=[b_pfm[pa]])
                    v = nxt("vb", 4)
                    K.op("act", lambda e, pa=pa, v=v: e.copy(out=vb[v][:, 0:4, 0:64], in_=pfm[pa][:, :256].rearrange("p (h d) -> p h d", d=64)),
                         r=[b_pfm[pa]], w=[b_vb[v]])
                    K.dma(vm_d[t * 128:(t + 1) * 128, :, :], vb[v][:, 0:4, :], b_vb[v], B["vm"])
            K.barrier()


    def phaseB(l, do_ctx):
        lam_init = 0.8 - 0.6 * math.exp(-0.3 * l)
        for kind in DBG_KINDS:
            with ExitStack() as st:
                if kind == "a":
                    NH, q_d, k_d, v_d, bq, bk, bv, scale = 6, qa_d, ka_d, va_d, B["qa"], B["ka"], B["va"], DA_SCALE
                elif kind == "n":
                    NH, q_d, k_d, v_d, bq, bk, bv, scale = 6, qn_d, kn_d, vn_d, B["qn"], B["kn"], B["vn"], NA_SCALE
                else:
                    NH, q_d, k_d, v_d, bq, bk, bv, scale = 4, qm_d, km_d, vm_d, B["qm"], B["km"], B["vm"], MLA_SCALE
                NCH = 3 if kind == "n" else 4
                kT = sb(st, "kT", [128, NCH, NTOK], BF16); b_kT = Buf()
                vv = sb(st, "vv", [128, NKC, NH, 65], BF16); b_vv = Buf()
                if kind != "n":
                    for h in range(4):
                        K.dma(kT[:96, h, :], k_d[h], bk, b_kT)
                else:
                    for c in range(3):
                        K.dma(kT[:, c, :], k_d[c * 128:(c + 1) * 128, :], bk, b_kT)
                for c0 in range(0, NKC, 8):
                    c1 = min(NKC, c0 + 8)
                    K.dma(vv[:, c0:c1, :, :], v_d[c0 * 128:c1 * 128, :, :].rearrange("(c p) h d -> p c h d", p=128), bv, b_vv)
                NSLOT = {"a": 12, "n": 6, "m": 4}[kind]
                qT = [sb(st, f"qT{i}", [128, NSLOT, 512], BF16) for i in range(2)]; b_qT = [Buf(), Buf()]
                if kind != "m":
                    for i in range(2):
                        K.op("pool", lambda e, i=i: e.memset(qT[i][:], 0.0), w=[b_qT[i]])
                pT = [sb(st, f"pT{i}", [128, 1024], BF16) for i in range(4)]; b_pT = [[Buf(), Buf()] for _ in range(4)]
                osb = [[sb(st, f"osb{p_}{i}", [65, 512], F32) for i in range(2)] for p_ in range(2)]; b_osb = [[Buf(), Buf()], [Buf(), Buf()]]
                mixt = [sb(st, f"mixt{i}", [128, 4, 128], F32) for i in range(2)]; b_mixt = [Buf(), Buf()]
                pending = []
                att_no = [0]

                def defer(fn):
                    pending.append([1, fn])

                def tick():
                    for it in pending:
                        it[0] -= 1
                    while pending and pending[0][0] <= 0:
                        pending.pop(0)[1]()
                mixo = [sb(st, f"mixo{i}", [128, 512], BF16) for i in range(2)]; b_mixo = [Buf(), Buf()]
                rc_ = sb(st, "rc_", [128, 2], F32); b_rc = Buf()
                o1 = sb(st, "o1", [128, 64], F32); b_o1 = Buf()
                o2 = sb(st, "o2", [128, 64], F32); b_o2 = Buf()
                jk = sb(st, "jk", [128, 64], F32); b_jk = Buf()
                s2 = sb(st, "s2", [128, 1], F32); b_s2 = Buf()
                sc2 = [ps(st, f"sc{i}", [128, 1024]) for i in range(2)]
                sc = [[sc2[i][:, s_ * 512:(s_ + 1) * 512] for s_ in range(2)] for i in range(2)]; b_sc = [[Buf(), Buf()], [Buf(), Buf()]]
                acc = [ps(st, f"acc{i}", [65, 512]) for i in range(2)]; b_acc = [Buf(), Buf()]
                pmisc = ps(st, "pmisc", [128, 512])
                _bp = Buf()
                ptk = [pmisc[:, 0:65], pmisc[:, 128:193]]; b_ptk = [_bp, _bp]
                ptm_t = ps(st, "ptm", [128, 128]); ptm = ptm_t[:]; b_ptm = Buf()
                cnt = {"sc": 0, "pT": 0, "mixo": 0, "q": 0}

                def nxt(k, n):
                    v = cnt[k] % n
                    cnt[k] += 1
                    return v

                if kind == "a":
                    dn = sb(st, "dn", [128, 64], F32); b_dn = Buf()
                    K.dma(dn[:], dnw[l].partition_broadcast(128), B["IN"], b_dn)
                    K.op("dve", lambda e: e.tensor_scalar_mul(out=dn[:], in0=dn[:], scalar1=1.0 - lam_init), r=[b_dn], w=[b_dn])
                if kind == "n":
                    tab = sb(st, "tab", [128, 6, NTAB, 128], F32); b_tab = Buf()
                    for h in range(6):
                        K.dma(tab[:, h, :, :], natab[l, h].rearrange("t k q -> k t q"), B["IN"], b_tab)
                    sbias = [sb(st, f"sbias{i}", [128, 256], F32) for i in range(2)]; b_sbias = [Buf(), Buf()]

                def attend(qb, qoff, nq, streams, kcs, tmap=None, hpair=0):
                    nk = len(kcs)
                    pend = None
                    one_bank = 2 * nq <= 512
                    for i, kc in enumerate(kcs):
                        sbi = nxt("sc", 2)
                        for s, (ch, nr, slot, vh) in enumerate(streams):
                            dst = sc[sbi][0][:, s * nq:(s + 1) * nq] if one_bank else sc[sbi][s][:, :nq]
                            K.op("pe", lambda e, s=s, ch=ch, nr=nr, slot=slot, kc=kc, sbi=sbi: e.matmul(
                                dst, lhsT=kT[:nr, ch, kc * 128:(kc + 1) * 128],
                                rhs=qT[qb][:nr, slot, qoff:qoff + nq], start=True, stop=True), r=[b_kT, b_qT[qb]],
                                w=[b_sc[sbi][0 if one_bank else s]])
                        pi = nxt("pT", 4)
                        if tmap is not None and kc in tmap:
                            assert one_bank
                            ti_ = tmap[kc]
                            K.op("dve", lambda e, sbi=sbi, ti_=ti_: e.scalar_tensor_tensor(
                                out=sbias[sbi][:, :2 * nq].rearrange("p (s q) -> p s q", s=2), in0=sc[sbi][0][:, :2 * nq].rearrange("p (s q) -> p s q", s=2),
                                scalar=scale, in1=tab[:, 2 * hpair:2 * hpair + 2, ti_, :], op0=ALU.mult, op1=ALU.add),
                                r=[b_sc[sbi][0], b_tab], w=[b_sbias[sbi]])
                            K.op("act", lambda e, sbi=sbi, pi=pi: e.activation(out=pT[pi][:, :2 * nq], in_=sbias[sbi][:, :2 * nq], func=AF.Exp),
                                 r=[b_sbias[sbi]], w=b_pT[pi])
                        elif one_bank:
                            K.op("act", lambda e, sbi=sbi, pi=pi: e.activation(out=pT[pi][:, :2 * nq], in_=sc[sbi][0][:, :2 * nq], func=AF.Exp, scale=scale),
                                 r=[b_sc[sbi][0]], w=b_pT[pi])
                        elif nq == 512 and MERGE_EXP:
                            K.op("act", lambda e, sbi=sbi, pi=pi: e.activation(out=pT[pi][:, :1024], in_=sc2[sbi][:, :1024], func=AF.Exp, scale=scale),
                                 r=b_sc[sbi], w=b_pT[pi])
                        else:
                            for s in range(2):
                                K.op("act", lambda e, sbi=sbi, pi=pi, s=s: e.activation(out=pT[pi][:, s * nq:(s + 1) * nq], in_=sc[sbi][s][:, :nq], func=AF.Exp, scale=scale),
                                     r=[b_sc[sbi][s]], w=[b_pT[pi][s]])
                        if pend is not None:
                            pend()

                        def mk(i=i, kc=kc, pi=pi):
                            for s, (ch, nr, slot, vh) in enumerate(streams):
                                K.op("pe", lambda e, s=s, vh=vh: e.matmul(acc[s][:, :nq], lhsT=vv[:, kc, vh, :], rhs=pT[pi][:, s * nq:(s + 1) * nq],
                                                                          start=(i == 0), stop=(i == nk - 1)), r=[b_vv, b_pT[pi][s]], w=[b_acc[s]])
                        pend = mk
                    pend()
                    par = att_no[0] % 2
                    att_no[0] += 1
                    for s in range(2):
                        K.op("act", lambda e, s=s: e.copy(out=osb[par][s][:, :nq], in_=acc[s][:, :nq]), r=[b_acc[s]], w=[b_osb[par][s]])
                    tick()
                    return par

                def fin(par, nq, slot0, diff_j, mo):
                    for qi in range(nq // 128):
                        slot = slot0 + qi
                        for s in range(2):
                            K.op("pe", lambda e, s=s: e.transpose(out=ptk[s], in_=osb[par][s][:65, qi * 128:(qi + 1) * 128], identity=ident[:65, :65]),
                                 r=[b_osb[par][s], b_ident], w=[b_ptk[s]])
                        for s in range(2):
                            K.op("dve", lambda e, s=s: e.reciprocal(out=rc_[:, s:s + 1], in_=ptk[s][:, 64:65]), r=[b_ptk[s]], w=[b_rc])
                        if diff_j is None:
                            for s in range(2):
                                K.op("dve", lambda e, s=s: e.tensor_scalar(out=mixt[mo][:, slot, s * 64:(s + 1) * 64], in0=ptk[s][:, 0:64], scalar1=rc_[:, s:s + 1],
                                                                            scalar2=None, op0=ALU.mult), r=[b_ptk[s], b_rc], w=[b_mixt[mo]])
                        else:
                            j = diff_j
                            K.op("dve", lambda e: e.tensor_scalar(out=o1[:], in0=ptk[0][:, 0:64], scalar1=rc_[:, 0:1], scalar2=None, op0=ALU.mult),
                                 r=[b_ptk[0], b_rc], w=[b_o1])
                            K.op("dve", lambda e: e.tensor_scalar(out=o2[:], in0=ptk[1][:, 0:64], scalar1=rc_[:, 1:2], scalar2=None, op0=ALU.mult),
                                 r=[b_ptk[1], b_rc], w=[b_o2])
                            K.op("dve", lambda e: e.scalar_tensor_tensor(out=o1[:], in0=o2[:], scalar=lam_sb[:, l, 0:1], in1=o1[:], op0=ALU.mult, op1=ALU.add),
                                 r=[b_o2, b_lam, b_o1], w=[b_o1])
                            K.op("pool", lambda e: e.memset(s2[:], 0.0), w=[b_s2])
                            K.op("dve", lambda e: e.scalar_tensor_tensor(out=jk[:], in0=o1[:], scalar=1.0, in1=o1[:], op0=ALU.mult, op1=ALU.mult, accum_out=s2[:]),
                                 r=[b_o1], w=[b_jk, b_s2])
                            rstd_op(s2[:], s2[:], 1.0 / 64, [b_s2], b_s2)
                            K.op("dve", lambda e: e.scalar_tensor_tensor(out=mixt[mo][:, slot, j * 64:(j + 1) * 64], in0=o1[:], scalar=s2[:, 0:1], in1=dn[:],
                                                                         op0=ALU.mult, op1=ALU.mult), r=[b_o1, b_s2, b_dn], w=[b_mixt[mo]])

                def flush(slot, mo, col):
                    K.op("pe", lambda e: e.transpose(out=ptm, in_=mixt[mo][:, slot, :], identity=ident[:]), r=[b_mixt[mo], b_ident], w=[b_ptm])
                    K.op("act", lambda e: e.copy(out=mixo[mo][:, col:col + 128], in_=ptm), r=[b_ptm], w=[b_mixo[mo]])

                def unit_done(par, nq, slot0, diff_j, mo, flush_slots, dma_args):
                    def stage1():
                        fin(par, nq, slot0, diff_j, mo)
                        if flush_slots:
                            def stage2():
                                for (slot, col) in flush_slots:
                                    flush(slot, mo, col)
                                if dma_args is not None:
                                    cg_, q0_, nq_ = dma_args
                                    K.dma(mix_d[cg_, :, q0_:q0_ + nq_], mixo[mo][:, :nq_], b_mixo[mo], B["mix"])
                            defer(stage2)
                    defer(stage1)

                qblocks = [(g * 512, min(512, N - g * 512)) for g in range((N + 511) // 512)]
                if do_ctx:
                    qblocks.append((N, CTX))
                def q_load(bi):
                    q0, nq = qblocks[bi]
                    qb = bi % 2
                    if kind == "m":
                        for h in range(4):
                            K.dma(qT[qb][:96, h, :nq], q_d[h, :, q0:q0 + nq], bq, b_qT[qb])
                    elif kind == "a":
                        for g in range(12):
                            r0 = (g % 3) * 32
                            K.dma(qT[qb][r0:r0 + 32, g, :nq], q_d[g // 3, r0:r0 + 32, q0:q0 + nq], bq, b_qT[qb])
                    else:
                        for h in range(6):
                            r0 = (h % 2) * 64
                            K.dma(qT[qb][r0:r0 + 64, h, :nq], q_d[(h // 2) * 128 + r0:(h // 2) * 128 + r0 + 64, q0:q0 + nq], bq, b_qT[qb])

                q_load(0)
                for bi, (q0, nq) in enumerate(qblocks):
                    is_ctx = q0 >= N
                    qb = bi % 2
                    if bi + 1 < len(qblocks):
                        q_load(bi + 1)
                    allk = list(range(NT, NT + 2)) if is_ctx else list(range(NKC))
                    for c in range(3 if kind != "m" else 2):
                        mo = nxt("mixo", 2)
                        cg = c + (0 if kind == "a" else 3 if kind == "n" else 6)
                        allslots = [(qi, qi * 128) for qi in range(nq // 128)]
                        if kind == "a":
                            for j in range(2):
                                hh_ = 2 * c + j
                                par = attend(qb, 0, nq, [((2 * hh_) // 3, 96, 2 * hh_, hh_), ((2 * hh_ + 1) // 3, 96, 2 * hh_ + 1, hh_)], allk)
                                unit_done(par, nq, 0, j, mo, allslots if j == 1 else None, (cg, q0, nq))
                        elif kind == "m":
                            par = attend(qb, 0, nq, [(2 * c, 96, 2 * c, 2 * c), (2 * c + 1, 96, 2 * c + 1, 2 * c + 1)], allk)
                            unit_done(par, nq, 0, None, mo, allslots, (cg, q0, nq))
                        else:
                            stn = [(c, 128, 2 * c, 2 * c), (c, 128, 2 * c + 1, 2 * c + 1)]
                            if is_ctx:
                                par = attend(qb, 0, nq, stn, allk)
                                unit_done(par, nq, 0, None, mo, allslots, (cg, q0, nq))
                            else:
                                for qi in range(nq // 128):
                                    qp = (q0 + qi * 128) // 128
                                    tmap = {kp: ti_ for (kp, ti_) in plan[qp]}
                                    kcs = [kp for (kp, ti_) in plan[qp]] + [NT, NT + 1]
                                    par = attend(qb, qi * 128, 128, stn, kcs, tmap, c)
                                    unit_done(par, 128, qi, None, mo, [(qi, qi * 128)], (cg, q0, nq) if qi == nq // 128 - 1 else None)
                while pending:
                    tick()
                K.barrier()

    def ln_norm(src, b_src, dst, b_dst, tmp):
        stt, mv, rs, nb, b_stt, b_mv, b_rs, b_nb = tmp
        K.op("dve", lambda e: e.bn_stats(out=stt[:, 0, :], in_=src[:, 0:512]), r=[b_src], w=[b_stt])
        K.op("dve", lambda e: e.bn_stats(out=stt[:, 1, :], in_=src[:, 512:1024]), r=[b_src], w=[b_stt])
        K.op("dve", lambda e: e.bn_aggr(out=mv[:], in_=stt[:].rearrange("p c s -> p (c s)")), r=[b_stt], w=[b_mv])
        rstd_op(rs[:], mv[:, 1:2], 1.0, [b_mv], b_rs)
        K.op("dve", lambda e: e.scalar_tensor_tensor(out=nb[:], in0=mv[:, 0:1], scalar=-1.0, in1=rs[:], op0=ALU.mult, op1=ALU.mult),
             r=[b_mv, b_rs], w=[b_nb])
        K.op("act", lambda e: e.activation(out=dst[:], in_=src[:], func=AF.Identity, bias=nb[:], scale=rs[:]), r=[b_src, b_nb, b_rs], w=[b_dst])

    def mk_tmp(st, pfx):
        return (sb(st, pfx + "stt", [128, 2, 6], F32), sb(st, pfx + "mv", [128, 2], F32), sb(st, pfx + "rs", [128, 1], F32), sb(st, pfx + "nb", [128, 1], F32),
                Buf(), Buf(), Buf(), Buf())

    def postnorm(py, b_py, xres, b_xres, gate, b_gate, g_bc, b_bc, b_gb, z, b_z, xn, b_xn, tmp):
        for hh in range(2):
            K.op("dve", lambda e, hh=hh: e.tensor_tensor(out=z[:, hh * 512:(hh + 1) * 512], in0=py[hh][:], in1=gate[:, hh * 512:(hh + 1) * 512], op=ALU.mult),
                 r=[b_py[hh], b_gate], w=[b_z])
        K.op("dve", lambda e: e.scalar_tensor_tensor(out=z[:], in0=xres[:], scalar=ALPHA, in1=z[:], op0=ALU.mult, op1=ALU.add), r=[b_xres, b_z], w=[b_z])
        ln_norm(z, b_z, xn, b_xn, tmp)
        K.op("dve", lambda e: e.tensor_tensor(out=xn[:], in0=xn[:], in1=g_bc[:], op=ALU.mult), r=[b_xn, b_gb[0]], w=[b_xn])
        K.op("pool", lambda e: e.tensor_tensor(out=z[:], in0=xn[:], in1=b_bc[:], op=ALU.add), r=[b_xn, b_gb[1]], w=[b_z])

    def bc_load(st, name, src_row, dbuf_src):
        t = sb(st, name, [128, D], F32)
        b = Buf()
        K.dma(t[:], src_row.partition_broadcast(128), dbuf_src, b)
        return t, b

    def tok_tiles(do_ctx):
        return list(range(NT)) + ([NT, NT + 1] if do_ctx else [])

    def phaseC1(l, do_ctx):
        with ExitStack() as st:
            wo = sb(st, "wo", [128, 8, D], BF16); b_wo = Buf()
            load_weight(st, "wo", w_out[l], 8, D, wo, b_wo, colblk=1024)
            gate = [bc_load(st, f"gate{m}", ada_d[l, m, 0, :], B["ada"]) for m in range(2)]
            g_bc, b_g = bc_load(st, "g_bc", lnp[l, 0, :], B["IN"])
            b_bc, b_b = bc_load(st, "b_bc", lnp[l, 1, :], B["IN"])
            b_gb = (b_g, b_b)
            mT = [sb(st, f"mT{i}", [128, 8, 128], BF16) for i in range(2)]; b_mT = [Buf(), Buf()]
            xr = [sb(st, f"xr{i}", [128, D], F32) for i in range(2)]; b_xr = [Buf(), Buf()]
            z = [sb(st, f"z{i}", [128, D], F32) for i in range(2)]; b_z = [Buf(), Buf()]
            xn = sb(st, "xn", [128, D], F32); b_xn = Buf()
            xn2 = sb(st, "xn2", [128, D], F32); b_xn2 = Buf()
            h2 = [sb(st, f"h2{i}", [128, 8, 128], BF16) for i in range(2)]; b_h2 = [Buf(), Buf()]
            tmp = mk_tmp(st, "c1")
            py = [ps(st, f"py{i}", [128, 512]) for i in range(2)]; b_py = [Buf(), Buf()]
            ptr = [ps(st, f"ptr{i}", [128, 128]) for i in range(2)]; b_ptr = [Buf(), Buf()]
            ntr = 0
            mi_of = lambda t: 1 if t >= NT else 0

            def c1_load(t):
                i = t % 2
                K.dma(mT[i][:], mix_d[:, :, t * 128:(t + 1) * 128].rearrange("c p t -> p c t"), B["mix"], b_mT[i])
                src, sbuf_ = x_tile_ap(l, t)
                K.dma(xr[i][:], src, sbuf_, b_xr[i])

            tl = tok_tiles(do_ctx)
            c1_load(tl[0])
            for ti_, t in enumerate(tl):
                i = t % 2
                mi = mi_of(t)
                if ti_ + 1 < len(tl):
                    c1_load(tl[ti_ + 1])
                for hh in range(2):
                    for fc in range(8):
                        K.op("pe", lambda e, hh=hh, fc=fc: e.matmul(py[hh][:], lhsT=mT[i][:, fc, :], rhs=wo[:, fc, hh * 512:(hh + 1) * 512],
                                                                    start=(fc == 0), stop=(fc == 7)), r=[b_mT[i], b_wo], w=[b_py[hh]])
                postnorm(py, b_py, xr[i], b_xr[i], gate[mi][0], gate[mi][1], g_bc, b_bc, b_gb, z[i], b_z[i], xn, b_xn, tmp)
                K.dma(x1_d[t * 128:(t + 1) * 128, :], z[i][:], b_z[i], B["x1"])
                ln_norm(z[i], b_z[i], xn2, b_xn2, tmp)
                for dc in range(8):
                    p = ntr % 2
                    ntr += 1
                    K.op("pe", lambda e, p=p, dc=dc: e.transpose(out=ptr[p][:], in_=xn2[:, dc * 128:(dc + 1) * 128], identity=ident[:]),
                         r=[b_xn2, b_ident], w=[b_ptr[p]])
                    K.op("dve", lambda e, p=p, dc=dc: e.tensor_scalar(out=h2[i][:, dc, :], in0=ptr[p][:], scalar1=modfm[:, l, 32 + dc, mi:mi + 1],
                                                                      scalar2=modfm[:, l, 24 + dc, mi:mi + 1], op0=ALU.mult, op1=ALU.add),
                         r=[b_ptr[p], b_mod], w=[b_h2[i]])
                K.dma(h2_d[:, :, t * 128:(t + 1) * 128].rearrange("c p t -> p c t"), h2[i][:], b_h2[i], B["h2"])
            K.barrier()

    def phaseC2(l, do_ctx):
        with ExitStack() as st:
            wu = sb(st, "wu", [128, 8, 2 * DFF], BF16); b_wu = Buf()
            load_weight(st, "wu", w_up[l], 8, 2 * DFF, wu, b_wu, colblk=1408)
            cp = sb(st, "cp", [128, 44, 4], F32); b_cp = Buf()
            K.dma(cp[:], convp[l], B["IN"], b_cp)
            hg = [sb(st, f"hg{i}", [128, 8, 514], BF16) for i in range(2)]; b_hg = [Buf(), Buf()]
            u = [[sb(st, f"u{g}{i}", [128, 514], F32) for i in range(2)] for g in range(2)]; b_u = [[Buf(), Buf()], [Buf(), Buf()]]
            cv = [[sb(st, f"cv{g}{i}", [128, 512], F32) for i in range(2)] for g in range(2)]; b_cv = [[Buf(), Buf()], [Buf(), Buf()]]
            ao = [sb(st, f"ao{i}", [128, 512], BF16) for i in range(2)]; b_ao = [Buf(), Buf()]
            pum = [[ps(st, f"pum{g}{i}", [128, 512]) for i in range(2)] for g in range(2)]; b_pum = [[Buf(), Buf()], [Buf(), Buf()]]
            puh_t = ps(st, "puh", [128, 512])
            puh = [[puh_t[:, (g * 2 + i) * 8:(g * 2 + i) * 8 + 2] for i in range(2)] for g in range(2)]; _bh = Buf(); b_puh = [[_bh, _bh], [_bh, _bh]]
            seqs = [(0, N)] + ([(N, N + CTX)] if do_ctx else [])
            glist = [(s0, s1, tok0) for (s0, s1) in seqs for tok0 in range(s0, s1, 512)]

            def c2_load(gi_):
                s0, s1, tok0 = glist[gi_]
                ntok = min(512, s1 - tok0)
                hb = gi_ % 2
                lo = tok0 - 1 if tok0 > s0 else tok0
                hi = tok0 + ntok + 1 if tok0 + ntok < s1 else tok0 + ntok
                if lo == tok0 or hi == tok0 + ntok:
                    K.op("pool", lambda e, hb=hb: e.memset(hg[hb][:], 0.0), w=[b_hg[hb]])
                K.dma(hg[hb][:, :, lo - tok0 + 1:hi - tok0 + 1], h2_d[:, :, lo:hi].rearrange("c p t -> p c t"), B["h2"], b_hg[hb])

            it = 0
            c2_load(0)
            for gi, (s0, s1, tok0) in enumerate(glist):
                if True:
                    ntok = min(512, s1 - tok0)
                    hb = gi % 2
                    if gi + 1 < len(glist):
                        c2_load(gi + 1)
                    for j in range(22):
                        i = it % 2
                        it += 1
                        for g in range(2):
                            col0 = g * DFF + j * 128
                            for dc in range(8):
                                K.op("pe", lambda e, g=g, dc=dc, col0=col0: e.matmul(pum[g][i][:, :ntok], lhsT=wu[:, dc, col0:col0 + 128], rhs=hg[hb][:, dc, 0:ntok],
                                                                                     start=(dc == 0), stop=(dc == 7)), r=[b_wu, b_hg[hb]], w=[b_pum[g][i]])
                            for dc in range(8):
                                K.op("pe", lambda e, g=g, dc=dc, col0=col0: e.matmul(puh[g][i], lhsT=wu[:, dc, col0:col0 + 128], rhs=hg[hb][:, dc, ntok:ntok + 2],
                                                                                     start=(dc == 0), stop=(dc == 7)), r=[b_wu, b_hg[hb]], w=[b_puh[g][i]])
                            K.op("act", lambda e, g=g: e.copy(out=u[g][i][:, 0:ntok], in_=pum[g][i][:, :ntok]), r=[b_pum[g][i]], w=[b_u[g][i]])
                            K.op("act", lambda e, g=g: e.copy(out=u[g][i][:, ntok:ntok + 2], in_=puh[g][i]), r=[b_puh[g][i]], w=[b_u[g][i]])
                        eng = "dve"
                        for g in range(2):
                            ch = g * 22 + j
                            K.op(eng, lambda e, g=g, ch=ch: e.tensor_scalar(out=cv[g][i][:, :ntok], in0=u[g][i][:, 0:ntok], scalar1=cp[:, ch, 0:1], scalar2=cp[:, ch, 3:4],
                                                                             op0=ALU.mult, op1=ALU.add), r=[b_u[g][i], b_cp], w=[b_cv[g][i]])
                        for k in (1, 2):
                            for g in range(2):
                                ch = g * 22 + j
                                K.op(eng, lambda e, g=g, ch=ch, k=k: e.scalar_tensor_tensor(out=cv[g][i][:, :ntok], in0=u[g][i][:, k:k + ntok], scalar=cp[:, ch, k:k + 1],
                                                                                            in1=cv[g][i][:, :ntok], op0=ALU.mult, op1=ALU.add),
                                     r=[b_u[g][i], b_cp, b_cv[g][i]], w=[b_cv[g][i]])
                        K.op("act", lambda e: e.activation(out=cv[0][i][:, :ntok], in_=cv[0][i][:, :ntok], func=AF.Silu), r=[b_cv[0][i]], w=[b_cv[0][i]])
                        K.op("dve", lambda e: e.tensor_tensor(out=ao[i][:, :ntok], in0=cv[0][i][:, :ntok], in1=cv[1][i][:, :ntok], op=ALU.mult),
                             r=[b_cv[0][i], b_cv[1][i]], w=[b_ao[i]])
                        K.dma(at_d[j, :, tok0:tok0 + ntok], ao[i][:, :ntok], b_ao[i], B["at"])
            K.barrier()

    def phaseC3(l, do_ctx):
        with ExitStack() as st:
            wd = sb(st, "wd", [128, 22, D], BF16); b_wd = Buf()
            load_weight(st, "wd", w_down[l], 22, D, wd, b_wd, colblk=1024)
            gate = [bc_load(st, f"gate{m}", ada_d[l, m, 1, :], B["ada"]) for m in range(2)]
            g_bc, b_g = bc_load(st, "g_bc", lnp[l, 2, :], B["IN"])
            b_bc, b_b = bc_load(st, "b_bc", lnp[l, 3, :], B["IN"])
            b_gb = (b_g, b_b)
            aT = [sb(st, f"aT{i}", [128, 22, 128], BF16) for i in range(2)]; b_aT = [Buf(), Buf()]
            xr = [sb(st, f"xr{i}", [128, D], F32) for i in range(2)]; b_xr = [Buf(), Buf()]
            z = [sb(st, f"z{i}", [128, D], F32) for i in range(2)]; b_z = [Buf(), Buf()]
            xn = sb(st, "xn", [128, D], F32); b_xn = Buf()
            tmp = mk_tmp(st, "c3")
            py = [ps(st, f"py{i}", [128, 512]) for i in range(2)]; b_py = [Buf(), Buf()]
            def c3_load(t):
                i = t % 2
                K.dma(aT[i][:], at_d[:, :, t * 128:(t + 1) * 128].rearrange("c p t -> p c t"), B["at"], b_aT[i])
                K.dma(xr[i][:], x1_d[t * 128:(t + 1) * 128, :], B["x1"], b_xr[i])

            tl = tok_tiles(do_ctx)
            c3_load(tl[0])
            for ti_, t in enumerate(tl):
                i = t % 2
                mi = 1 if t >= NT else 0
                if ti_ + 1 < len(tl):
                    c3_load(tl[ti_ + 1])
                for hh in range(2):
                    for fc in range(22):
                        K.op("pe", lambda e, hh=hh, fc=fc: e.matmul(py[hh][:], lhsT=aT[i][:, fc, :], rhs=wd[:, fc, hh * 512:(hh + 1) * 512],
                                                                    start=(fc == 0), stop=(fc == 21)), r=[b_aT[i], b_wd], w=[b_py[hh]])
                postnorm(py, b_py, xr[i], b_xr[i], gate[mi][0], gate[mi][1], g_bc, b_bc, b_gb, z[i], b_z[i], xn, b_xn, tmp)
                dst, dbuf = x_out_ap(t)
                K.dma(dst, z[i][:], b_z[i], dbuf)
            K.barrier()

    for l in range(DEPTH):
        do_ctx = l < DEPTH - 1
        for nm, fn in (("A", lambda: phaseA(l)), ("B", lambda: phaseB(l, do_ctx)), ("C1", lambda: phaseC1(l, do_ctx)),
                       ("C2", lambda: phaseC2(l, do_ctx)), ("C3", lambda: phaseC3(l, do_ctx))):
            if stop_after is not None and stop_after == "0":
                break
            fn()
            if stop_after == nm:
                break
        if stop_after is not None:
            break
    K.barrier()
    es.close()
    return nc


_NC_CACHE = {}


def _host_inputs(N, DEPTH, x, c, ctx, c_ctx, w_ada, b_ada, w_in, lam_q1, lam_k1, lam_q2, lam_k2, diff_norm_w, na_rpb,
                 mla_q_norm_w, mla_kv_norm_w, w_uq, w_ukv, w_out, ln1_g, ln1_b, w_up, conv_w, conv_b, w_down, ln2_g, ln2_b):
    f = lambda a: np.ascontiguousarray(np.asarray(a, dtype=np.float32))
    L = DEPTH
    w_in = f(w_in)[:L]
    plan, tkeys = _na_plan(N)
    cosT, sinT = _rope_tables(N)
    w_ukv = f(w_ukv)[:L].reshape(L, 128, 4, 128)
    shared = {
        "w_ada": f(w_ada)[:L], "b_ada": f(b_ada)[:L],
        "w_a": np.ascontiguousarray(w_in[:, :, _win_cols()]),
        "lamv": np.ascontiguousarray(np.stack([np.stack([f(lam_q1)[:L], f(lam_q2)[:L]], axis=1), np.stack([f(lam_k1)[:L], f(lam_k2)[:L]], axis=1)], axis=1)),
        "dnw": f(diff_norm_w)[:L],
        "natab": _na_tables(f(na_rpb)[:L], tkeys),
        "qnw": np.ascontiguousarray(np.concatenate([f(mla_q_norm_w)[:L], f(mla_kv_norm_w)[:L]], axis=1)),
        "w_uq": f(w_uq)[:L], "w_uqr": np.ascontiguousarray(f(w_uq)[:L][:, :, _wuq_rot_cols()]),
        "w_ukn": np.ascontiguousarray(w_ukv[:, :, :, 0:64].reshape(L, 128, 256)),
        "w_ukv": np.ascontiguousarray(w_ukv[:, :, :, 64:128].reshape(L, 128, 256)),
        "w_out": f(w_out)[:L], "w_up": f(w_up)[:L], "w_down": f(w_down)[:L],
        "lnp": np.ascontiguousarray(np.stack([f(ln1_g)[:L], f(ln1_b)[:L], f(ln2_g)[:L], f(ln2_b)[:L]], axis=1)),
        "convp": np.ascontiguousarray(np.concatenate([f(conv_w)[:L], f(conv_b)[:L][:, None, :]], axis=1).reshape(L, 4, 44, 128).transpose(0, 3, 2, 1)),
        "ident": np.eye(128, dtype=np.float32), "cosT": cosT, "sinT": sinT,
    }
    x = f(x); ctx = f(ctx); c = f(c); c_ctx = f(c_ctx)
    maps = []
    for b in range(x.shape[0]):
        m = dict(shared)
        m["x"] = x[b]
        m["ctx"] = ctx[b]
        m["cc"] = np.ascontiguousarray(np.stack([c[b], c_ctx], axis=0).reshape(2, 8, 128).transpose(2, 1, 0))
        maps.append(m)
    return maps


def run(N, DEPTH, inputs, stop_after=None):
    key = (N, DEPTH, stop_after)
    if key not in _NC_CACHE:
        _NC_CACHE[key] = build(N, DEPTH, stop_after)
    nc = _NC_CACHE[key]
    maps = _host_inputs(N, DEPTH, **inputs)
    res = run_bass_kernel_spmd(nc, maps, core_ids=list(range(len(maps))))
    return np.stack([np.asarray(r["y"], dtype=np.float32) for r in res.results], axis=0)


def kernel(**inputs):
    N = inputs["x"].shape[1]
    return run(N, DEPTH_FULL, inputs)
```
